# Optimizing a Trainium2 kernel written in Bass

```python
import jax, jax.numpy as jnp
from jax import lax
import numpy as np

D_MODEL = 1024
BATCH = 32
SEQ = 2048
DEPTH = 4
DEC_BATCH = 32
DEC_SEQ = 64
PAST_LEN = 1024

CHUNK = 64
N_EVEN = (DEPTH + 1) // 2
N_ODD = DEPTH // 2
H_A = 4
DK_A = 128
DV_A = 128
CONV_A = 4
H_B = 8
HD_B = 64
H_IDX = 4
D_IDX = 64
TOPK_MAX = 256
Q_BLOCK = 128
H_C = 16
HD_C = 64
BAND_CHUNKS = 8
WINDOW = BAND_CHUNKS * CHUNK
REL_CLIP = 128
N_REL = CHUNK + REL_CLIP
D_FF = 2816
CONV_F = 3
EPS = 1e-6

A_COLS = (H_A * DK_A, H_A * DK_A, H_A * DV_A, H_A * DV_A, H_A, H_A)
B_COLS = (H_B * HD_B, HD_B, HD_B, H_IDX * D_IDX, D_IDX, H_IDX)
IN_EVEN = sum(A_COLS) + sum(B_COLS)
MIX_EVEN = H_A * DV_A + H_B * HD_B
MIX_ODD = H_C * HD_C
STATE_KEYS = ('state_dn_S', 'state_dn_conv', 'cache_dsa_k', 'cache_dsa_v', 'cache_dsa_kidx',
              'cache_band_k', 'cache_band_v', 'state_ffn_conv')

kernel_name = 'hybrid_streaming_encoder_step'


def rms_norm(x, g):
    xf = x.astype(jnp.float32)
    y = xf * lax.rsqrt(jnp.mean(xf * xf, -1, keepdims=True) + EPS)
    return (y * g.astype(jnp.float32)).astype(x.dtype)


def l2_norm(x):
    xf = x.astype(jnp.float32)
    return (xf * lax.rsqrt(jnp.sum(xf * xf, -1, keepdims=True) + EPS)).astype(x.dtype)


def split_cols(h, sizes):
    offs = np.cumsum(sizes)[:-1].tolist()
    return jnp.split(h, offs, axis=-1)


def causal_dwconv(x, buf, w):
    k, L = w.shape[0], x.shape[1]
    xp = jnp.concatenate([buf.astype(x.dtype), x], axis=1)
    y = xp[:, 0:L] * w[0]
    for j in range(1, k):
        y = y + xp[:, j:j + L] * w[j]
    return y, xp[:, L:]


def gated_delta_chunked(q, k, v, g, beta, s0, c):
    B, L, H, DK = q.shape
    DV = v.shape[-1]
    n = L // c
    f32 = jnp.float32

    def blocks(t):
        t = t.astype(f32).reshape((B, n, c, H) + t.shape[3:])
        return jnp.transpose(t, (1, 0, 3, 2) + tuple(range(4, t.ndim)))

    qc = blocks(q) * DK ** -0.5
    kc = blocks(k)
    vc = blocks(v)
    gcum = jnp.cumsum(blocks(g), -1)
    bc = blocks(beta)
    tril = jnp.tril(jnp.ones((c, c), bool))
    strict = jnp.tril(jnp.ones((c, c), bool), -1)
    decay = jnp.exp(jnp.where(tril, gcum[..., :, None] - gcum[..., None, :], -jnp.inf))
    kb = kc * bc[..., None]
    m = jnp.where(strict, jnp.einsum('nbhid,nbhjd->nbhij', kb, kc) * decay, 0.0)
    eye = jnp.eye(c, dtype=f32)
    tmat = lax.linalg.triangular_solve(eye + m, jnp.broadcast_to(eye, m.shape), left_side=True, lower=True)
    u = jnp.einsum('nbhij,nbhjd->nbhid', tmat, vc * bc[..., None])
    w = jnp.einsum('nbhij,nbhjd->nbhid', tmat, kb * jnp.exp(gcum)[..., None])

    def step(S, inp):
        qi, ki, ui, wi, gi, di = inp
        v_new = ui - jnp.einsum('bhcd,bhde->bhce', wi, S)
        attn = jnp.einsum('bhid,bhjd->bhij', qi, ki) * di
        o = jnp.einsum('bhcd,bhde->bhce', qi * jnp.exp(gi)[..., None], S) + jnp.einsum('bhij,bhje->bhie', attn, v_new)
        glast = gi[..., -1]
        S = S * jnp.exp(glast)[..., None, None] + jnp.einsum(
            'bhcd,bhce->bhde', ki * jnp.exp(glast[..., None] - gi)[..., None], v_new)
        return S, o

    S, o = lax.scan(step, s0.astype(f32), (qc, kc, u, w, gcum, decay))
    o = jnp.transpose(o, (1, 0, 3, 2, 4)).reshape(B, L, H, DV)
    return o.astype(v.dtype), S


def deltanet_mixer(qa, ka, va, za, ba, aa, conv_buf, s0, conv_w, a_log, dt_bias, o_gain, chunk):
    B, L = qa.shape[:2]
    qkv, new_buf = causal_dwconv(jnp.concatenate([qa, ka, va], -1), conv_buf, conv_w)
    qkv = jax.nn.silu(qkv)
    q, k, v = split_cols(qkv, (H_A * DK_A, H_A * DK_A, H_A * DV_A))
    q = l2_norm(q.reshape(B, L, H_A, DK_A))
    k = l2_norm(k.reshape(B, L, H_A, DK_A))
    v = v.reshape(B, L, H_A, DV_A)
    beta = jax.nn.sigmoid(ba.astype(jnp.float32))
    g = -jnp.exp(a_log.astype(jnp.float32)) * jax.nn.softplus(aa.astype(jnp.float32) + dt_bias.astype(jnp.float32))
    o, s_new = gated_delta_chunked(q, k, v, g, beta, s0, chunk)
    o = rms_norm(o, o_gain) * jax.nn.silu(za.reshape(B, L, H_A, DV_A))
    return o.reshape(B, L, H_A * DV_A), new_buf, s_new


def indexer_scores(qi, ki, wi):
    dots = jnp.einsum('bqhd,bsd->bqhs', qi, ki).astype(jnp.float32) * D_IDX ** -0.5
    return jnp.einsum('bqhs,bqh->bqs', jax.nn.relu(dots), wi.astype(jnp.float32) * H_IDX ** -0.5)


def sparse_attend(q, k, v, sel, valid):
    gather = jax.vmap(lambda t, i: t[i])
    ks = gather(k, sel)
    vs = gather(v, sel)
    s = jnp.einsum('bqhd,bqkd->bqhk', q, ks).astype(jnp.float32) * HD_B ** -0.5
    s = jnp.where(valid[:, :, None, :], s, -jnp.inf)
    p = jax.nn.softmax(s, -1).astype(v.dtype)
    return jnp.einsum('bqhk,bqkd->bqhd', p, vs)


def dsa_prompt(q, k, v, qi, ki, wi):
    B, S = q.shape[:2]
    topk = min(TOPK_MAX, S // 4)
    key_chunk = jnp.arange(S) // CHUNK

    def block(i):
        q0 = i * Q_BLOCK
        qb = lax.dynamic_slice_in_dim(q, q0, Q_BLOCK, 1)
        qib = lax.dynamic_slice_in_dim(qi, q0, Q_BLOCK, 1)
        wib = lax.dynamic_slice_in_dim(wi, q0, Q_BLOCK, 1)
        q_chunk = (q0 + jnp.arange(Q_BLOCK)) // CHUNK
        score = indexer_scores(qib, ki, wib)
        score = jnp.where(key_chunk[None, None, :] <= q_chunk[None, :, None], score, -jnp.inf)
        _, sel = lax.top_k(score, topk)
        valid = key_chunk[sel] <= q_chunk[None, :, None]
        return sparse_attend(qb, k, v, sel, valid)

    out = lax.map(block, jnp.arange(S // Q_BLOCK))
    return jnp.moveaxis(out, 0, 1).reshape(B, S, H_B, HD_B)


def dsa_sample(q, k, v, qi, ki, wi, ck, cv, cki):
    k_all = jnp.concatenate([ck.astype(k.dtype), k], 1)
    v_all = jnp.concatenate([cv.astype(v.dtype), v], 1)
    ki_all = jnp.concatenate([cki.astype(ki.dtype), ki], 1)
    topk = min(TOPK_MAX, k_all.shape[1] // 4)
    _, sel = lax.top_k(indexer_scores(qi, ki_all, wi), topk)
    return sparse_attend(q, k_all, v_all, sel, jnp.ones(sel.shape, bool))


def rel_bias_block(rel_bias, n_q, n_k, offset):
    dist = (offset + jnp.arange(n_q))[:, None] - jnp.arange(n_k)[None, :]
    idx = jnp.clip(dist, -(CHUNK - 1), REL_CLIP) + (CHUNK - 1)
    return rel_bias[:, idx].astype(jnp.float32)


def band_attend(q, k, v, bias, valid):
    s = jnp.einsum('bqhd,bkhd->bhqk', q, k).astype(jnp.float32) * HD_C ** -0.5 + bias
    s = jnp.where(valid, s, -jnp.inf)
    p = jax.nn.softmax(s, -1).astype(v.dtype)
    return jnp.einsum('bhqk,bkhd->bqhd', p, v)


def band_prompt(q, k, v, rel_bias):
    B, S, H, D = q.shape
    pad = jnp.zeros((B, WINDOW, H, D), k.dtype)
    kp = jnp.concatenate([pad, k], 1)
    vp = jnp.concatenate([pad, v], 1)
    bias = rel_bias_block(rel_bias, CHUNK, WINDOW + CHUNK, WINDOW)

    def one(n):
        start = n * CHUNK
        qc = lax.dynamic_slice_in_dim(q, start, CHUNK, 1)
        kc = lax.dynamic_slice_in_dim(kp, start, WINDOW + CHUNK, 1)
        vc = lax.dynamic_slice_in_dim(vp, start, WINDOW + CHUNK, 1)
        valid = (start - WINDOW + jnp.arange(WINDOW + CHUNK)) >= 0
        return band_attend(qc, kc, vc, bias, valid[None, None, None, :])

    out = lax.map(one, jnp.arange(S // CHUNK))
    return jnp.moveaxis(out, 0, 1).reshape(B, S, H, D)


def band_sample(q, k, v, ck, cv, rel_bias):
    T, Wc = q.shape[1], ck.shape[1]
    k_all = jnp.concatenate([ck.astype(k.dtype), k], 1)
    v_all = jnp.concatenate([cv.astype(v.dtype), v], 1)
    bias = rel_bias_block(rel_bias, T, Wc + T, Wc)
    return band_attend(q, k_all, v_all, bias, jnp.ones((1, 1, 1, Wc + T), bool))


def conv_ffn(h, buf, w_a, w_g, conv_w, conv_b, w_down):
    a, new_buf = causal_dwconv(h @ w_a, buf, conv_w)
    return (jax.nn.silu(a + conv_b) * (h @ w_g)) @ w_down, new_buf


def trunk(x, past, p):
    B, L, _ = x.shape
    first = past is None
    chunk = CHUNK if first else L
    out = {name: [] for name in STATE_KEYS}
    for layer in range(DEPTH):
        h = rms_norm(x, p['norm_mix'][layer])
        if layer % 2 == 0:
            e = layer // 2
            qa, ka, va, za, ba, aa, qb, kb, vb, qi, ki, wi = split_cols(h @ p['w_in_even'][e], A_COLS + B_COLS)
            if first:
                conv_buf = jnp.zeros((B, CONV_A - 1, 3 * H_A * DK_A), x.dtype)
                s0 = jnp.zeros((B, H_A, DK_A, DV_A), jnp.float32)
            else:
                conv_buf = past['state_dn_conv'][e]
                s0 = past['state_dn_S'][e]
            o_a, buf_a, s_a = deltanet_mixer(qa, ka, va, za, ba, aa, conv_buf, s0, p['dn_conv_w'][e],
                                             p['dn_a_log'][e], p['dn_dt_bias'][e], p['dn_o_gain'][e], chunk)
            q_b = rms_norm(qb.reshape(B, L, H_B, HD_B), p['dsa_q_gain'][e])
            k_b = rms_norm(kb, p['dsa_k_gain'][e])
            qi = qi.reshape(B, L, H_IDX, D_IDX)
            if first:
                o_b = dsa_prompt(q_b, k_b, vb, qi, ki, wi)
            else:
                o_b = dsa_sample(q_b, k_b, vb, qi, ki, wi, past['cache_dsa_k'][e], past['cache_dsa_v'][e],
                                 past['cache_dsa_kidx'][e])
            mix = jnp.concatenate([o_a, o_b.reshape(B, L, H_B * HD_B)], -1) @ p['w_out_even'][e]
            out['state_dn_S'].append(s_a)
            out['state_dn_conv'].append(buf_a)
            out['cache_dsa_k'].append(k_b)
            out['cache_dsa_v'].append(vb)
            out['cache_dsa_kidx'].append(ki)
        else:
            j = layer // 2
            qc, kc, vc = split_cols(h @ p['w_in_odd'][j], (MIX_ODD, MIX_ODD, MIX_ODD))
            qc = rms_norm(qc.reshape(B, L, H_C, HD_C), p['band_q_gain'][j])
            kc = rms_norm(kc.reshape(B, L, H_C, HD_C), p['band_k_gain'][j])
            vc = vc.reshape(B, L, H_C, HD_C)
            if first:
                o_c = band_prompt(qc, kc, vc, p['band_rel_bias'][j])
                keep = min(WINDOW, L)
                new_k, new_v = kc[:, L - keep:], vc[:, L - keep:]
            else:
                o_c = band_sample(qc, kc, vc, past['cache_band_k'][j], past['cache_band_v'][j], p['band_rel_bias'][j])
                new_k, new_v = kc, vc
            mix = o_c.reshape(B, L, MIX_ODD) @ p['w_out_odd'][j]
            out['cache_band_k'].append(new_k)
            out['cache_band_v'].append(new_v)
        x = x + mix
        h = rms_norm(x, p['norm_ffn'][layer])
        fbuf = jnp.zeros((B, CONV_F - 1, D_FF), x.dtype) if first else past['state_ffn_conv'][layer]
        f, fbuf_new = conv_ffn(h, fbuf, p['ffn_w_a'][layer], p['ffn_w_g'][layer], p['ffn_conv_w'][layer],
                               p['ffn_conv_b'][layer], p['ffn_w_down'][layer])
        x = x + f
        out['state_ffn_conv'].append(fbuf_new)
    y = rms_norm(x, p['norm_final'])
    return y, {name: jnp.stack(v) for name, v in out.items()}


def setup_inputs(seed: int = 0) -> dict:
    key = jax.random.key(seed)
    ks = iter(jax.random.split(key, 40))

    def nrm(shape, scale=1.0):
        return jax.random.normal(next(ks), shape, jnp.float32) * scale

    def gain(shape):
        return 1.0 + 0.1 * nrm(shape)

    band_len = min(WINDOW, PAST_LEN)
    c_a = 3 * H_A * DK_A
    return {
        'x_prompt': nrm((BATCH, SEQ, D_MODEL)),
        'x_sample': nrm((DEC_BATCH, DEC_SEQ, D_MODEL)),
        'state_dn_S': nrm((N_EVEN, DEC_BATCH, H_A, DK_A, DV_A), 0.1),
        'state_dn_conv': nrm((N_EVEN, DEC_BATCH, CONV_A - 1, c_a)),
        'cache_dsa_k': nrm((N_EVEN, DEC_BATCH, PAST_LEN, HD_B)),
        'cache_dsa_v': nrm((N_EVEN, DEC_BATCH, PAST_LEN, HD_B)),
        'cache_dsa_kidx': nrm((N_EVEN, DEC_BATCH, PAST_LEN, D_IDX)),
        'cache_band_k': nrm((N_ODD, DEC_BATCH, band_len, H_C, HD_C)),
        'cache_band_v': nrm((N_ODD, DEC_BATCH, band_len, H_C, HD_C)),
        'state_ffn_conv': nrm((DEPTH, DEC_BATCH, CONV_F - 1, D_FF)),
        'norm_mix': gain((DEPTH, D_MODEL)),
        'norm_ffn': gain((DEPTH, D_MODEL)),
        'norm_final': gain((D_MODEL,)),
        'w_in_even': nrm((N_EVEN, D_MODEL, IN_EVEN), D_MODEL ** -0.5),
        'dn_conv_w': nrm((N_EVEN, CONV_A, c_a), CONV_A ** -0.5),
        'dn_a_log': jnp.log(jax.random.uniform(next(ks), (N_EVEN, H_A), jnp.float32, 1.0, 16.0)),
        'dn_dt_bias': jax.random.uniform(next(ks), (N_EVEN, H_A), jnp.float32, -4.0, -2.0),
        'dn_o_gain': gain((N_EVEN, DV_A)),
        'dsa_q_gain': gain((N_EVEN, HD_B)),
        'dsa_k_gain': gain((N_EVEN, HD_B)),
        'w_out_even': nrm((N_EVEN, MIX_EVEN, D_MODEL), 0.5 * MIX_EVEN ** -0.5),
        'w_in_odd': nrm((N_ODD, D_MODEL, 3 * MIX_ODD), D_MODEL ** -0.5),
        'band_q_gain': gain((N_ODD, HD_C)),
        'band_k_gain': gain((N_ODD, HD_C)),
        'band_rel_bias': nrm((N_ODD, H_C, N_REL), 0.2),
        'w_out_odd': nrm((N_ODD, MIX_ODD, D_MODEL), 0.5 * MIX_ODD ** -0.5),
        'ffn_w_a': nrm((DEPTH, D_MODEL, D_FF), D_MODEL ** -0.5),
        'ffn_w_g': nrm((DEPTH, D_MODEL, D_FF), D_MODEL ** -0.5),
        'ffn_conv_w': nrm((DEPTH, CONV_F, D_FF), CONV_F ** -0.5),
        'ffn_conv_b': nrm((DEPTH, D_FF), 0.02),
        'ffn_w_down': nrm((DEPTH, D_FF, D_MODEL), 0.5 * D_FF ** -0.5),
    }


def reference(x_prompt, x_sample, state_dn_S, state_dn_conv, cache_dsa_k, cache_dsa_v, cache_dsa_kidx,
              cache_band_k, cache_band_v, state_ffn_conv, norm_mix, norm_ffn, norm_final, w_in_even,
              dn_conv_w, dn_a_log, dn_dt_bias, dn_o_gain, dsa_q_gain, dsa_k_gain, w_out_even, w_in_odd,
              band_q_gain, band_k_gain, band_rel_bias, w_out_odd, ffn_w_a, ffn_w_g, ffn_conv_w, ffn_conv_b,
              ffn_w_down):
    p = dict(norm_mix=norm_mix, norm_ffn=norm_ffn, norm_final=norm_final, w_in_even=w_in_even,
             dn_conv_w=dn_conv_w, dn_a_log=dn_a_log, dn_dt_bias=dn_dt_bias, dn_o_gain=dn_o_gain,
             dsa_q_gain=dsa_q_gain, dsa_k_gain=dsa_k_gain, w_out_even=w_out_even, w_in_odd=w_in_odd,
             band_q_gain=band_q_gain, band_k_gain=band_k_gain, band_rel_bias=band_rel_bias,
             w_out_odd=w_out_odd, ffn_w_a=ffn_w_a, ffn_w_g=ffn_w_g, ffn_conv_w=ffn_conv_w,
             ffn_conv_b=ffn_conv_b, ffn_w_down=ffn_w_down)
    past = dict(state_dn_S=state_dn_S, state_dn_conv=state_dn_conv, cache_dsa_k=cache_dsa_k,
                cache_dsa_v=cache_dsa_v, cache_dsa_kidx=cache_dsa_kidx, cache_band_k=cache_band_k,
                cache_band_v=cache_band_v, state_ffn_conv=state_ffn_conv)
    y_prompt, sp = trunk(x_prompt, None, p)
    y_sample, ss = trunk(x_sample, past, p)
    return (y_prompt, y_sample,
            sp['state_dn_S'], ss['state_dn_S'],
            sp['state_dn_conv'], ss['state_dn_conv'],
            sp['cache_dsa_k'], ss['cache_dsa_k'],
            sp['cache_dsa_v'], ss['cache_dsa_v'],
            sp['cache_dsa_kidx'], ss['cache_dsa_kidx'],
            sp['cache_band_k'], ss['cache_band_k'],
            sp['cache_band_v'], ss['cache_band_v'],
            sp['state_ffn_conv'], ss['state_ffn_conv'])
```

```python
import contextlib
import os
import numpy as np
import concourse.bass as bass
import concourse.mybir as mybir
from concourse.bass_utils import run_bass_kernel_spmd

F32 = mybir.dt.float32
BF16 = mybir.dt.bfloat16
AF = mybir.ActivationFunctionType
ALU = mybir.AluOpType

D = 1024
KD = 8
DEPTH = 4
CHUNK = 64
H_A, DK_A, DV_A, CONV_A = 4, 128, 128, 4
H_B, HD_B, H_IDX, D_IDX, TOPK_MAX = 8, 64, 4, 64, 256
H_C, HD_C, BAND_CHUNKS, REL_CLIP = 16, 64, 8, 128
WINDOW = BAND_CHUNKS * CHUNK
D_FF = 2816
KF = 22
CONV_F = 3
EPS = 1e-6
NEG = -30000.0
NCST = 1344
SBUF_LEFT = None

ENGINES = ("pe", "act", "dve", "pool", "sp")


class Tile:
    __slots__ = ("name", "t", "last_w", "pw", "readers", "dsem", "dcount", "psum", "gh")

    def __init__(self, name, t, psum=False):
        self.name = name
        self.t = t
        self.psum = psum
        self.gh = None
        self.last_w = None
        self.pw = []
        self.readers = []
        self.dsem = None
        self.dcount = 0

    def __getitem__(self, k):
        return self.t[k]


class Op:
    __slots__ = ("eng", "fn", "deps", "is_dma", "sem", "semval", "needs_inc", "dtile")

    def __init__(self, eng, fn, is_dma):
        self.eng = eng
        self.fn = fn
        self.deps = []
        self.is_dma = is_dma
        self.sem = None
        self.semval = 0
        self.needs_inc = False
        self.dtile = None


class _Rec:
    def __getattr__(self, name):
        def f(*a, **k):
            self.call = (name, a, k)
            return self
        return f


def _freeze(fn):
    r = _Rec()
    fn(r)
    name, a, k = r.call
    return lambda eng, name=name, a=a, k=k: getattr(eng, name)(*a, **k)


class Prog:
    def __init__(self, nc):
        self.nc = nc
        self.streams = {e: [] for e in ENGINES}
        self.tiles = []
        self.out_ops = []
        self.nops = 0

    def tile(self, name, t, psum=False):
        tl = Tile(name, t, psum)
        self.tiles.append(tl)
        return tl

    def _add(self, op, reads, writes, pwrites, pe_sync=False):
        deps = {}
        for t in reads:
            if t.last_w is not None:
                deps[id(t.last_w)] = t.last_w
            for w in t.pw:
                deps[id(w)] = w
            if t.psum:
                for r in t.readers:
                    if r.eng != op.eng:
                        deps[id(r)] = r
        for t in writes:
            if t.last_w is not None:
                deps[id(t.last_w)] = t.last_w
            for w in t.pw:
                deps[id(w)] = w
            for r in t.readers:
                deps[id(r)] = r
        for t in pwrites:
            if t.last_w is not None:
                deps[id(t.last_w)] = t.last_w
            if t.gh is not None:
                deps[id(t.gh)] = t.gh
            for r in t.readers:
                deps[id(r)] = r
        deps.pop(id(op), None)
        for d in deps.values():
            if d.eng == "pe" and op.eng == "pe" and not pe_sync:
                continue
            op.deps.append(d)
            d.needs_inc = True
        for t in writes:
            t.last_w = op
            t.pw = []
            t.readers = []
            t.gh = None
        for t in pwrites:
            if t.readers:
                t.pw = [op]
                t.readers = []
                t.gh = op
            else:
                t.pw.append(op)
        for t in reads:
            if t.last_w is not op and op not in t.pw:
                t.readers.append(op)
        self.streams[op.eng].append(op)
        self.nops += 1
        return op

    def op(self, eng, fn, reads=(), writes=(), pwrites=(), pe_sync=False):
        return self._add(Op(eng, _freeze(fn), False), reads, writes, pwrites, pe_sync)

    def dma(self, eng, fn, tile, reads=(), writes=(), pwrites=(), is_output=False):
        op = Op(eng, _freeze(fn), True)
        op.dtile = tile
        tile.dcount += 1
        op.semval = 16 * tile.dcount
        self._add(op, reads, writes, pwrites)
        op.needs_inc = True
        if is_output:
            self.out_ops.append(op)
        return op

    def emit(self):
        nc = self.nc
        with contextlib.ExitStack() as es:
            esem = {e: es.enter_context(nc.semaphore("s_" + e)) for e in ENGINES}
            for t in self.tiles:
                if t.dcount:
                    t.dsem = es.enter_context(nc.semaphore("d_" + t.name))
            for e in ENGINES:
                c = 0
                for op in self.streams[e]:
                    if op.is_dma:
                        op.sem = op.dtile.dsem
                    else:
                        op.sem = esem[e]
                        if op.needs_inc:
                            c += 1
                        op.semval = c
            finals = {}
            for op in self.out_ops:
                k = id(op.sem)
                if k not in finals or finals[k][1] < op.semval:
                    finals[k] = (op.sem, op.semval)
            block = es.enter_context(nc.Block())

            def gen(ename, eng, is_last):
                known = {}
                for op in self.streams[ename]:
                    need = {}
                    for d in op.deps:
                        k = id(d.sem)
                        if known.get(k, 0) >= d.semval:
                            continue
                        if k not in need or need[k][1] < d.semval:
                            need[k] = (d.sem, d.semval)
                    for k, (s, v) in need.items():
                        eng.wait_ge(s, v)
                        known[k] = v
                    ins = op.fn(eng)
                    if op.needs_inc:
                        ins.then_inc(op.sem, 16 if op.is_dma else 1)
                if is_last:
                    for k, (s, v) in finals.items():
                        eng.wait_ge(s, v)

            @block.tensor
            def _(e):
                gen("pe", e, False)

            @block.scalar
            def _(e):
                gen("act", e, False)

            @block.vector
            def _(e):
                gen("dve", e, False)

            @block.gpsimd
            def _(e):
                gen("pool", e, False)

            @block.sync
            def _(e):
                gen("sp", e, True)


def fm_slabs(W, kc_rows=128):
    K, N = W.shape
    KC = K // 128
    NCH = (N + 127) // 128
    Wp = np.zeros((K, NCH * 128), np.float32)
    Wp[:, :N] = W
    a = Wp.reshape(KC, 128, NCH, 128).transpose(2, 1, 0, 3)
    return np.ascontiguousarray(a.reshape(NCH * 128, KC * 128))


def tm_slabs(W, ncol):
    K, N = W.shape
    KC = K // 128
    G = N // ncol
    a = W.reshape(KC, 128, G, ncol).transpose(2, 1, 0, 3)
    return np.ascontiguousarray(a.reshape(G * 128, KC * ncol))


def feat_major(v, kc):
    return np.ascontiguousarray(np.moveaxis(v.reshape(v.shape[:-1] + (kc, 128)), -1, 0))


class Cfg:
    def __init__(self, n_pseq=4, seq=2048, n_sseq=4, dec_seq=64, past=1024, layers=DEPTH):
        self.n_pseq, self.seq, self.n_sseq, self.dec_seq, self.past = n_pseq, seq, n_sseq, dec_seq, past
        self.layers = layers
        self.TT = 512
        self.ntile = seq // self.TT
        self.n_even = (layers + 1) // 2
        self.n_odd = layers // 2
        self.band_len = min(WINDOW, past)


def build_program(cfg):
    nc = bass.Bass("TRN2", target_bir_lowering=False)
    P = Prog(nc)
    es = contextlib.ExitStack()
    L = cfg.layers
    TT = cfg.TT

    def din(name, shape, dt=F32):
        return nc.dram_tensor(name, list(shape), dt, kind="ExternalInput").ap()

    def dout(name, shape, dt=F32):
        return nc.dram_tensor(name, list(shape), dt, kind="ExternalOutput").ap()

    def dscr(name, shape, dt):
        return nc.dram_tensor(name, list(shape), dt, kind="Internal").ap()

    _n = [0]

    def sb(name, shape, dt):
        return P.tile(name, es.enter_context(nc.sbuf_tensor("sb_" + name, list(shape), dt)))

    xp_d = din("xp", [cfg.n_pseq, 128, KD, cfg.seq])
    xs_d = din("xs", [128, KD, cfg.n_sseq * cfg.dec_seq])
    yp_d = dout("yp", [cfg.n_pseq, 128, KD, cfg.seq])
    ys_d = dout("ys", [128, KD, cfg.n_sseq * cfg.dec_seq])
    ffnst_d = din("ffnst", [128, L, KF, cfg.n_sseq, 2])
    ffnc_p_d = dout("ffnc_p", [L, cfg.n_pseq, 128, KF, 2])
    ffnc_s_d = dout("ffnc_s", [128, L, KF, cfg.n_sseq, 2])
    vecs_d = din("vecs", [128, 2 * L * KD + KD + L * KF * 4])
    consts_d = din("consts", [128, NCST])
    NO = max(cfg.n_odd, 1)
    bgain_d = din("bgain", [128, NO, 2])
    btab_d = din("btab", [NO, 128, H_C, 256])
    bfar_d = din("bfar", [128, NO, H_C])
    bkc_d = din("bkc", [NO, cfg.n_sseq, 128, KD, 512])
    bvc_d = din("bvc", [NO, cfg.n_sseq, 128, 4, 1024])
    bk_p_d = dout("bk_p", [NO, cfg.n_pseq, 128, KD, 512])
    bv_p_d = dout("bv_p", [NO, cfg.n_pseq, 512, 1024])
    bk_s_d = dout("bk_s", [NO, 128, KD, cfg.n_sseq * cfg.dec_seq])
    bv_s_d = dout("bv_s", [NO, cfg.n_sseq * cfg.dec_seq, 1024])
    NE = cfg.n_even
    egain_d = din("egain", [128, NE, 4])
    dkc_d = din("dkc", [NE, cfg.n_sseq, 64, cfg.past])
    dic_d = din("dic", [NE, cfg.n_sseq, 64, cfg.past])
    dvc_d = din("dvc", [NE, cfg.n_sseq, 128, cfg.past // 128, 64])
    dk_p_d = dout("dk_p", [NE, cfg.n_pseq, 64, cfg.seq])
    dv_p_d = dout("dv_p", [NE, cfg.n_pseq, 64, cfg.seq])
    di_p_d = dout("di_p", [NE, cfg.n_pseq, 64, cfg.seq])
    dk_s_d = dout("dk_s", [NE, 64, cfg.n_sseq * cfg.dec_seq])
    dv_s_d = dout("dv_s", [NE, 64, cfg.n_sseq * cfg.dec_seq])
    di_s_d = dout("di_s", [NE, 64, cfg.n_sseq * cfg.dec_seq])
    dnw_d = din("dnw", [128, NE, 12, 4])
    dnc_d = din("dnc", [128, NE, 8])
    dSst_d = din("dSst", [NE, cfg.n_sseq, 128, 4, 128])
    dcst_d = din("dcst", [128, NE, 12, cfg.n_sseq, 3])
    dS_p_d = dout("dS_p", [NE, cfg.n_pseq, 128, 4, 128])
    dS_s_d = dout("dS_s", [NE, cfg.n_sseq, 128, 4, 128])
    dc_p_d = dout("dc_p", [NE, cfg.n_pseq, 128, 12, 3])
    dc_s_d = dout("dc_s", [128, NE, 12, cfg.n_sseq, 3])
    kscr_d = dscr("kscr", [NO, 2, 128, KD * 512], BF16)
    vscr_d = dscr("vscr", [NO, 2, 128, 4 * 1024], BF16)
    kscr_tl = [[P.tile("kscr%d_%d" % (a, b), None) for b in range(2)] for a in range(NO)]
    vscr_tl = [[P.tile("vscr%d_%d" % (a, b), None) for b in range(2)] for a in range(NO)]

    wspec = {}
    for l in range(L):
        wspec["ffa%d" % l] = (KF * 128, KD * 128)
        wspec["ffg%d" % l] = (KF * 128, KD * 128)
        wspec["ffd%d" % l] = (2 * KD * 128, 11 * 128)
        if l % 2 == 0:
            wspec["eA%d" % l] = (17 * 128, KD * 128)
            wspec["eB%d" % l] = (14 * 128, KD * 128)
            wspec["eo%d" % l] = (KD * 128, 12 * 128)
        if l % 2 == 1:
            wspec["oqk%d" % l] = (16 * 128, KD * 128)
            wspec["ov%d" % l] = (8 * 128, 4 * 256)
            wspec["oo%d" % l] = (KD * 128, KD * 128)
    w_in = {k: din("wf_" + k, s) for k, s in wspec.items()}
    w_bf = {k: dscr("wb_" + k, s, BF16) for k, s in wspec.items()}
    w_tl = {k: P.tile("wscr_" + k, None) for k in wspec}

    NW = 4
    wbuf = [sb("wbuf%d" % i, [128, 1536], BF16) for i in range(NW)]
    Kcur = sb("Kcur", [128, KD, 512], BF16)
    Kprev = sb("Kprev", [128, KD, 512], BF16)
    Vcur = sb("Vcur", [128, 4, 1024], BF16)
    Vprev = sb("Vprev", [128, 4, 1024], BF16)
    stg_f = [Kcur, Kprev]
    stg_b = [Vcur, Vprev]
    stg_fv = [Kcur[:].bitcast(F32).rearrange("p a b -> p (a b)"), Kprev[:].bitcast(F32).rearrange("p a b -> p (a b)")]
    stg_bv = [Vcur[:].rearrange("p a b -> p (a b)"), Vprev[:].rearrange("p a b -> p (a b)")]
    qT = sb("qT", [128, KD, 512], BF16)
    oT = sb("oT", [128, KD, 512], BF16)
    Wf = [sb("Wf%d" % i, [128, 512], F32) for i in range(4)]
    Wb = [sb("Wb%d" % i, [128, 512], BF16) for i in range(4)]
    PTf = [sb("PTf%d" % i, [128, 384], BF16) for i in range(3)]
    PTn = [sb("PTn%d" % i, [128, 256], BF16) for i in range(3)]
    tnr = [sb("tnr%d" % i, [128, 256], F32) for i in range(2)]
    Tn = sb("Tn", [128, H_C, 256], F32)
    bgain = sb("bgain", [128, NO, 2], F32)
    bfar = sb("bfar", [128, NO, H_C], F32)
    blk64 = sb("blk64", [128, 128], BF16)
    ones64 = sb("ones64", [128, 64], BF16)
    ident = sb("ident", [128, 128], BF16)
    identf = sb("identf", [128, 128], F32)
    idrep = sb("idrep", [128, 512], BF16)
    idrep64 = sb("idrep64", [128, 512], BF16)
    egain = sb("egain", [128, NE, 4], F32)
    SK = max(cfg.seq, cfg.past + 256)
    NVB = SK // 128
    kbT_c = [sb("kbT_c%d" % e_, [64, SK], BF16) for e_ in range(NE)]
    kiT_c = [sb("kiT_c%d" % e_, [64, SK], BF16) for e_ in range(NE)]
    Vb_c = [sb("Vb_c%d" % e_, [128, NVB, 64], BF16) for e_ in range(NE)]
    oTb = Kprev
    PTd = [sb("PTd%d" % i, [128, 1024], BF16) for i in range(2)]
    wi_tm = sb("wi_tm", [128, 4, 4], F32)
    wtab = sb("wtab", [128, 16], F32)
    bis = sb("bis", [128, 8], F32)
    S_f = [sb("S_f%d" % e_, [128, 4, 128], F32) for e_ in range(NE)]
    S_b = [sb("S_b%d" % e_, [128, 4, 128], BF16) for e_ in range(NE)]
    dtail = [sb("dtail%d" % e_, [128, 12, 3], F32) for e_ in range(NE)]
    dnw = sb("dnw", [128, NE, 12, 4], F32)
    dnc = sb("dnc", [128, NE, 8], F32)
    nexpA = sb("nexpA", [128, NE, 4], F32)
    dcst = sb("dcst", [128, NE, 12, cfg.n_sseq, 3], F32)
    dcso = sb("dcso", [128, NE, 12, cfg.n_sseq, 3], F32)
    gb_tm = sb("gb_tm", [128, 4, 8], F32)
    bt_tm = sb("bt_tm", [128, 4, 4], F32)
    lnb_tm = sb("lnb_tm", [128, 4, 4], F32)
    g_tm = sb("g_tm", [128, 4, 4], F32)
    gsm = sb("gsm", [128, 8], F32)
    sm2 = sb("sm2", [128, 16], F32)
    bl8 = sb("bl8", [128, 8], F32)
    egl8 = sb("egl8", [128, 8], F32)
    nwT = sb("nwT", [128, 4, 128], BF16)
    qgT = sb("qgT", [128, 4, 128], BF16)
    vnew = sb("vnew", [128, 4, 128], BF16)
    vecs = sb("vecs", [128, 2 * L * KD + KD + L * KF * 4], F32)
    cst = sb("cst", [128, NCST], F32)
    onesD = sb("onesD", [128, 128], BF16)
    xT = sb("xT", [128, KD, TT], F32)
    hT = sb("hT", [128, KD, TT], BF16)
    rstd = sb("rstd", [128, TT], F32)
    aT = [sb("aT%d" % i, [128, TT + 8], F32) for i in range(2)]
    cv = [sb("cv%d" % i, [128, TT], F32) for i in range(2)]
    sg = [sb("sg%d" % i, [128, TT], F32) for i in range(2)]
    actT = sb("actT", [128, KF, TT], BF16)
    sq = actT
    ftail = sb("ftail", [128, L, KF, 2], F32)
    banks = [P.tile("bank%d" % i, es.enter_context(nc.psum_tensor("bank%d" % i, [128, 512], F32)), psum=True) for i in range(8)]
    _bk = [0]

    def bank():
        b = banks[_bk[0] % 8]
        _bk[0] += 1
        return b

    _rr = [0]

    def evac_eng():
        _rr[0] += 1
        return "act" if _rr[0] % 2 else "dve"

    def o_nmix(l):
        return l * KD

    def o_nffn(l):
        return L * KD + l * KD

    o_nfin = 2 * L * KD
    o_fcv = 2 * L * KD + KD

    P.dma("sp", lambda e: e.dma_start(out=vecs[:], in_=vecs_d[:, :]), vecs, writes=[vecs])
    P.dma("sp", lambda e: e.dma_start(out=cst[:], in_=consts_d[:, :]), cst, writes=[cst])
    P.op("dve", lambda e: e.tensor_copy(out=onesD[:], in_=cst[:, 0:128]), reads=[cst], writes=[onesD])
    P.op("dve", lambda e: e.tensor_copy(out=blk64[:], in_=cst[:, 128:256]), reads=[cst], writes=[blk64])
    P.op("dve", lambda e: e.tensor_copy(out=ones64[:], in_=cst[:, 256:320]), reads=[cst], writes=[ones64])
    P.op("dve", lambda e: e.tensor_copy(out=ident[:], in_=cst[:, 320:448]), reads=[cst], writes=[ident])
    P.op("dve", lambda e: e.tensor_copy(out=identf[:], in_=cst[:, 320:448]), reads=[cst], writes=[identf])
    for r_ in range(4):
        P.op("dve", lambda e, r_=r_: e.tensor_copy(out=idrep[:, r_ * 128:(r_ + 1) * 128], in_=cst[:, 320:448]), reads=[cst], writes=[idrep] if r_ == 0 else (), pwrites=() if r_ == 0 else [idrep])
    for r_ in range(8):
        P.op("dve", lambda e, r_=r_: e.tensor_copy(out=idrep64[0:64, r_ * 64:(r_ + 1) * 64], in_=cst[0:64, 320:384]), reads=[cst], writes=[idrep64] if r_ == 0 else (), pwrites=() if r_ == 0 else [idrep64])
        P.op("dve", lambda e, r_=r_: e.tensor_copy(out=idrep64[64:128, r_ * 64:(r_ + 1) * 64], in_=cst[64:128, 384:448]), reads=[cst], pwrites=[idrep64])
    P.dma("sp", lambda e: e.dma_start(out=egain[:], in_=egain_d[:, :, :]), egain, writes=[egain])
    P.dma("sp", lambda e: e.dma_start(out=dnw[:], in_=dnw_d[:, :, :, :]), dnw, writes=[dnw])
    P.dma("sp", lambda e: e.dma_start(out=dnc[:], in_=dnc_d[:, :, :]), dnc, writes=[dnc])
    P.dma("sp", lambda e: e.dma_start(out=dcst[:], in_=dcst_d[:, :, :, :, :]), dcst, writes=[dcst])
    P.op("act", lambda e: e.activation(out=nexpA[:], in_=dnc[:, :, 0:4], func=AF.Exp), reads=[dnc], writes=[nexpA])
    P.op("dve", lambda e: e.tensor_scalar(out=nexpA[:], in0=nexpA[:], scalar1=-1.0, scalar2=None, op0=ALU.mult), reads=[nexpA], writes=[nexpA])
    P.dma("sp", lambda e: e.dma_start(out=bgain[:], in_=bgain_d[:, :, :]), bgain, writes=[bgain])
    P.dma("sp", lambda e: e.dma_start(out=bfar[:], in_=bfar_d[:, :, :]), bfar, writes=[bfar])

    def convert(key):
        R, C = wspec[key]
        src, dst, tl = w_in[key], w_bf[key], w_tl[key]
        i = 0
        for r0 in range(0, R, 128):
            for c0 in range(0, C, 2048):
                cw = min(2048, C - c0)
                f, b = stg_f[i % 2], stg_b[i % 2]
                fv, bv = stg_fv[i % 2], stg_bv[i % 2]
                P.dma("sp", lambda e, fv=fv, r0=r0, c0=c0, cw=cw: e.dma_start(out=fv[:, 0:cw], in_=src[r0:r0 + 128, c0:c0 + cw]), f, writes=[f])
                ce = ("pool", "dve", "act")[i % 3]
                if ce == "act":
                    P.op("act", lambda e, fv=fv, bv=bv, cw=cw: e.activation(out=bv[:, 0:cw], in_=fv[:, 0:cw], func=AF.Copy), reads=[f], writes=[b])
                else:
                    P.op(ce, lambda e, fv=fv, bv=bv, cw=cw: e.tensor_copy(out=bv[:, 0:cw], in_=fv[:, 0:cw]), reads=[f], writes=[b])
                P.dma("act", lambda e, bv=bv, r0=r0, c0=c0, cw=cw: e.dma_start(out=dst[r0:r0 + 128, c0:c0 + cw], in_=bv[:, 0:cw]), b, reads=[b], pwrites=[tl])
                i += 1

    for l in range(L):
        if l % 2 == 0:
            convert("eA%d" % l)
            convert("eB%d" % l)
            convert("eo%d" % l)
        if l % 2 == 1:
            convert("oqk%d" % l)
            convert("ov%d" % l)
            convert("oo%d" % l)
        convert("ffa%d" % l)
        convert("ffg%d" % l)
        convert("ffd%d" % l)

    class WStream:
        def __init__(self):
            self.reqs = []
            self.loaded = 0
            self.used = 0

        def plan(self, lst):
            self.reqs.extend(lst)

        def _load(self, i):
            key, c, width = self.reqs[i]
            buf = wbuf[i % NW]
            src, tl = w_bf[key], w_tl[key]
            P.dma("sp", lambda e: e.dma_start(out=buf[:, 0:width], in_=src[c * 128:(c + 1) * 128, 0:width]), buf, reads=[tl], writes=[buf])

        def get(self, key, c):
            i = self.used
            assert self.reqs[i][0] == key and self.reqs[i][1] == c, (self.reqs[i], key, c)
            while self.loaded < min(len(self.reqs), i + NW):
                self._load(self.loaded)
                self.loaded += 1
            self.used += 1
            return wbuf[i % NW]

    ws = WStream()

    def rmsnorm(n, goff):
        P.op("act", lambda e: e.activation(out=sq[:, 0:KD, 0:n], in_=xT[:, :, 0:n], func=AF.Square), reads=[xT], writes=[sq])
        b = bank()
        for k in range(KD):
            P.op("pe", lambda e, k=k: e.matmul(b[:, 0:n], onesD[:], sq[:, k, 0:n], start=(k == 0), stop=(k == KD - 1)), reads=[onesD, sq], writes=[b] if k == 0 else (), pwrites=() if k == 0 else [b])
        P.op("act", lambda e: e.activation(out=rstd[:, 0:n], in_=b[:, 0:n], func=AF.Ln, bias=EPS, scale=1.0), reads=[b], writes=[rstd])
        P.op("act", lambda e: e.activation(out=rstd[:, 0:n], in_=rstd[:, 0:n], func=AF.Exp, scale=-0.5), reads=[rstd], writes=[rstd])
        for k in range(KD):
            P.op("dve", lambda e, k=k: e.scalar_tensor_tensor(out=hT[:, k, 0:n], in0=xT[:, k, 0:n], scalar=vecs[:, goff + k:goff + k + 1], in1=rstd[:, 0:n], op0=ALU.mult, op1=ALU.mult),
                 reads=[xT, vecs, rstd], writes=[hT] if k == 0 else (), pwrites=() if k == 0 else [hT])

    def proj_fm(key, c, n, src, kc, m=128):
        wt = ws.get(key, c)
        b = bank()
        for k in range(kc):
            P.op("pe", lambda e, k=k: e.matmul(b[0:m, 0:n], wt[:, k * 128:k * 128 + m], src[:, k, 0:n], start=(k == 0), stop=(k == kc - 1)),
                 reads=[wt, src], writes=[b] if k == 0 else (), pwrites=() if k == 0 else [b])
        return b

    def ffn(l, n, nseq, last_tile, job_out):
        rmsnorm(n, o_nffn(l))
        ls = n // nseq
        seg = ls + 2
        for c in range(KF):
            a_t, cv_t, sg_t = aT[c % 2], cv[c % 2], sg[c % 2]
            xpv = a_t[:, 0:nseq * seg].rearrange("p (s c) -> p s c", c=seg)
            cvv = cv_t[:, 0:n].rearrange("p (s c) -> p s c", c=ls)
            if nseq == 1:
                t_src, t_dst, t_tl, t_dtl = ftail[:, l, c:c + 1, :], ftail[:, l, c:c + 1, :], ftail, ftail
            else:
                t_src, t_dst, t_tl, t_dtl = ftail_s[:, l, c, :, :], fout_s[:, l, c, :, :], ftail_s, fout_s
            ba = proj_fm("ffa%d" % l, c, n, hT, KD)
            P.op("act", lambda e, ba=ba, xpv=xpv: e.activation(out=xpv[:, :, 2:seg], in_=ba[:, 0:n].rearrange("p (s c) -> p s c", c=ls), func=AF.Copy), reads=[ba], writes=[a_t])
            P.op("pool", lambda e, xpv=xpv, t_src=t_src: e.tensor_copy(out=xpv[:, :, 0:2], in_=t_src), reads=[t_tl], pwrites=[a_t])
            bg = proj_fm("ffg%d" % l, c, n, hT, KD)
            wv = o_fcv + (l * KF + c) * 4
            P.op("dve", lambda e, xpv=xpv, cvv=cvv, wv=wv: e.tensor_scalar(out=cvv, in0=xpv[:, :, 0:ls], scalar1=vecs[:, wv:wv + 1], scalar2=None, op0=ALU.mult), reads=[a_t, vecs], writes=[cv_t])
            P.op("dve", lambda e, xpv=xpv, cvv=cvv, wv=wv: e.scalar_tensor_tensor(out=cvv, in0=xpv[:, :, 1:1 + ls], scalar=vecs[:, wv + 1:wv + 2], in1=cvv, op0=ALU.mult, op1=ALU.add), reads=[a_t, vecs, cv_t], writes=[cv_t])
            P.op("dve", lambda e, xpv=xpv, cvv=cvv, wv=wv: e.scalar_tensor_tensor(out=cvv, in0=xpv[:, :, 2:2 + ls], scalar=vecs[:, wv + 2:wv + 3], in1=cvv, op0=ALU.mult, op1=ALU.add), reads=[a_t, vecs, cv_t], writes=[cv_t])
            P.op("pool", lambda e, xpv=xpv, t_dst=t_dst: e.tensor_copy(out=t_dst, in_=xpv[:, :, ls:ls + 2]), reads=[a_t], pwrites=[t_dtl])
            P.op("act", lambda e, cv_t=cv_t, sg_t=sg_t, wv=wv: e.activation(out=sg_t[:, 0:n], in_=cv_t[:, 0:n], func=AF.Silu, bias=vecs[:, wv + 3:wv + 4], scale=1.0), reads=[cv_t, vecs], writes=[sg_t])
            P.op("dve", lambda e, sg_t=sg_t, bg=bg, c=c: e.tensor_tensor(out=actT[:, c, 0:n], in0=bg[:, 0:n], in1=sg_t[:, 0:n], op=ALU.mult), reads=[bg, sg_t], writes=[actT] if c == 0 else (), pwrites=() if c == 0 else [actT])
        if nseq == 1 and last_tile:
            P.dma("act", lambda e: e.dma_start(out=ffnc_p_d[l, job_out], in_=ftail[:, l, :, :]), ftail, reads=[ftail], is_output=True)
        for c in range(KD):
            bd = bank()
            for hf in range(2):
                wt = ws.get("ffd%d" % l, 2 * c + hf)
                for k in range(11):
                    st, sp_ = (hf == 0 and k == 0), (hf == 1 and k == 10)
                    P.op("pe", lambda e, bd=bd, wt=wt, k=k, hf=hf, st=st, sp_=sp_: e.matmul(bd[:, 0:n], wt[:, k * 128:(k + 1) * 128], actT[:, hf * 11 + k, 0:n], start=st, stop=sp_),
                         reads=[wt, actT], writes=[bd] if st else (), pwrites=() if st else [bd])
            P.op("dve", lambda e, bd=bd, c=c: e.tensor_tensor(out=xT[:, c, 0:n], in0=bd[:, 0:n], in1=xT[:, c, 0:n], op=ALU.add), reads=[bd, xT], pwrites=[xT])

    def plan_ffn(l):
        lst = []
        for c in range(KF):
            lst.append(("ffa%d" % l, c, KD * 128))
            lst.append(("ffg%d" % l, c, KD * 128))
        for c in range(2 * KD):
            lst.append(("ffd%d" % l, c, 11 * 128))
        return lst

    _wf = [0]

    def wf():
        _wf[0] += 1
        return Wf[_wf[0] % 4]

    _wb = [0]

    def wbt():
        _wb[0] += 1
        return Wb[_wb[0] % 4]

    def final_norm(n, dst_fn):
        P.op("act", lambda e: e.activation(out=sq[:, 0:KD, 0:n], in_=xT[:, :, 0:n], func=AF.Square), reads=[xT], writes=[sq])
        b = bank()
        for k in range(KD):
            P.op("pe", lambda e, k=k: e.matmul(b[:, 0:n], onesD[:], sq[:, k, 0:n], start=(k == 0), stop=(k == KD - 1)), reads=[onesD, sq], writes=[b] if k == 0 else (), pwrites=() if k == 0 else [b])
        P.op("act", lambda e: e.activation(out=rstd[:, 0:n], in_=b[:, 0:n], func=AF.Ln, bias=EPS, scale=1.0), reads=[b], writes=[rstd])
        P.op("act", lambda e: e.activation(out=rstd[:, 0:n], in_=rstd[:, 0:n], func=AF.Exp, scale=-0.5), reads=[rstd], writes=[rstd])
        for k in range(KD):
            y = wf()
            P.op("dve", lambda e, k=k, y=y: e.scalar_tensor_tensor(out=y[:, 0:n], in0=xT[:, k, 0:n], scalar=vecs[:, o_nfin + k:o_nfin + k + 1], in1=rstd[:, 0:n], op0=ALU.mult, op1=ALU.mult),
                 reads=[xT, vecs, rstd], writes=[y])
            P.dma("act", lambda e, k=k, y=y: e.dma_start(out=dst_fn(k), in_=y[:, 0:n]), y, reads=[y], is_output=True)

    def headnorm(b, n, gcol, dst_fn):
        sqb = wbt()
        P.op("act", lambda e: e.activation(out=sqb[:, 0:n], in_=b[:, 0:n], func=AF.Square), reads=[b], writes=[sqb])
        b2 = bank()
        P.op("pe", lambda e: e.matmul(b2[:, 0:n], blk64[:], sqb[:, 0:n], start=True, stop=True), reads=[blk64, sqb], writes=[b2])
        rr = wf()
        P.op("act", lambda e: e.activation(out=rr[:, 0:n], in_=b2[:, 0:n], func=AF.Ln, bias=EPS, scale=1.0), reads=[b2], writes=[rr])
        P.op("act", lambda e: e.activation(out=rr[:, 0:n], in_=rr[:, 0:n], func=AF.Exp, scale=-0.5), reads=[rr], writes=[rr])
        return rr

    def band_inproj(l, n, want_out, kout_fn, vout_fn):
        jo = l // 2
        rmsnorm(n, o_nmix(l))
        def post_qk(c, b):
            rr = headnorm(b, n, None, None)
            if c < 8:
                P.op("dve", lambda e, b=b, rr=rr, c=c: e.scalar_tensor_tensor(out=qT[:, c, 0:n], in0=b[:, 0:n], scalar=bgain[:, jo, 0:1], in1=rr[:, 0:n], op0=ALU.mult, op1=ALU.mult),
                     reads=[b, bgain, rr], writes=[qT] if c == 0 else (), pwrites=() if c == 0 else [qT])
            else:
                kf = wf()
                P.op("dve", lambda e, b=b, rr=rr, kf=kf: e.scalar_tensor_tensor(out=kf[:, 0:n], in0=b[:, 0:n], scalar=bgain[:, jo, 1:2], in1=rr[:, 0:n], op0=ALU.mult, op1=ALU.mult),
                     reads=[b, bgain, rr], writes=[kf])
                P.op("pool", lambda e, kf=kf, c=c: e.tensor_copy(out=Kcur[:, c - 8, 0:n], in_=kf[:, 0:n]), reads=[kf], writes=[Kcur] if c == 8 else (), pwrites=() if c == 8 else [Kcur])
                if want_out:
                    P.dma("act", lambda e, kf=kf, c=c: e.dma_start(out=kout_fn(c - 8), in_=kf[:, 0:n]), kf, reads=[kf], is_output=True)
        pend = []
        for c in range(16):
            b = proj_fm("oqk%d" % l, c, n, hT, KD)
            pend.append((c, b))
            if len(pend) > 1:
                post_qk(*pend.pop(0))
        while pend:
            post_qk(*pend.pop(0))
        ntb = n // 128
        for g in range(4):
            bks = [bank() for _ in range(ntb)]
            for kh in range(2):
                wt = ws.get("ov%d" % l, 2 * g + kh)
                for tb in range(ntb):
                    b = bks[tb]
                    for kk in range(4):
                        k = kh * 4 + kk
                        P.op("pe", lambda e, b=b, k=k, kk=kk, tb=tb, wt=wt: e.matmul(b[:, 0:256], hT[:, k, tb * 128:(tb + 1) * 128], wt[:, kk * 256:(kk + 1) * 256], start=(k == 0), stop=(k == KD - 1)),
                             reads=[wt, hT], writes=[b] if k == 0 else (), pwrites=() if k == 0 else [b])
            for tb in range(ntb):
                b = bks[tb]
                first = (g == 0 and tb == 0)
                P.op("act", lambda e, b=b, tb=tb, g=g: e.activation(out=Vcur[:, tb, g * 256:(g + 1) * 256], in_=b[:, 0:256], func=AF.Copy), reads=[b], writes=[Vcur] if first else (), pwrites=() if first else [Vcur])
                if want_out:
                    vf = wf()
                    P.op("dve", lambda e, b=b, vf=vf: e.tensor_copy(out=vf[:, 0:256], in_=b[:, 0:256]), reads=[b], writes=[vf])
                    P.dma("act", lambda e, vf=vf, tb=tb, g=g: e.dma_start(out=vout_fn(tb, g), in_=vf[:, 0:256]), vf, reads=[vf], is_output=True)

    _bc = [0]

    def band_qgroup(jo, c0, nq, blocks, is_prompt):
        valid = [jb for jb in range(5) if blocks[jb] is not None]
        far = [jb for jb in valid if jb <= 2]
        def stage1(h):
            hg, hh = h // 4, h % 4
            hc, hp, hl = h // 2, h % 2, hh // 2
            r = _bc[0] % 3
            _bc[0] += 1
            bA, bB = banks[2 + 2 * r], banks[3 + 2 * r]
            ptf, ptn, tn = PTf[r], PTn[r], tnr[_bc[0] % 2]
            hs = slice(hp * 64, hp * 64 + 64)
            firstA, firstB = True, True
            for jb in valid:
                Kt, koff, Vt, vtb, nk, pb = blocks[jb]
                if jb <= 2:
                    dstb, col = bA, jb * nq
                else:
                    dstb, col = bB, (0 if jb == 4 else nq)
                fl = firstA if jb <= 2 else firstB
                P.op("pe", lambda e, dstb=dstb, col=col, Kt=Kt, koff=koff, nk=nk, pb=pb, hs=hs, hc=hc: e.matmul(dstb[pb:pb + nk, col:col + nq], Kt[hs, hc, koff:koff + nk], qT[hs, hc, c0:c0 + nq], start=True, stop=True),
                     reads=[Kt, qT], writes=[dstb] if fl else (), pwrites=() if fl else [dstb])
                if jb <= 2:
                    firstA = False
                else:
                    firstB = False
            if far:
                f0 = far[0] * nq
                P.op("act", lambda e, bA=bA, ptf=ptf, f0=f0, h=h: e.activation(out=ptf[:, f0:3 * nq], in_=bA[:, f0:3 * nq], func=AF.Exp, bias=bfar[:, jo, h:h + 1], scale=0.125),
                     reads=[bA, bfar], writes=[ptf])
                if is_prompt and 0 in far:
                    P.op("pool", lambda e, ptf=ptf: e.memset(ptf[0:64, 64:128], 0.0), reads=(), pwrites=[ptf])
            _, _, _, _, nk4, pb4 = blocks[4]
            if 3 in valid and nk4 == 128 and nq == 128:
                P.op("dve", lambda e, bB=bB, tn=tn, h=h: e.scalar_tensor_tensor(out=tn[:, 0:256], in0=bB[:, 0:256], scalar=0.125, in1=Tn[:, h, 0:256], op0=ALU.mult, op1=ALU.add), reads=[bB, Tn], writes=[tn])
                P.op("act", lambda e, tn=tn, ptn=ptn: e.activation(out=ptn[:, 0:256], in_=tn[:, 0:256], func=AF.Exp), reads=[tn], writes=[ptn])
            else:
                ps4 = slice(pb4, pb4 + nk4)
                tcol = pb4 if nq == 64 else 0
                P.op("dve", lambda e, bB=bB, tn=tn, h=h, ps4=ps4, tcol=tcol: e.scalar_tensor_tensor(out=tn[ps4, 0:nq], in0=bB[ps4, 0:nq], scalar=0.125, in1=Tn[ps4, h, tcol:tcol + nq], op0=ALU.mult, op1=ALU.add), reads=[bB, Tn], writes=[tn])
                P.op("act", lambda e, tn=tn, ptn=ptn, ps4=ps4: e.activation(out=ptn[ps4, 0:nq], in_=tn[ps4, 0:nq], func=AF.Exp), reads=[tn], writes=[ptn])
                if 3 in valid:
                    P.op("dve", lambda e, bB=bB, tn=tn, h=h: e.scalar_tensor_tensor(out=tn[:, nq:2 * nq], in0=bB[:, nq:2 * nq], scalar=0.125, in1=Tn[:, h, 128:128 + nq], op0=ALU.mult, op1=ALU.add), reads=[bB, Tn], pwrites=[tn])
                    P.op("act", lambda e, tn=tn, ptn=ptn: e.activation(out=ptn[:, nq:2 * nq], in_=tn[:, nq:2 * nq], func=AF.Exp), reads=[tn], pwrites=[ptn])
            return dict(h=h, hg=hg, hh=hh, hc=hc, hp=hp, hl=hl, hs=hs, ptf=ptf, ptn=ptn)

        def stage2(cx):
            h, hg, hh, hc, hp, hl, hs, ptf, ptn = (cx[k_] for k_ in ("h", "hg", "hh", "hc", "hp", "hl", "hs", "ptf", "ptn"))
            acc = banks[hg % 2]
            srcs = []
            for jb in valid:
                Kt, koff, Vt, vtb, nk, pb = blocks[jb]
                if jb <= 2:
                    srcs.append((ptf[pb:pb + nk, jb * nq:(jb + 1) * nq], ptf, Vt, vtb, nk, pb))
                else:
                    col = 0 if jb == 4 else nq
                    srcs.append((ptn[pb:pb + nk, col:col + nq], ptn, Vt, vtb, nk, pb))
            for i, (src, srct, Vt, vtb, nk, pb) in enumerate(srcs):
                st, sp_ = (i == 0), (i == len(srcs) - 1)
                fw = (hh == 0 and i == 0)
                P.op("pe", lambda e, acc=acc, hs=hs, hl=hl, Vt=Vt, vtb=vtb, pb=pb, nk=nk, h=h, src=src, st=st, sp_=sp_: e.matmul(acc[hs, hl * nq:(hl + 1) * nq], Vt[pb:pb + nk, vtb, h * 64:(h + 1) * 64], src, start=st, stop=sp_),
                     reads=[Vt, srct], writes=[acc] if fw else (), pwrites=() if fw else [acc])
            for i, (src, srct, Vt, vtb, nk, pb) in enumerate(srcs):
                st, sp_ = (i == 0), (i == len(srcs) - 1)
                P.op("pe", lambda e, acc=acc, hs=hs, hl=hl, pb=pb, nk=nk, src=src, st=st, sp_=sp_: e.matmul(acc[hs, 256 + hl * nq:256 + (hl + 1) * nq], ones64[pb:pb + nk, :], src, start=st, stop=sp_),
                     reads=[ones64, srct], pwrites=[acc])
            if hh == 3:
                norm_group(hg)

        def norm_group(hg):
            acc = banks[hg % 2]
            rd = wf()
            P.op("act", lambda e, acc=acc, rd=rd: e.activation(out=rd[:, 0:2 * nq], in_=acc[:, 256:256 + 2 * nq], func=AF.Ln), reads=[acc], writes=[rd])
            P.op("act", lambda e, rd=rd: e.activation(out=rd[:, 0:2 * nq], in_=rd[:, 0:2 * nq], func=AF.Exp, scale=-1.0), reads=[rd], writes=[rd])
            P.op("dve", lambda e, acc=acc, rd=rd, hg=hg: e.tensor_tensor(out=oT[:, 2 * hg:2 * hg + 2, c0:c0 + nq], in0=acc[:, 0:2 * nq].rearrange("p (a b) -> p a b", a=2), in1=rd[:, 0:2 * nq].rearrange("p (a b) -> p a b", a=2), op=ALU.mult),
                 reads=[acc, rd], pwrites=[oT])


        prev = None
        for h in range(H_C):
            cx = stage1(h)
            if prev is not None:
                stage2(prev)
            prev = cx
        stage2(prev)

    def out_proj(key, l, n, src, kc):
        for c in range(KD):
            b = proj_fm(key, c, n, src, kc)
            P.op("dve", lambda e, b=b, c=c: e.tensor_tensor(out=xT[:, c, 0:n], in0=b[:, 0:n], in1=xT[:, c, 0:n], op=ALU.add), reads=[b, xT], pwrites=[xT])

    def band_layer_prompt(l, j, t):
        jo = l // 2
        n = TT
        last = (t == cfg.ntile - 1)
        P.dma("sp", lambda e: e.dma_start(out=Tn[:], in_=btab_d[jo]), Tn, writes=[Tn])
        if t > 0:
            sl = (t - 1) % 2
            P.dma("sp", lambda e: e.dma_start(out=Kprev[:].rearrange("p a b -> p (a b)"), in_=kscr_d[jo, sl]), Kprev, reads=[kscr_tl[jo][sl]], writes=[Kprev])
            P.dma("sp", lambda e: e.dma_start(out=Vprev[:].rearrange("p a b -> p (a b)"), in_=vscr_d[jo, sl]), Vprev, reads=[vscr_tl[jo][sl]], writes=[Vprev])
        band_inproj(l, n, last, lambda c: bk_p_d[jo, j, :, c, :], lambda tb, g: bv_p_d[jo, j, tb * 128:(tb + 1) * 128, g * 256:(g + 1) * 256])
        if not last:
            sl = t % 2
            P.dma("act", lambda e: e.dma_start(out=kscr_d[jo, sl], in_=Kcur[:].rearrange("p a b -> p (a b)")), Kcur, reads=[Kcur], writes=[kscr_tl[jo][sl]])
            P.dma("act", lambda e: e.dma_start(out=vscr_d[jo, sl], in_=Vcur[:].rearrange("p a b -> p (a b)")), Vcur, reads=[Vcur], writes=[vscr_tl[jo][sl]])
        for m in range(4):
            blocks = []
            for jb in range(5):
                rel = m + jb - 4
                if t * 4 + rel < 0:
                    blocks.append(None)
                elif rel >= 0:
                    blocks.append((Kcur, rel * 128, Vcur, rel, 128, 0))
                else:
                    blocks.append((Kprev, (4 + rel) * 128, Vprev, 4 + rel, 128, 0))
            band_qgroup(jo, m * 128, 128, blocks, True)
        out_proj("oo%d" % l, l, n, oT, KD)

    def band_layer_sample(l):
        jo = l // 2
        n = ns_tok
        ls = cfg.dec_seq
        P.dma("sp", lambda e: e.dma_start(out=Tn[:], in_=btab_d[jo]), Tn, writes=[Tn])
        band_inproj(l, n, True, lambda c: bk_s_d[jo, :, c, :], lambda tb, g: bv_s_d[jo, tb * 128:(tb + 1) * 128, g * 256:(g + 1) * 256])
        for s_ in range(cfg.n_sseq):
            for c in range(KD):
                f = wf()
                P.dma("sp", lambda e, f=f, c=c, s_=s_: e.dma_start(out=f[:, 0:512], in_=bkc_d[jo, s_, :, c, :]), f, writes=[f])
                P.op("pool", lambda e, f=f, c=c: e.tensor_copy(out=Kprev[:, c, :], in_=f[:, 0:512]), reads=[f], writes=[Kprev] if c == 0 else (), pwrites=() if c == 0 else [Kprev])
            for tb in range(4):
                for hf in range(2):
                    f = wf()
                    P.dma("sp", lambda e, f=f, tb=tb, hf=hf, s_=s_: e.dma_start(out=f[:, 0:512], in_=bvc_d[jo, s_, :, tb, hf * 512:(hf + 1) * 512]), f, writes=[f])
                    fw = (tb == 0 and hf == 0)
                    P.op("pool", lambda e, f=f, tb=tb, hf=hf: e.tensor_copy(out=Vprev[:, tb, hf * 512:(hf + 1) * 512], in_=f[:, 0:512]), reads=[f], writes=[Vprev] if fw else (), pwrites=() if fw else [Vprev])
            blocks = [(Kprev, jb * 128, Vprev, jb, 128, 0) for jb in range(4)]
            pb = (s_ % 2) * 64
            blocks.append((Kcur, s_ * ls, Vcur, (s_ * ls) // 128, ls, pb))
            band_qgroup(jo, s_ * ls, ls, blocks, False)
        out_proj("oo%d" % l, l, n, oT, KD)


    qbT = qT
    qiT = Kcur
    isc_t = Tn
    iscv = Tn[:].rearrange("p a b -> p (a b)")
    junk = Vprev
    junkv = Vprev[:].rearrange("p a b -> p (a b)")
    negsel = Vcur
    negvf = Vcur[:].rearrange("p a b -> p (a b)")
    NIT = 12

    def norm64(b, n, gcol):
        sqb = wbt()
        P.op("act", lambda e: e.activation(out=sqb[0:64, 0:n], in_=b[0:64, 0:n], func=AF.Square), reads=[b], writes=[sqb])
        b2 = bank()
        P.op("pe", lambda e: e.matmul(b2[0:64, 0:n], blk64[0:64, 0:64], sqb[0:64, 0:n], start=True, stop=True), reads=[blk64, sqb], writes=[b2])
        rr = wf()
        P.op("act", lambda e: e.activation(out=rr[0:64, 0:n], in_=b2[0:64, 0:n], func=AF.Ln, bias=EPS, scale=1.0), reads=[b2], writes=[rr])
        P.op("act", lambda e: e.activation(out=rr[0:64, 0:n], in_=rr[0:64, 0:n], func=AF.Exp, scale=-0.5), reads=[rr], writes=[rr])
        return rr

    def dsa_inproj(l, n, kcol0, vblk0, kout, vout, iout):
        e_ = l // 2
        key = "eB%d" % l
        def post_qb(h, b):
            rr = norm64(b, n, None)
            P.op("dve", lambda e, b=b, rr=rr, h=h: e.scalar_tensor_tensor(out=qbT[0:64, h, 0:n], in0=b[0:64, 0:n], scalar=egain[0:64, e_, 0:1], in1=rr[0:64, 0:n], op0=ALU.mult, op1=ALU.mult),
                 reads=[b, egain, rr], writes=[qbT] if h == 0 else (), pwrites=() if h == 0 else [qbT])
        pend = []
        for h in range(H_B):
            b = proj_fm(key, h, n, hT, KD, m=64)
            pend.append((h, b))
            if len(pend) > 1:
                post_qb(*pend.pop(0))
        while pend:
            post_qb(*pend.pop(0))
        b = proj_fm(key, 8, n, hT, KD)
        rr = norm64(b, n, None)
        kv = wf()
        P.op("dve", lambda e, b=b, rr=rr, kv=kv: e.scalar_tensor_tensor(out=kv[0:64, 0:n], in0=b[0:64, 0:n], scalar=egain[0:64, e_, 1:2], in1=rr[0:64, 0:n], op0=ALU.mult, op1=ALU.mult), reads=[b, egain, rr], writes=[kv])
        P.op("dve", lambda e, b=b, kv=kv: e.tensor_copy(out=kv[64:128, 0:n], in_=b[64:128, 0:n]), reads=[b], pwrites=[kv])
        P.op("pool", lambda e, kv=kv: e.tensor_copy(out=kbT_c[e_][0:64, kcol0:kcol0 + n], in_=kv[0:64, 0:n]), reads=[kv], pwrites=[kbT_c[e_]])
        vbf = wbt()
        P.op("pool", lambda e, kv=kv, vbf=vbf: e.tensor_copy(out=vbf[64:128, 0:n], in_=kv[64:128, 0:n]), reads=[kv], writes=[vbf])
        P.dma("act", lambda e, kv=kv: e.dma_start(out=kout, in_=kv[0:64, 0:n]), kv, reads=[kv], is_output=True)
        P.dma("act", lambda e, kv=kv: e.dma_start(out=vout, in_=kv[64:128, 0:n]), kv, reads=[kv], is_output=True)
        for tb in range(n // 128):
            b2 = bank()
            P.op("pe", lambda e, b2=b2, vbf=vbf, tb=tb: e.matmul(b2[:, 0:64], vbf[64:128, tb * 128:(tb + 1) * 128], ident[64:128, 64:128], start=True, stop=True), reads=[vbf, ident], writes=[b2])
            P.op("act", lambda e, b2=b2, tb=tb: e.activation(out=Vb_c[e_][:, vblk0 + tb, :], in_=b2[:, 0:64], func=AF.Copy), reads=[b2], pwrites=[Vb_c[e_]])
        for h in range(H_IDX):
            b = proj_fm(key, 9 + h, n, hT, KD, m=64)
            P.op("act", lambda e, b=b, h=h: e.activation(out=qiT[0:64, h, 0:n], in_=b[0:64, 0:n], func=AF.Copy), reads=[b], writes=[qiT] if h == 0 else (), pwrites=() if h == 0 else [qiT])
        b = proj_fm(key, 13, n, hT, KD)
        kw = wf()
        P.op("act", lambda e, b=b, kw=kw: e.activation(out=kw[:, 0:n], in_=b[:, 0:n], func=AF.Copy), reads=[b], writes=[kw])
        P.op("pool", lambda e, kw=kw: e.tensor_copy(out=kiT_c[e_][0:64, kcol0:kcol0 + n], in_=kw[0:64, 0:n]), reads=[kw], pwrites=[kiT_c[e_]])
        P.dma("act", lambda e, kw=kw: e.dma_start(out=iout, in_=kw[0:64, 0:n]), kw, reads=[kw], is_output=True)
        for tb in range(n // 128):
            b2 = bank()
            P.op("pe", lambda e, b2=b2, kw=kw, tb=tb: e.matmul(b2[:, 0:4], kw[64:68, tb * 128:(tb + 1) * 128], identf[64:68, 64:68], start=True, stop=True), reads=[kw, identf], writes=[b2])
            P.op("act", lambda e, b2=b2, tb=tb: e.activation(out=wi_tm[:, tb, :], in_=b2[:, 0:4], func=AF.Copy), reads=[b2], writes=[wi_tm] if tb == 0 else (), pwrites=() if tb == 0 else [wi_tm])

    _dq = [0]

    def dsa_qblock(e_, nq, c0, wrow, wtb, segs, blocks, NK, mask_a, select, par):
        qs = wrow
        negv = negvf[:, par * 2048:(par + 1) * 2048]
        return (lambda: _dsa_idx(e_, nq, c0, qs, wtb, segs), lambda: _dsa_sel(nq, qs, NK, mask_a, select, negv),
                lambda: _dsa_main(e_, nq, c0, qs, blocks, negv), lambda: _dsa_epi(nq, c0))

    def _dsa_idx(e_, nq, c0, qs, wtb, segs):
        wrow = qs
        for hi in range(H_IDX):
            col = 0
            for (kap, w) in segs:
                bq = banks[4 + (_dq[0] % 4)]
                _dq[0] += 1
                P.op("pe", lambda e, bq=bq, kap=kap, w=w, hi=hi: e.matmul(bq[qs, 0:w], qiT[0:64, hi, c0:c0 + nq], kap, start=True, stop=True), reads=[qiT, kiT_c[e_]], writes=[bq])
                if hi == 0:
                    P.op("dve", lambda e, bq=bq, w=w, col=col: e.tensor_scalar(out=iscv[qs, col:col + w], in0=bq[qs, 0:w], scalar1=0.0, scalar2=wi_tm[wrow, wtb, 0:1], op0=ALU.max, op1=ALU.mult),
                         reads=[bq, wi_tm], writes=[isc_t] if col == 0 else (), pwrites=() if col == 0 else [isc_t])
                else:
                    r_ = wf()
                    P.op("act", lambda e, bq=bq, r_=r_, w=w: e.activation(out=r_[qs, 0:w], in_=bq[qs, 0:w], func=AF.Relu), reads=[bq], writes=[r_])
                    P.op("dve", lambda e, r_=r_, w=w, col=col, hi=hi: e.scalar_tensor_tensor(out=iscv[qs, col:col + w], in0=r_[qs, 0:w], scalar=wi_tm[wrow, wtb, hi:hi + 1], in1=iscv[qs, col:col + w], op0=ALU.mult, op1=ALU.add),
                         reads=[r_, wi_tm, isc_t], pwrites=[isc_t])
                col += w

    def _dsa_sel(nq, qs, NK, mask_a, select, negv):
        if select:
            P.op("dve", lambda e: e.tensor_reduce(out=bis[qs, 0:1], in_=iscv[qs, 0:NK], axis=mybir.AxisListType.X, op=ALU.min), reads=[isc_t], writes=[bis])
            P.op("dve", lambda e: e.tensor_reduce(out=bis[qs, 1:2], in_=iscv[qs, 0:NK], axis=mybir.AxisListType.X, op=ALU.max), reads=[isc_t], pwrites=[bis])
        if mask_a:
            P.op("dve", lambda e: e.memset(iscv[0:64, NK - 64:NK], -1e30), reads=[isc_t], pwrites=[isc_t])
        if select:
            P.op("dve", lambda e: e.tensor_tensor(out=bis[qs, 2:3], in0=bis[qs, 1:2], in1=bis[qs, 0:1], op=ALU.subtract), reads=[bis], pwrites=[bis])
            P.op("dve", lambda e: e.tensor_scalar(out=wtab[qs, 0:NIT + 1], in0=cst[qs, 1218:1218 + NIT + 1], scalar1=bis[qs, 2:3], scalar2=None, op0=ALU.mult), reads=[cst, bis], writes=[wtab])
            P.op("dve", lambda e: e.tensor_tensor(out=bis[qs, 4:5], in0=bis[qs, 0:1], in1=wtab[qs, 0:1], op=ALU.add), reads=[bis, wtab], pwrites=[bis])
            for it in range(NIT):
                P.op("dve", lambda e: e.tensor_scalar(out=junkv[qs, 0:NK], in0=iscv[qs, 0:NK], scalar1=bis[qs, 4:5], scalar2=None, op0=ALU.is_ge, op1=ALU.add, accum_out=bis[qs, 5:6]), reads=[isc_t, bis], writes=[junk], pwrites=[bis])
                P.op("dve", lambda e, it=it: e.scalar_tensor_tensor(out=bis[qs, 6:7], in0=bis[qs, 5:6], scalar=float(TOPK_MAX), in1=wtab[qs, it:it + 1], op0=ALU.is_ge, op1=ALU.mult), reads=[bis, wtab], pwrites=[bis])
                P.op("dve", lambda e, it=it: e.scalar_tensor_tensor(out=bis[qs, 4:5], in0=bis[qs, 4:5], scalar=wtab[qs, it + 1:it + 2], in1=bis[qs, 6:7], op0=ALU.subtract, op1=ALU.add), reads=[bis, wtab], pwrites=[bis])
            P.op("dve", lambda e: e.tensor_tensor(out=bis[qs, 3:4], in0=bis[qs, 4:5], in1=wtab[qs, NIT:NIT + 1], op=ALU.subtract), reads=[bis, wtab], pwrites=[bis])
            P.op("dve", lambda e: e.tensor_scalar(out=negv[qs, 0:NK], in0=iscv[qs, 0:NK], scalar1=bis[qs, 3:4], scalar2=NEG, op0=ALU.is_lt, op1=ALU.mult), reads=[isc_t, bis], writes=[negsel])
        else:
            P.op("dve", lambda e: e.tensor_scalar(out=negv[qs, 0:NK], in0=iscv[qs, 0:NK], scalar1=-1e29, scalar2=NEG, op0=ALU.is_lt, op1=ALU.mult), reads=[isc_t], writes=[negsel])

    def _dsa_main(e_, nq, c0, qs, blocks, negv):
        hpb = 512 // nq
        nbk = H_B // hpb
        idr = idrep if nq == 128 else idrep64
        accO = [banks[0], banks[1]][:nbk]
        accD = [banks[2], banks[3]][:nbk]
        def st1(bi):
            kap, vap, nk, pb, col0 = blocks[bi]
            pt = PTd[bi % 2]
            ks = slice(pb, pb + nk)
            for g in range(nbk):
                bs = banks[4 + 2 * (bi % 2) + g]
                P.op("pe", lambda e, bs=bs, kap=kap, g=g, ks=ks: e.matmul(bs[ks, 0:512], kap, qbT[0:64, g * hpb:(g + 1) * hpb, c0:c0 + nq], start=True, stop=False), reads=[kbT_c[e_], qbT], writes=[bs])
                P.op("pe", lambda e, bs=bs, ks=ks, col0=col0, nk=nk: e.matmul(bs[ks, 0:512], negv[qs, col0:col0 + nk], idr[qs, :], start=False, stop=True), reads=[negsel, idr], pwrites=[bs], pe_sync=(qs.start != 0))
                P.op("act", lambda e, bs=bs, pt=pt, g=g, ks=ks: e.activation(out=pt[ks, g * 512:(g + 1) * 512], in_=bs[ks, 0:512], func=AF.Exp, scale=0.125), reads=[bs], writes=[pt] if g == 0 else (), pwrites=() if g == 0 else [pt])

        def st2(bi):
            kap, vap, nk, pb, col0 = blocks[bi]
            pt = PTd[bi % 2]
            ks = slice(pb, pb + nk)
            st, sp_ = (bi == 0), (bi == len(blocks) - 1)
            for g in range(nbk):
                P.op("pe", lambda e, g=g, vap=vap, pt=pt, ks=ks, st=st, sp_=sp_: e.matmul(accO[g][0:64, 0:512], vap, pt[ks, g * 512:(g + 1) * 512], start=st, stop=sp_), reads=[Vb_c[e_], pt], writes=[accO[g]] if st else (), pwrites=() if st else [accO[g]])
                P.op("pe", lambda e, g=g, pt=pt, ks=ks, st=st, sp_=sp_: e.matmul(accD[g][0:64, 0:512], ones64[ks, :], pt[ks, g * 512:(g + 1) * 512], start=st, stop=sp_), reads=[ones64, pt], writes=[accD[g]] if st else (), pwrites=() if st else [accD[g]])

        st1(0)
        for bi in range(1, len(blocks)):
            st1(bi)
            st2(bi - 1)
        st2(len(blocks) - 1)

    def _dsa_epi(nq, c0):
        hpb = 512 // nq
        nbk = H_B // hpb
        accO = [banks[0], banks[1]][:nbk]
        accD = [banks[2], banks[3]][:nbk]
        for g in range(nbk):
            rd = wf()
            P.op("act", lambda e, g=g, rd=rd: e.activation(out=rd[0:64, :], in_=accD[g][0:64, :], func=AF.Ln), reads=[accD[g]], writes=[rd])
            P.op("act", lambda e, rd=rd: e.activation(out=rd[0:64, :], in_=rd[0:64, :], func=AF.Exp, scale=-1.0), reads=[rd], writes=[rd])
            P.op("dve", lambda e, g=g, rd=rd: e.tensor_tensor(out=oTb[0:64, g * hpb:(g + 1) * hpb, c0:c0 + nq], in0=accO[g][0:64, :].rearrange("p (a b) -> p a b", a=hpb), in1=rd[0:64, :].rearrange("p (a b) -> p a b", a=hpb), op=ALU.mult),
                 reads=[accO[g], rd], pwrites=[oTb])

    def dsa_pipeline(qbs, started=False):
        if not started:
            qbs[0][0]()
            qbs[0][1]()
        for m in range(1, len(qbs)):
            qbs[m][0]()
            qbs[m - 1][2]()
            qbs[m - 1][3]()
            qbs[m][1]()
        qbs[-1][2]()
        qbs[-1][3]()
        return
        qbs[0][0]()
        qbs[0][1]()
        for m in range(1, len(qbs)):
            qbs[m][0]()
            qbs[m - 1][2]()
            qbs[m][1]()
            qbs[m - 1][3]()
        qbs[-1][2]()
        qbs[-1][3]()

    def even_out_proj(l, n, with_a):
        key = "eo%d" % l
        for c in range(KD):
            wt = ws.get(key, c)
            b = bank()
            ks = list(range(4)) if with_a else []
            ks += list(range(4, 12))
            for i, k in enumerate(ks):
                st, sp_ = (i == 0), (i == len(ks) - 1)
                if k < 4:
                    P.op("pe", lambda e, b=b, wt=wt, k=k, st=st, sp_=sp_: e.matmul(b[:, 0:n], wt[:, k * 128:(k + 1) * 128], oT[:, k, 0:n], start=st, stop=sp_), reads=[wt, oT], writes=[b] if st else (), pwrites=() if st else [b])
                else:
                    P.op("pe", lambda e, b=b, wt=wt, k=k, st=st, sp_=sp_: e.matmul(b[:, 0:n], wt[0:64, k * 128:(k + 1) * 128], oTb[0:64, k - 4, 0:n], start=st, stop=sp_), reads=[wt, oTb], writes=[b] if st else (), pwrites=() if st else [b])
            P.op("dve", lambda e, b=b, c=c: e.tensor_tensor(out=xT[:, c, 0:n], in0=b[:, 0:n], in1=xT[:, c, 0:n], op=ALU.add), reads=[b, xT], pwrites=[xT])

    def even_layer_prompt(l, j, t):
        e_ = l // 2
        n = TT
        t0 = t * TT
        rmsnorm(n, o_nmix(l))
        dst = dn_layer_A(l, n, 1, j, t)
        dsa_inproj(l, n, t0, t * 4, dk_p_d[e_, j, :, t0:t0 + n], dv_p_d[e_, j, :, t0:t0 + n], di_p_d[e_, j, :, t0:t0 + n])
        qbs = []
        for m in range(4):
            NK = t0 + (m + 1) * 128
            segs = []
            c_ = 0
            while c_ < NK:
                w = min(512, NK - c_)
                segs.append((kiT_c[e_][0:64, c_:c_ + w], w))
                c_ += w
            blocks = [(kbT_c[e_][0:64, kb * 128:(kb + 1) * 128], Vb_c[e_][:, kb, :], 128, 0, kb * 128) for kb in range(NK // 128)]
            qbs.append(dsa_qblock(e_, 128, m * 128, slice(0, 128), m, segs, blocks, NK, True, NK > TOPK_MAX, m % 2))
        qbs[0][0]()
        qbs[0][1]()
        dn_recur(dst)
        dsa_pipeline(qbs, started=True)
        even_out_proj(l, n, DN_ON)

    def even_layer_sample(l):
        e_ = l // 2
        n = ns_tok
        ls = cfg.dec_seq
        PC = cfg.past
        rmsnorm(n, o_nmix(l))
        dst = dn_layer_A(l, n, cfg.n_sseq, None, None)
        dsa_inproj(l, n, PC, PC // 128, dk_s_d[e_, :, :], dv_s_d[e_, :, :], di_s_d[e_, :, :])
        dn_recur(dst)
        for s_ in range(cfg.n_sseq):
            for hf in range(PC // 512):
                for (src, dstc) in ((dkc_d, kbT_c[e_]), (dic_d, kiT_c[e_])):
                    f = wf()
                    P.dma("sp", lambda e, f=f, src=src, hf=hf, s_=s_: e.dma_start(out=f[0:64, 0:512], in_=src[e_, s_, :, hf * 512:(hf + 1) * 512]), f, writes=[f])
                    P.op("pool", lambda e, f=f, dstc=dstc, hf=hf: e.tensor_copy(out=dstc[0:64, hf * 512:(hf + 1) * 512], in_=f[0:64, 0:512]), reads=[f], pwrites=[dstc])
            f = wf()
            nvb = PC // 128
            P.dma("sp", lambda e, f=f, s_=s_: e.dma_start(out=f[:, 0:nvb * 64].rearrange("p (a b) -> p a b", a=nvb), in_=dvc_d[e_, s_, :, :, :]), f, writes=[f])
            P.op("pool", lambda e, f=f: e.tensor_copy(out=Vb_c[e_][:, 0:nvb, :], in_=f[:, 0:nvb * 64].rearrange("p (a b) -> p a b", a=nvb)), reads=[f], pwrites=[Vb_c[e_]])
            NK = PC + ls
            segs = []
            c_ = 0
            while c_ < PC:
                w = min(512, PC - c_)
                segs.append((kiT_c[e_][0:64, c_:c_ + w], w))
                c_ += w
            segs.append((kiT_c[e_][0:64, PC + s_ * ls:PC + (s_ + 1) * ls], ls))
            blocks = [(kbT_c[e_][0:64, kb * 128:(kb + 1) * 128], Vb_c[e_][:, kb, :], 128, 0, kb * 128) for kb in range(PC // 128)]
            pb = (s_ % 2) * 64
            blocks.append((kbT_c[e_][0:64, PC + s_ * ls:PC + (s_ + 1) * ls], Vb_c[e_][pb:pb + ls, PC // 128 + s_ // 2, :], ls, pb, PC))
            st_ = dsa_qblock(e_, ls, s_ * ls, slice(pb, pb + ls), s_ // 2, segs, blocks, NK, False, NK > TOPK_MAX, 0)
            for f_ in st_:
                f_()
        even_out_proj(l, n, DN_ON)

    DN_ON = True
    actflat = actT[:].rearrange("p a b -> p (a b)")
    qkn = actT[:, 13:21, :]
    vzv = Tn[:].rearrange("p a b -> p (a b)")[:, 2048:4096].bitcast(BF16).rearrange("p (a b) -> p a b", a=8)
    vz_t = Tn
    orawv = hT[:].rearrange("p a b -> p (a b)").bitcast(F32).rearrange("p (a b) -> p a b", a=4)
    FW = [aT[0], aT[1], cv[0], cv[1], sg[0], sg[1]]
    kpv = Kprev[:].rearrange("p a b -> p (a b)")
    BW = [(Kprev, kpv[:, i * 512:(i + 1) * 512]) for i in range(8)]
    BW += [(PTd[0], PTd[0][:, 0:512]), (PTd[0], PTd[0][:, 512:1024]), (PTd[1], PTd[1][:, 0:512]), (PTd[1], PTd[1][:, 512:1024])]
    cU, cSC, cNU, cNSU, cNSL, cALL1 = (cst[:, 448:576], cst[:, 576:704], cst[:, 704:832], cst[:, 832:960], cst[:, 960:1088], cst[:, 1088:1216])
    cCH = cst[:, 1216:1218]

    def v4(ap):
        return ap.rearrange("p (a b) -> p a b", a=4)

    def bc_h(ap128):
        return ap128.unsqueeze(1).to_broadcast([128, 4, 128])

    def bc_x(ap4):
        return ap4.unsqueeze(2).to_broadcast([128, 4, 128])

    def dn_layer_A(l, n, nseq, j, t):
        e_ = l // 2
        key = "eA%d" % l
        ls = n // nseq
        seg = ls + 3
        SEGT = nseq * seg
        nblk = n // 128
        prompt = (nseq == 1)
        if prompt and t == 0:
            P.op("pool", lambda e: e.memset(S_f[e_][:], 0.0), writes=[S_f[e_]])
            P.op("pool", lambda e: e.memset(S_b[e_][:], 0.0), writes=[S_b[e_]])
            P.op("pool", lambda e: e.memset(dtail[e_][:], 0.0), writes=[dtail[e_]])
        for c in range(12):
            b = proj_fm(key, c, n, hT, KD)
            xv = actflat[:, c * SEGT:(c + 1) * SEGT].rearrange("p (s w) -> p s w", w=seg)
            P.op("act", lambda e, b=b, xv=xv: e.activation(out=xv[:, :, 3:seg], in_=b[:, 0:n].rearrange("p (s w) -> p s w", w=ls), func=AF.Copy), reads=[b], pwrites=[actT])
            tsrc = dtail[e_][:, c:c + 1, :] if prompt else dcst[:, e_, c, :, :]
            ttl = dtail[e_] if prompt else dcst
            P.op("pool", lambda e, xv=xv, tsrc=tsrc: e.tensor_copy(out=xv[:, :, 0:3], in_=tsrc), reads=[ttl], pwrites=[actT])
            acc = FW[2 + c % 2]
            accv = acc[:, 0:n].rearrange("p (s w) -> p s w", w=ls)
            P.op("dve", lambda e, xv=xv, accv=accv, c=c: e.tensor_scalar(out=accv, in0=xv[:, :, 0:ls], scalar1=dnw[:, e_, c, 0:1], scalar2=None, op0=ALU.mult), reads=[actT, dnw], writes=[acc])
            for jt in range(1, 4):
                P.op("dve", lambda e, xv=xv, accv=accv, c=c, jt=jt: e.scalar_tensor_tensor(out=accv, in0=xv[:, :, jt:jt + ls], scalar=dnw[:, e_, c, jt:jt + 1], in1=accv, op0=ALU.mult, op1=ALU.add), reads=[actT, dnw, acc], writes=[acc])
            if prompt:
                P.op("pool", lambda e, xv=xv, c=c: e.tensor_copy(out=dtail[e_][:, c:c + 1, :], in_=xv[:, :, ls:ls + 3]), reads=[actT], pwrites=[dtail[e_]])
            else:
                P.op("pool", lambda e, xv=xv, c=c: e.tensor_copy(out=dcso[:, e_, c, :, :], in_=xv[:, :, ls:ls + 3]), reads=[actT], pwrites=[dcso])
            if c < 8:
                raw = wbt()
                P.op("act", lambda e, acc=acc, raw=raw: e.activation(out=raw[:, 0:n], in_=acc[:, 0:n], func=AF.Silu), reads=[acc], writes=[raw])
                sqb = wbt()
                P.op("act", lambda e, raw=raw, sqb=sqb: e.activation(out=sqb[:, 0:n], in_=raw[:, 0:n], func=AF.Square), reads=[raw], writes=[sqb])
                b2 = bank()
                P.op("pe", lambda e, b2=b2, sqb=sqb: e.matmul(b2[:, 0:n], onesD[:], sqb[:, 0:n], start=True, stop=True), reads=[onesD, sqb], writes=[b2])
                rr = wf()
                P.op("act", lambda e, b2=b2, rr=rr: e.activation(out=rr[:, 0:n], in_=b2[:, 0:n], func=AF.Ln, bias=EPS, scale=float(D)), reads=[b2], writes=[rr])
                P.op("act", lambda e, rr=rr: e.activation(out=rr[:, 0:n], in_=rr[:, 0:n], func=AF.Exp, scale=-0.5), reads=[rr], writes=[rr])
                scl = DK_A ** -0.5 if c < 4 else 1.0
                P.op("dve", lambda e, raw=raw, rr=rr, c=c, scl=scl: e.scalar_tensor_tensor(out=qkn[:, c, 0:n], in0=raw[:, 0:n], scalar=scl, in1=rr[:, 0:n], op0=ALU.mult, op1=ALU.mult), reads=[raw, rr], pwrites=[actT])
            else:
                P.op("act", lambda e, acc=acc, c=c: e.activation(out=vzv[:, c - 8, 0:n], in_=acc[:, 0:n], func=AF.Silu), reads=[acc], pwrites=[vz_t])
        for c in range(12, 16):
            b = proj_fm(key, c, n, hT, KD)
            P.op("act", lambda e, b=b, c=c: e.activation(out=vzv[:, c - 8, 0:n], in_=b[:, 0:n], func=AF.Silu), reads=[b], pwrites=[vz_t])
        b = proj_fm(key, 16, n, hT, KD)
        bt8 = wf()
        P.op("act", lambda e, b=b, bt8=bt8: e.activation(out=bt8[0:8, 0:n], in_=b[0:8, 0:n], func=AF.Copy), reads=[b], writes=[bt8])
        b2 = bank()
        for blk in range(nblk):
            P.op("pe", lambda e, b2=b2, bt8=bt8, blk=blk: e.matmul(b2[:, blk * 8:(blk + 1) * 8], bt8[0:8, blk * 128:(blk + 1) * 128], identf[0:8, 0:8], start=True, stop=True), reads=[bt8, identf], writes=[b2] if blk == 0 else (), pwrites=() if blk == 0 else [b2])
        P.op("act", lambda e, b2=b2: e.activation(out=gb_tm[:, 0:nblk, :], in_=b2[:, 0:nblk * 8].rearrange("p (a b) -> p a b", b=8), func=AF.Copy), reads=[b2], writes=[gb_tm])
        nb = slice(0, nblk)
        P.op("act", lambda e: e.activation(out=bt_tm[:, nb, :], in_=gb_tm[:, nb, 0:4], func=AF.Sigmoid), reads=[gb_tm], writes=[bt_tm])
        P.op("act", lambda e: e.activation(out=lnb_tm[:, nb, :], in_=bt_tm[:, nb, :], func=AF.Ln), reads=[bt_tm], writes=[lnb_tm])
        P.op("dve", lambda e: e.tensor_tensor(out=g_tm[:, nb, :], in0=gb_tm[:, nb, 4:8], in1=dnc[:, e_:e_ + 1, 4:8].to_broadcast([128, nblk, 4]), op=ALU.add), reads=[gb_tm, dnc], writes=[g_tm])
        P.op("act", lambda e: e.activation(out=g_tm[:, nb, :], in_=g_tm[:, nb, :], func=AF.Exp), reads=[g_tm], writes=[g_tm])
        P.op("act", lambda e: e.activation(out=g_tm[:, nb, :], in_=g_tm[:, nb, :], func=AF.Ln, bias=1.0), reads=[g_tm], writes=[g_tm])
        P.op("dve", lambda e: e.tensor_tensor(out=g_tm[:, nb, :], in0=g_tm[:, nb, :], in1=nexpA[:, e_:e_ + 1, :].to_broadcast([128, nblk, 4]), op=ALU.mult), reads=[g_tm, nexpA], writes=[g_tm])
        return dict(e_=e_, n=n, nseq=nseq, nblk=nblk, prompt=prompt, j=j, t=t)

    def dn_recur(st):
        e_, n, nseq, nblk, prompt, j, t = st["e_"], st["n"], st["nseq"], st["nblk"], st["prompt"], st["j"], st["t"]
        Sf, Sb = S_f[e_], S_b[e_]
        (tX, aX), (tXT, aXT), (tAT, aAT), (tTa, aTa), (tTb, aTb), (tPa, aPa), (tPb, aPb), (tQa, aQa), (tQb, aQb), (tkbg, akbg), (tkgl, akgl), (tvb, avb) = BW
        F0, F1, F2, F3, F4, F5 = FW
        for blk in range(nblk):
            tb = slice(blk * 128, (blk + 1) * 128)
            gcol = g_tm[:, blk, :]
            P.op("dve", lambda e, gcol=gcol: e.tensor_tensor(out=v4(F0[:, 0:512]), in0=bc_h(cU), in1=bc_x(gcol), op=ALU.mult), reads=[cst, g_tm], writes=[F0])
            for h in range(4):
                P.op("dve", lambda e, h=h, blk=blk: e.scalar_tensor_tensor(out=F1[:, h * 128:(h + 1) * 128], in0=identf[:], scalar=lnb_tm[:, blk, h:h + 1], in1=F0[:, h * 128:(h + 1) * 128], op0=ALU.mult, op1=ALU.add), reads=[identf, lnb_tm, F0], writes=[F1] if h == 0 else (), pwrites=() if h == 0 else [F1])
            P.op("dve", lambda e, gcol=gcol: e.tensor_tensor(out=bl8[:].rearrange("p (a b) -> p a b", a=4), in0=gcol.unsqueeze(2).to_broadcast([128, 4, 2]), in1=cCH.unsqueeze(1).to_broadcast([128, 4, 2]), op=ALU.mult), reads=[g_tm, cst], writes=[bl8])
            gG, gGp, gGF, gS = bank(), bank(), bank(), bank()
            P.op("pe", lambda e, gG=gG: e.matmul(gG[:, 0:512], cSC, F0[:, 0:512], start=True, stop=True), reads=[cst, F0], writes=[gG])
            P.op("pe", lambda e, gGp=gGp: e.matmul(gGp[:, 0:512], cSC, F1[:, 0:512], start=True, stop=True), reads=[cst, F1], writes=[gGp])
            P.op("pe", lambda e, gGF=gGF: e.matmul(gGF[:, 0:512], cALL1, F0[:, 0:512], start=True, stop=True), reads=[cst, F0], writes=[gGF])
            P.op("pe", lambda e, gS=gS, gcol=gcol: e.matmul(gS[:, 0:4], cU, gcol, start=True, stop=True), reads=[cst, g_tm], writes=[gS])
            P.op("pe", lambda e, gS=gS, gcol=gcol: e.matmul(gS[:, 4:8], cSC, gcol, start=True, stop=True), reads=[cst, g_tm], pwrites=[gS])
            P.op("pe", lambda e, gS=gS: e.matmul(gS[:, 8:16], cALL1, bl8[:], start=True, stop=True), reads=[cst, bl8], pwrites=[gS])
            P.op("act", lambda e, gS=gS: e.activation(out=gsm[:], in_=gS[:, 0:8], func=AF.Copy), reads=[gS], writes=[gsm])
            P.op("act", lambda e, gS=gS: e.activation(out=egl8[:], in_=gS[:, 8:16], func=AF.Exp), reads=[gS], writes=[egl8])
            P.op("act", lambda e: e.activation(out=sm2[:, 0:4], in_=gsm[:, 0:4], func=AF.Exp), reads=[gsm], writes=[sm2])
            P.op("dve", lambda e: e.tensor_tensor(out=sm2[:, 4:8], in0=gsm[:, 4:8], in1=gsm[:, 0:4], op=ALU.subtract), reads=[gsm], pwrites=[sm2])
            P.op("act", lambda e: e.activation(out=sm2[:, 4:8], in_=sm2[:, 4:8], func=AF.Exp), reads=[sm2], pwrites=[sm2])
            P.op("dve", lambda e, blk=blk: e.tensor_tensor(out=sm2[:, 8:12], in0=gsm[:, 0:4], in1=lnb_tm[:, blk, :], op=ALU.add), reads=[gsm, lnb_tm], pwrites=[sm2])
            P.op("dve", lambda e, blk=blk: e.tensor_tensor(out=sm2[:, 12:16], in0=sm2[:, 0:4], in1=bt_tm[:, blk, :], op=ALU.mult), reads=[sm2, bt_tm], pwrites=[sm2])
            P.op("dve", lambda e, gG=gG: e.tensor_tensor(out=v4(F2[:, 0:512]), in0=v4(gG[:, 0:512]), in1=bc_x(gsm[:, 0:4]), op=ALU.subtract), reads=[gG, gsm], writes=[F2])
            P.op("dve", lambda e: e.tensor_tensor(out=v4(F2[:, 0:512]), in0=v4(F2[:, 0:512]), in1=bc_h(cNU), op=ALU.add), reads=[F2, cst], writes=[F2])
            P.op("act", lambda e: e.activation(out=F2[:, 0:512], in_=F2[:, 0:512], func=AF.Exp), reads=[F2], writes=[F2])
            P.op("dve", lambda e, gGp=gGp: e.tensor_tensor(out=v4(F3[:, 0:512]), in0=v4(gGp[:, 0:512]), in1=bc_x(gsm[:, 0:4]), op=ALU.subtract), reads=[gGp, gsm], writes=[F3])
            P.op("dve", lambda e: e.tensor_tensor(out=v4(F3[:, 0:512]), in0=v4(F3[:, 0:512]), in1=bc_h(cNSU), op=ALU.add), reads=[F3, cst], writes=[F3])
            P.op("act", lambda e: e.activation(out=F3[:, 0:512], in_=F3[:, 0:512], func=AF.Exp), reads=[F3], writes=[F3])
            P.op("dve", lambda e, gG=gG: e.tensor_tensor(out=v4(F4[:, 0:512]), in0=bc_x(sm2[:, 8:12]), in1=v4(gG[:, 0:512]), op=ALU.subtract), reads=[gG, sm2], writes=[F4])
            P.op("dve", lambda e: e.tensor_tensor(out=v4(F4[:, 0:512]), in0=v4(F4[:, 0:512]), in1=bc_h(cNSL), op=ALU.add), reads=[F4, cst], writes=[F4])
            P.op("act", lambda e: e.activation(out=F4[:, 0:512], in_=F4[:, 0:512], func=AF.Exp), reads=[F4], writes=[F4])
            P.op("act", lambda e, gGF=gGF: e.activation(out=F5[:, 0:512], in_=gGF[:, 0:512], func=AF.Exp), reads=[gGF], writes=[F5])
            bKK, bKQ, bKT, bVT = bank(), bank(), bank(), bank()
            for h in range(4):
                fw = (h == 0)
                P.op("pe", lambda e, h=h, bKK=bKK: e.matmul(bKK[:, h * 128:(h + 1) * 128], qkn[:, 4 + h, tb], qkn[:, 4 + h, tb], start=True, stop=True), reads=[actT], writes=[bKK] if fw else (), pwrites=() if fw else [bKK])
            for h in range(4):
                fw = (h == 0)
                P.op("pe", lambda e, h=h, bKQ=bKQ: e.matmul(bKQ[:, h * 128:(h + 1) * 128], qkn[:, 4 + h, tb], qkn[:, h, tb], start=True, stop=True), reads=[actT], writes=[bKQ] if fw else (), pwrites=() if fw else [bKQ])
            for h in range(4):
                fw = (h == 0)
                P.op("pe", lambda e, h=h, bKT=bKT: e.matmul(bKT[:, h * 128:(h + 1) * 128], qkn[:, 4 + h, tb], ident[:], start=True, stop=True), reads=[actT, ident], writes=[bKT] if fw else (), pwrites=() if fw else [bKT])
            for h in range(4):
                fw = (h == 0)
                P.op("pe", lambda e, h=h, bVT=bVT: e.matmul(bVT[:, h * 128:(h + 1) * 128], vzv[:, h, tb], ident[:], start=True, stop=True), reads=[vz_t, ident], writes=[bVT] if fw else (), pwrites=() if fw else [bVT])
            P.op("dve", lambda e, bKK=bKK: e.tensor_tensor(out=aX, in0=bKK[:, 0:512], in1=F3[:, 0:512], op=ALU.mult), reads=[bKK, F3], pwrites=[tX])
            P.op("dve", lambda e, bKK=bKK: e.tensor_tensor(out=aXT, in0=bKK[:, 0:512], in1=F4[:, 0:512], op=ALU.mult), reads=[bKK, F4], pwrites=[tXT])
            P.op("dve", lambda e, bKQ=bKQ: e.tensor_tensor(out=aAT, in0=bKQ[:, 0:512], in1=F2[:, 0:512], op=ALU.mult), reads=[bKQ, F2], pwrites=[tAT])
            P.op("dve", lambda e, bKT=bKT: e.tensor_tensor(out=v4(akbg), in0=v4(bKT[:, 0:512]), in1=bc_x(sm2[:, 12:16]), op=ALU.mult), reads=[bKT, sm2], pwrites=[tkbg])
            P.op("dve", lambda e, bKT=bKT: e.tensor_tensor(out=v4(akgl), in0=v4(bKT[:, 0:512]), in1=bc_x(sm2[:, 4:8]), op=ALU.mult), reads=[bKT, sm2], pwrites=[tkgl])
            P.op("dve", lambda e, bVT=bVT, blk=blk: e.tensor_tensor(out=v4(avb), in0=v4(bVT[:, 0:512]), in1=bc_x(bt_tm[:, blk, :]), op=ALU.mult), reads=[bVT, bt_tm], pwrites=[tvb])
            P.op("dve", lambda e: e.tensor_tensor(out=qgT[:], in0=qkn[:, 0:4, tb], in1=v4(F5[:, 0:512]), op=ALU.mult), reads=[actT, F5], writes=[qgT])
            P.op("dve", lambda e: e.scalar_tensor_tensor(out=v4(aTa), in0=v4(aX), scalar=-1.0, in1=bc_h(ident[:]), op0=ALU.mult, op1=ALU.add), reads=[tX, ident], pwrites=[tTa])
            Tc, Tn_ = (tTa, aTa), (tTb, aTb)
            Pc, PTc = (tX, aX), (tXT, aXT)
            Pn, PTn_ = [(tPa, aPa), (tPb, aPb)], [(tQa, aQa), (tQb, aQb)]
            for lv in range(1, 6):
                pn, ptn = Pn[lv % 2], PTn_[lv % 2]
                if lv < 5:
                    bp = bank()
                    for h in range(4):
                        fw = (h == 0)
                        P.op("pe", lambda e, h=h, bp=bp, Pc=Pc, PTc=PTc: e.matmul(bp[:, h * 128:(h + 1) * 128], PTc[1][:, h * 128:(h + 1) * 128], Pc[1][:, h * 128:(h + 1) * 128], start=True, stop=True), reads=[Pc[0], PTc[0]], writes=[bp] if fw else (), pwrites=() if fw else [bp])
                    P.op("act", lambda e, bp=bp, pn=pn: e.activation(out=pn[1], in_=bp[:, 0:512], func=AF.Copy), reads=[bp], pwrites=[pn[0]])
                bq_ = bank()
                for h in range(4):
                    fw = (h == 0)
                    P.op("pe", lambda e, h=h, bq_=bq_, Pc=Pc, PTc=PTc: e.matmul(bq_[:, h * 128:(h + 1) * 128], Pc[1][:, h * 128:(h + 1) * 128], PTc[1][:, h * 128:(h + 1) * 128], start=True, stop=True), reads=[Pc[0], PTc[0]], writes=[bq_] if fw else (), pwrites=() if fw else [bq_])
                P.op("dve", lambda e, bq_=bq_, ptn=ptn: e.tensor_copy(out=ptn[1], in_=bq_[:, 0:512]), reads=[bq_], pwrites=[ptn[0]])
                bt_ = bank()
                for h in range(4):
                    fw = (h == 0)
                    P.op("pe", lambda e, h=h, bt_=bt_, ptn=ptn, Tc=Tc: e.matmul(bt_[:, h * 128:(h + 1) * 128], ptn[1][:, h * 128:(h + 1) * 128], Tc[1][:, h * 128:(h + 1) * 128], start=True, stop=False), reads=[ptn[0], Tc[0]], writes=[bt_] if fw else (), pwrites=() if fw else [bt_])
                    P.op("pe", lambda e, h=h, bt_=bt_, Tc=Tc: e.matmul(bt_[:, h * 128:(h + 1) * 128], ident[:], Tc[1][:, h * 128:(h + 1) * 128], start=False, stop=True), reads=[ident, Tc[0]], pwrites=[bt_])
                P.op("act", lambda e, bt_=bt_, Tn_=Tn_: e.activation(out=Tn_[1], in_=bt_[:, 0:512], func=AF.Copy), reads=[bt_], pwrites=[Tn_[0]])
                Tc, Tn_ = Tn_, Tc
                Pc, PTc = pn, ptn
            TT = Tc
            bw_ = bank()
            for h in range(4):
                fw = (h == 0)
                P.op("pe", lambda e, h=h, bw_=bw_, TT=TT: e.matmul(bw_[:, h * 128:(h + 1) * 128], akbg[:, h * 128:(h + 1) * 128], TT[1][:, h * 128:(h + 1) * 128], start=True, stop=True), reads=[tkbg, TT[0]], writes=[bw_] if fw else (), pwrites=() if fw else [bw_])
            P.op("act", lambda e, bw_=bw_: e.activation(out=nwT[:].rearrange("p a b -> p (a b)"), in_=bw_[:, 0:512], func=AF.Copy, scale=-1.0), reads=[bw_], writes=[nwT])
            for c in range(2):
                cs = slice(c * 64, c * 64 + 64)
                if not prompt:
                    sq_ = 2 * blk + c
                    P.dma("sp", lambda e, sq_=sq_: e.dma_start(out=Sf[:], in_=dSst_d[e_, sq_]), Sf, writes=[Sf])
                    P.op("act", lambda e: e.activation(out=Sb[:], in_=Sf[:], func=AF.Copy), reads=[Sf], writes=[Sb])
                P.op("dve", lambda e: e.tensor_tensor(out=Sf[:], in0=Sf[:], in1=egl8[:].rearrange("p (a b) -> p a b", a=4)[:, :, c:c + 1].to_broadcast([128, 4, 128]), op=ALU.mult), reads=[Sf, egl8], writes=[Sf])
                bvn = bank()
                for h in range(4):
                    fw = (h == 0)
                    hs_ = slice(h * 128, (h + 1) * 128)
                    P.op("pe", lambda e, h=h, hs_=hs_, bvn=bvn, TT=TT: e.matmul(bvn[cs, hs_], TT[1][cs, h * 128 + c * 64:h * 128 + c * 64 + 64], avb[cs, hs_], start=True, stop=False), reads=[TT[0], tvb], writes=[bvn] if fw else (), pwrites=() if fw else [bvn])
                    P.op("pe", lambda e, h=h, hs_=hs_, bvn=bvn: e.matmul(bvn[cs, hs_], nwT[:, h, cs], Sb[:, h, :], start=False, stop=True), reads=[nwT, Sb], pwrites=[bvn])
                P.op("act", lambda e, bvn=bvn: e.activation(out=vnew[cs, :, :].rearrange("p a b -> p (a b)"), in_=bvn[cs, 0:512], func=AF.Copy), reads=[bvn], writes=[vnew])
                bo = bank()
                for h in range(4):
                    fw = (h == 0)
                    P.op("pe", lambda e, h=h, bo=bo: e.matmul(bo[:, h * 64:(h + 1) * 64], Sb[:, h, :], qgT[:, h, cs], start=True, stop=False), reads=[Sb, qgT], writes=[bo] if fw else (), pwrites=() if fw else [bo])
                    P.op("pe", lambda e, h=h, bo=bo: e.matmul(bo[:, h * 64:(h + 1) * 64], vnew[cs, h, :], aAT[cs, h * 128 + c * 64:h * 128 + c * 64 + 64], start=False, stop=True), reads=[vnew, tAT], pwrites=[bo])
                t0c = blk * 128 + c * 64
                P.op("dve", lambda e, bo=bo, t0c=t0c: e.tensor_copy(out=orawv[:, :, t0c:t0c + 64], in_=bo[:, 0:256].rearrange("p (a b) -> p a b", a=4)), reads=[bo], pwrites=[hT])
                bs_ = bank()
                for h in range(4):
                    fw = (h == 0)
                    hs_ = slice(h * 128, (h + 1) * 128)
                    P.op("pe", lambda e, h=h, hs_=hs_, bs_=bs_: e.matmul(bs_[:, hs_], akgl[cs, hs_], vnew[cs, h, :], start=True, stop=True), reads=[tkgl, vnew], writes=[bs_] if fw else (), pwrites=() if fw else [bs_])
                P.op("dve", lambda e, bs_=bs_: e.tensor_tensor(out=Sb[:], in0=Sf[:], in1=v4(bs_[:, 0:512]), op=ALU.add), reads=[Sf, bs_], writes=[Sb])
                P.op("dve", lambda e, bs_=bs_: e.tensor_tensor(out=Sf[:], in0=Sf[:], in1=v4(bs_[:, 0:512]), op=ALU.add), reads=[Sf, bs_], writes=[Sf])
                if not prompt:
                    P.dma("act", lambda e, sq_=sq_: e.dma_start(out=dS_s_d[e_, sq_], in_=Sf[:]), Sf, reads=[Sf], is_output=True)
        for h in range(4):
            sqb = wbt()
            P.op("act", lambda e, h=h, sqb=sqb: e.activation(out=sqb[:, 0:n], in_=orawv[:, h, 0:n], func=AF.Square), reads=[hT], writes=[sqb])
            b2 = bank()
            P.op("pe", lambda e, b2=b2, sqb=sqb: e.matmul(b2[:, 0:n], onesD[:], sqb[:, 0:n], start=True, stop=True), reads=[onesD, sqb], writes=[b2])
            rr = wf()
            P.op("act", lambda e, b2=b2, rr=rr: e.activation(out=rr[:, 0:n], in_=b2[:, 0:n], func=AF.Ln, bias=EPS, scale=float(D) / DV_A), reads=[b2], writes=[rr])
            P.op("act", lambda e, rr=rr: e.activation(out=rr[:, 0:n], in_=rr[:, 0:n], func=AF.Exp, scale=-0.5), reads=[rr], writes=[rr])
            P.op("dve", lambda e, h=h, rr=rr: e.scalar_tensor_tensor(out=rr[:, 0:n], in0=orawv[:, h, 0:n], scalar=egain[:, e_, 2:3], in1=rr[:, 0:n], op0=ALU.mult, op1=ALU.mult), reads=[hT, egain, rr], writes=[rr])
            P.op("dve", lambda e, h=h, rr=rr: e.tensor_tensor(out=oT[:, h, 0:n], in0=rr[:, 0:n], in1=vzv[:, 4 + h, 0:n], op=ALU.mult), reads=[rr, vz_t], pwrites=[oT])
        if prompt and t == cfg.ntile - 1:
            P.dma("act", lambda e: e.dma_start(out=dS_p_d[e_, j], in_=Sf[:]), Sf, reads=[Sf], is_output=True)
            P.dma("act", lambda e: e.dma_start(out=dc_p_d[e_, j], in_=dtail[e_][:]), dtail[e_], reads=[dtail[e_]], is_output=True)

    def plan_even(l):
        lst = [("eA%d" % l, c, KD * 128) for c in range(17)]
        lst += [("eB%d" % l, c, KD * 128) for c in range(14)]
        lst += [("eo%d" % l, c, 12 * 128) for c in range(KD)]
        return lst

    def plan_band(l):
        lst = [("oqk%d" % l, c, KD * 128) for c in range(16)]
        lst += [("ov%d" % l, g, 4 * 256) for g in range(8)]
        lst += [("oo%d" % l, c, KD * 128) for c in range(KD)]
        return lst

    ns_tok = cfg.n_sseq * cfg.dec_seq
    ftail_s = sb("ftail_s", [128, L, KF, cfg.n_sseq, 2], F32)
    fout_s = sb("fout_s", [128, L, KF, cfg.n_sseq, 2], F32)

    def plan_layer(l):
        lst = []
        if l % 2 == 1:
            lst += plan_band(l)
        else:
            lst += plan_even(l)
        lst += plan_ffn(l)
        return lst

    for j in range(cfg.n_pseq):
        for t in range(cfg.ntile):
            for l in range(L):
                ws.plan(plan_layer(l))
    for l in range(L):
        ws.plan(plan_layer(l))

    for j in range(cfg.n_pseq):
        P.op("pool", lambda e: e.memset(ftail[:], 0.0), writes=[ftail])
        for t in range(cfg.ntile):
            P.dma("sp", lambda e, j=j, t=t: e.dma_start(out=xT[:], in_=xp_d[j, :, :, t * TT:(t + 1) * TT]), xT, writes=[xT])
            for l in range(L):
                if l % 2 == 1:
                    band_layer_prompt(l, j, t)
                else:
                    even_layer_prompt(l, j, t)
                ffn(l, TT, 1, t == cfg.ntile - 1, j)
            final_norm(TT, lambda k, j=j, t=t: yp_d[j, :, k, t * TT:(t + 1) * TT])
    P.dma("sp", lambda e: e.dma_start(out=ftail_s[:], in_=ffnst_d[:, :, :, :, :]), ftail_s, writes=[ftail_s])
    P.dma("sp", lambda e: e.dma_start(out=xT[:, :, 0:ns_tok], in_=xs_d[:, :, :]), xT, writes=[xT])
    for l in range(L):
        if l % 2 == 1:
            band_layer_sample(l)
        else:
            even_layer_sample(l)
        ffn(l, ns_tok, cfg.n_sseq, True, None)
    P.dma("act", lambda e: e.dma_start(out=ffnc_s_d[:, :, :, :, :], in_=fout_s[:]), fout_s, reads=[fout_s], is_output=True)
    P.dma("act", lambda e: e.dma_start(out=dc_s_d[:, :, :, :, :], in_=dcso[:]), dcso, reads=[dcso], is_output=True)
    final_norm(ns_tok, lambda k: ys_d[:, k, :])

    global SBUF_LEFT
    SBUF_LEFT = nc.sbuf_bytes_remaining
    P.emit()
    es.close()
    return nc, P


def make_consts():
    c = np.zeros((128, NCST), np.float32)
    c[:, 0:128] = 1.0 / D
    for p in range(128):
        c[p, 128 + (p // 64) * 64:128 + (p // 64) * 64 + 64] = 1.0 / 64
    c[:, 256:320] = 1.0
    c[:, 320:448] = np.eye(128, dtype=np.float32)
    p, x = np.meshgrid(np.arange(128), np.arange(128), indexing="ij")
    same = (p // 64) == (x // 64)
    c[:, 448:576] = (same & (p <= x)).astype(np.float32)
    c[:, 576:704] = same.astype(np.float32)
    c[:, 704:832] = np.where(same & (x >= p), 0.0, NEG)
    c[:, 832:960] = np.where(same & (x > p), 0.0, NEG)
    c[:, 960:1088] = np.where(same & (x < p), 0.0, NEG)
    c[:, 1088:1216] = 1.0
    c[:, 1216] = (np.arange(128) < 64)
    c[:, 1217] = (np.arange(128) >= 64)
    for i in range(16):
        c[:, 1218 + i] = 2.0 ** -(i + 1)
    return c


def prep_core_inputs(cfg, inp, core):
    L = cfg.layers
    NO = max(cfg.n_odd, 1)
    ps = slice(core * cfg.n_pseq, (core + 1) * cfg.n_pseq)
    ss = slice(core * cfg.n_sseq, (core + 1) * cfg.n_sseq)
    m = {}
    xp = inp["x_prompt"][ps]
    m["xp"] = np.ascontiguousarray(xp.reshape(cfg.n_pseq, cfg.seq, KD, 128).transpose(0, 3, 2, 1))
    xs = inp["x_sample"][ss].reshape(cfg.n_sseq * cfg.dec_seq, KD, 128)
    m["xs"] = np.ascontiguousarray(xs.transpose(2, 1, 0))
    st = inp["state_ffn_conv"][:L, ss]
    m["ffnst"] = np.ascontiguousarray(st.reshape(L, cfg.n_sseq, 2, KF, 128).transpose(4, 0, 3, 1, 2))
    vec = np.zeros((128, 2 * L * KD + KD + L * KF * 4), np.float32)
    vec[:, 0:L * KD] = feat_major(inp["norm_mix"][:L], KD).reshape(128, L * KD)
    vec[:, L * KD:2 * L * KD] = feat_major(inp["norm_ffn"][:L], KD).reshape(128, L * KD)
    vec[:, 2 * L * KD:2 * L * KD + KD] = feat_major(inp["norm_final"], KD)
    fc = np.concatenate([inp["ffn_conv_w"][:L], inp["ffn_conv_b"][:L, None, :]], 1)
    fc = fc.reshape(L, 4, KF, 128).transpose(3, 0, 2, 1)
    vec[:, 2 * L * KD + KD:] = fc.reshape(128, L * KF * 4)
    m["vecs"] = vec
    m["consts"] = make_consts()
    bg = np.zeros((128, NO, 2), np.float32)
    bt = np.zeros((NO, 128, H_C, 256), np.float32)
    bf_ = np.zeros((128, NO, H_C), np.float32)
    bkc = np.zeros((NO, cfg.n_sseq, 128, KD, 512), np.float32)
    bvc = np.zeros((NO, cfg.n_sseq, 128, 4, 1024), np.float32)
    pp, xx = np.meshgrid(np.arange(128), np.arange(256), indexing="ij")
    u = xx - pp
    idx = np.clip(u, -(CHUNK - 1), REL_CLIP) + (CHUNK - 1)
    msk = (xx < 64) & (pp >= 64)
    for jo in range(cfg.n_odd):
        bg[:, jo, 0] = np.tile(inp["band_q_gain"][jo], 2)
        bg[:, jo, 1] = np.tile(inp["band_k_gain"][jo], 2)
        rb = inp["band_rel_bias"][jo]
        t = rb[:, idx]
        t = np.where(msk[None], np.float32(NEG), t)
        bt[jo] = t.transpose(1, 0, 2)
        bf_[:, jo, :] = rb[None, :, REL_CLIP + CHUNK - 1]
        ck = inp["cache_band_k"][jo, ss].reshape(cfg.n_sseq, 512, KD, 128)
        bkc[jo] = ck.transpose(0, 3, 2, 1)
        cv_ = inp["cache_band_v"][jo, ss].reshape(cfg.n_sseq, 4, 128, 1024)
        bvc[jo] = cv_.transpose(0, 2, 1, 3)
    m["bgain"], m["btab"], m["bfar"], m["bkc"], m["bvc"] = bg, bt, bf_, bkc, bvc
    NE = cfg.n_even
    eg = np.zeros((128, NE, 4), np.float32)
    PC = cfg.past
    dkc = np.zeros((NE, cfg.n_sseq, 64, PC), np.float32)
    dic = np.zeros((NE, cfg.n_sseq, 64, PC), np.float32)
    dvc = np.zeros((NE, cfg.n_sseq, 128, PC // 128, 64), np.float32)
    for e_ in range(NE):
        eg[:, e_, 0] = np.tile(inp["dsa_q_gain"][e_], 2)
        eg[:, e_, 1] = np.tile(inp["dsa_k_gain"][e_], 2)
        eg[:, e_, 2] = inp["dn_o_gain"][e_]
        dkc[e_] = inp["cache_dsa_k"][e_, ss].transpose(0, 2, 1)
        dic[e_] = inp["cache_dsa_kidx"][e_, ss].transpose(0, 2, 1)
        dvc[e_] = inp["cache_dsa_v"][e_, ss].reshape(cfg.n_sseq, PC // 128, 128, 64).transpose(0, 2, 1, 3)
    m["egain"], m["dkc"], m["dic"], m["dvc"] = eg, dkc, dic, dvc
    dnw = np.zeros((128, NE, 12, 4), np.float32)
    dnc = np.zeros((128, NE, 8), np.float32)
    dSst = np.zeros((NE, cfg.n_sseq, 128, 4, 128), np.float32)
    dcst = np.zeros((128, NE, 12, cfg.n_sseq, 3), np.float32)
    for e_ in range(NE):
        dnw[:, e_] = inp["dn_conv_w"][e_].reshape(4, 12, 128).transpose(2, 1, 0)
        dnc[:, e_, 0:4] = inp["dn_a_log"][e_][None, :]
        dnc[:, e_, 4:8] = inp["dn_dt_bias"][e_][None, :]
        dSst[e_] = inp["state_dn_S"][e_, ss].transpose(0, 2, 1, 3)
        dcst[:, e_] = inp["state_dn_conv"][e_, ss].reshape(cfg.n_sseq, 3, 12, 128).transpose(3, 2, 0, 1)
    m["dnw"], m["dnc"], m["dSst"], m["dcst"] = dnw, dnc, dSst, dcst
    return m


def prep_weights(cfg, inp):
    L = cfg.layers
    w = {}
    for l in range(L):
        w["wf_ffa%d" % l] = fm_slabs(inp["ffn_w_a"][l])
        w["wf_ffg%d" % l] = fm_slabs(inp["ffn_w_g"][l])
        fd = fm_slabs(inp["ffn_w_down"][l])
        w["wf_ffd%d" % l] = np.ascontiguousarray(fd.reshape(KD, 128, 2, 11 * 128).transpose(0, 2, 1, 3).reshape(2 * KD * 128, 11 * 128))
        if l % 2 == 0:
            e_ = l // 2
            we = inp["w_in_even"][e_]
            A = np.zeros((D, 17 * 128), np.float32)
            A[:, 0:2048] = we[:, 0:2048]
            A[:, 2048:2056] = we[:, 2048:2056]
            w["wf_eA%d" % l] = fm_slabs(A)
            B = np.zeros((D, 14 * 128), np.float32)
            for h in range(H_B):
                B[:, h * 128:h * 128 + 64] = we[:, 2056 + h * 64:2056 + (h + 1) * 64]
            B[:, 8 * 128:8 * 128 + 64] = we[:, 2568:2632]
            B[:, 8 * 128 + 64:9 * 128] = we[:, 2632:2696]
            for h in range(H_IDX):
                B[:, (9 + h) * 128:(9 + h) * 128 + 64] = we[:, 2696 + h * 64:2696 + (h + 1) * 64]
            B[:, 13 * 128:13 * 128 + 64] = we[:, 2952:3016]
            B[:, 13 * 128 + 64:13 * 128 + 68] = we[:, 3016:3020]
            w["wf_eB%d" % l] = fm_slabs(B)
            wo = inp["w_out_even"][e_]
            WO = np.zeros((12 * 128, D), np.float32)
            WO[0:512] = wo[0:512]
            for h in range(H_B):
                WO[512 + h * 128:512 + h * 128 + 64] = wo[512 + h * 64:512 + (h + 1) * 64]
            w["wf_eo%d" % l] = fm_slabs(WO)
        if l % 2 == 1:
            j = l // 2
            wi = inp["w_in_odd"][j]
            w["wf_oqk%d" % l] = fm_slabs(wi[:, 0:2048])
            tv = tm_slabs(wi[:, 2048:3072], 256)
            w["wf_ov%d" % l] = np.ascontiguousarray(tv.reshape(4, 128, 2, 4 * 256).transpose(0, 2, 1, 3).reshape(8 * 128, 4 * 256))
            w["wf_oo%d" % l] = fm_slabs(inp["w_out_odd"][j])
    return w


_CACHE = {}


def run_cfg(cfg, inp, n_cores, runner=None):
    key = (cfg.n_pseq, cfg.seq, cfg.n_sseq, cfg.dec_seq, cfg.past, cfg.layers)
    if key not in _CACHE:
        _CACHE[key] = build_program(cfg)
    nc, P = _CACHE[key]
    wts = prep_weights(cfg, inp)
    in_maps = []
    for c in range(n_cores):
        m = prep_core_inputs(cfg, inp, c)
        m.update(wts)
        in_maps.append(m)
    if runner is None:
        res = run_bass_kernel_spmd(nc, in_maps, core_ids=list(range(n_cores))).results
    else:
        res = runner(nc, in_maps)
    return res


def assemble(cfg, res, n_cores):
    L = cfg.layers
    NO = cfg.n_odd
    o = {}
    o["yp"] = np.concatenate([r["yp"].transpose(0, 3, 2, 1).reshape(cfg.n_pseq, cfg.seq, D) for r in res], 0)
    o["ys"] = np.concatenate([r["ys"].transpose(2, 1, 0).reshape(cfg.n_sseq, cfg.dec_seq, D) for r in res], 0)
    o["ffn_p"] = np.concatenate([r["ffnc_p"].transpose(0, 1, 4, 3, 2).reshape(L, cfg.n_pseq, 2, D_FF) for r in res], 1)
    o["ffn_s"] = np.concatenate([r["ffnc_s"].transpose(1, 3, 4, 2, 0).reshape(L, cfg.n_sseq, 2, D_FF) for r in res], 1)
    NE = cfg.n_even
    for nm, kp, ks_ in (("dsa_k", "dk_p", "dk_s"), ("dsa_v", "dv_p", "dv_s"), ("dsa_ki", "di_p", "di_s")):
        o[nm + "_p"] = np.concatenate([r[kp].transpose(0, 1, 3, 2) for r in res], 1)
        o[nm + "_s"] = np.concatenate([r[ks_].transpose(0, 2, 1).reshape(NE, cfg.n_sseq, cfg.dec_seq, 64) for r in res], 1)
    o["dn_S_p"] = np.concatenate([r["dS_p"].transpose(0, 1, 3, 2, 4) for r in res], 1)
    o["dn_S_s"] = np.concatenate([r["dS_s"].transpose(0, 1, 3, 2, 4) for r in res], 1)
    o["dn_conv_p"] = np.concatenate([r["dc_p"].transpose(0, 1, 4, 3, 2).reshape(NE, cfg.n_pseq, 3, 1536) for r in res], 1)
    o["dn_conv_s"] = np.concatenate([r["dc_s"].transpose(1, 3, 4, 2, 0).reshape(NE, cfg.n_sseq, 3, 1536) for r in res], 1)
    if NO:
        o["band_k_p"] = np.concatenate([r["bk_p"][:NO].transpose(0, 1, 4, 3, 2).reshape(NO, cfg.n_pseq, 512, H_C, HD_C) for r in res], 1)
        o["band_v_p"] = np.concatenate([r["bv_p"][:NO].reshape(NO, cfg.n_pseq, 512, H_C, HD_C) for r in res], 1)
        o["band_k_s"] = np.concatenate([r["bk_s"][:NO].transpose(0, 3, 2, 1).reshape(NO, cfg.n_sseq, cfg.dec_seq, H_C, HD_C) for r in res], 1)
        o["band_v_s"] = np.concatenate([r["bv_s"][:NO].reshape(NO, cfg.n_sseq, cfg.dec_seq, H_C, HD_C) for r in res], 1)
    return o


def kernel(**inputs):
    inp = {k: np.asarray(v) for k, v in inputs.items()}
    cfg = Cfg()
    res = run_cfg(cfg, inp, 8)
    o = assemble(cfg, res, 8)
    B, S, DB, DS = 32, cfg.seq, 32, cfg.dec_seq
    z = lambda *sh: np.zeros(sh, np.float32)
    return (o["yp"], o["ys"],
            o.get("dn_S_p", z(2, B, 4, 128, 128)), o.get("dn_S_s", z(2, DB, 4, 128, 128)),
            o.get("dn_conv_p", z(2, B, 3, 1536)), o.get("dn_conv_s", z(2, DB, 3, 1536)),
            o.get("dsa_k_p", z(2, B, S, 64)), o.get("dsa_k_s", z(2, DB, DS, 64)),
            o.get("dsa_v_p", z(2, B, S, 64)), o.get("dsa_v_s", z(2, DB, DS, 64)),
            o.get("dsa_ki_p", z(2, B, S, 64)), o.get("dsa_ki_s", z(2, DB, DS, 64)),
            o.get("band_k_p", z(2, B, 512, 16, 64)), o.get("band_k_s", z(2, DB, DS, 16, 64)),
            o.get("band_v_p", z(2, B, 512, 16, 64)), o.get("band_v_s", z(2, DB, DS, 16, 64)),
            o["ffn_p"], o["ffn_s"])
```

```python
import contextlib
import os
import numpy as np
import concourse.bass as bass
import concourse.mybir as mybir
from concourse.bass_utils import run_bass_kernel_spmd

F32 = mybir.dt.float32
BF16 = mybir.dt.bfloat16
AF = mybir.ActivationFunctionType
ALU = mybir.AluOpType

D = 1024
KD = 8
DEPTH = 4
CHUNK = 64
H_A, DK_A, DV_A, CONV_A = 4, 128, 128, 4
H_B, HD_B, H_IDX, D_IDX, TOPK_MAX = 8, 64, 4, 64, 256
H_C, HD_C, BAND_CHUNKS, REL_CLIP = 16, 64, 8, 128
WINDOW = BAND_CHUNKS * CHUNK
D_FF = 2816
KF = 22
CONV_F = 3
EPS = 1e-6
NEG = -30000.0
NCST = 1344
SBUF_LEFT = None

ENGINES = ("pe", "act", "dve", "pool", "sp")


class Tile:
    __slots__ = ("name", "t", "last_w", "pw", "readers", "dsem", "dcount", "psum", "gh")

    def __init__(self, name, t, psum=False):
        self.name = name
        self.t = t
        self.psum = psum
        self.gh = None
        self.last_w = None
        self.pw = []
        self.readers = []
        self.dsem = None
        self.dcount = 0

    def __getitem__(self, k):
        return self.t[k]


class Op:
    __slots__ = ("eng", "fn", "deps", "is_dma", "sem", "semval", "needs_inc", "dtile")

    def __init__(self, eng, fn, is_dma):
        self.eng = eng
        self.fn = fn
        self.deps = []
        self.is_dma = is_dma
        self.sem = None
        self.semval = 0
        self.needs_inc = False
        self.dtile = None


class _Rec:
    def __getattr__(self, name):
        def f(*a, **k):
            self.call = (name, a, k)
            return self
        return f


def _freeze(fn):
    r = _Rec()
    fn(r)
    name, a, k = r.call
    return lambda eng, name=name, a=a, k=k: getattr(eng, name)(*a, **k)


class Prog:
    def __init__(self, nc):
        self.nc = nc
        self.streams = {e: [] for e in ENGINES}
        self.tiles = []
        self.out_ops = []
        self.nops = 0

    def tile(self, name, t, psum=False):
        tl = Tile(name, t, psum)
        self.tiles.append(tl)
        return tl

    def _add(self, op, reads, writes, pwrites, pe_sync=False):
        deps = {}
        for t in reads:
            if t.last_w is not None:
                deps[id(t.last_w)] = t.last_w
            for w in t.pw:
                deps[id(w)] = w
            if t.psum:
                for r in t.readers:
                    if r.eng != op.eng:
                        deps[id(r)] = r
        for t in writes:
            if t.last_w is not None:
                deps[id(t.last_w)] = t.last_w
            for w in t.pw:
                deps[id(w)] = w
            for r in t.readers:
                deps[id(r)] = r
        for t in pwrites:
            if t.last_w is not None:
                deps[id(t.last_w)] = t.last_w
            if t.gh is not None:
                deps[id(t.gh)] = t.gh
            for r in t.readers:
                deps[id(r)] = r
        deps.pop(id(op), None)
        for d in deps.values():
            if d.eng == "pe" and op.eng == "pe" and not pe_sync:
                continue
            op.deps.append(d)
            d.needs_inc = True
        for t in writes:
            t.last_w = op
            t.pw = []
            t.readers = []
            t.gh = None
        for t in pwrites:
            if t.readers:
                t.pw = [op]
                t.readers = []
                t.gh = op
            else:
                t.pw.append(op)
        for t in reads:
            if t.last_w is not op and op not in t.pw:
                t.readers.append(op)
        self.streams[op.eng].append(op)
        self.nops += 1
        return op

    def op(self, eng, fn, reads=(), writes=(), pwrites=(), pe_sync=False):
        return self._add(Op(eng, _freeze(fn), False), reads, writes, pwrites, pe_sync)

    def dma(self, eng, fn, tile, reads=(), writes=(), pwrites=(), is_output=False):
        op = Op(eng, _freeze(fn), True)
        op.dtile = tile
        tile.dcount += 1
        op.semval = 16 * tile.dcount
        self._add(op, reads, writes, pwrites)
        op.needs_inc = True
        if is_output:
            self.out_ops.append(op)
        return op

    def emit(self):
        nc = self.nc
        with contextlib.ExitStack() as es:
            esem = {e: es.enter_context(nc.semaphore("s_" + e)) for e in ENGINES}
            for t in self.tiles:
                if t.dcount:
                    t.dsem = es.enter_context(nc.semaphore("d_" + t.name))
            for e in ENGINES:
                c = 0
                for op in self.streams[e]:
                    if op.is_dma:
                        op.sem = op.dtile.dsem
                    else:
                        op.sem = esem[e]
                        if op.needs_inc:
                            c += 1
                        op.semval = c
            finals = {}
            for op in self.out_ops:
                k = id(op.sem)
                if k not in finals or finals[k][1] < op.semval:
                    finals[k] = (op.sem, op.semval)
            block = es.enter_context(nc.Block())

            def gen(ename, eng, is_last):
                known = {}
                for op in self.streams[ename]:
                    need = {}
                    for d in op.deps:
                        k = id(d.sem)
                        if known.get(k, 0) >= d.semval:
                            continue
                        if k not in need or need[k][1] < d.semval:
                            need[k] = (d.sem, d.semval)
                    for k, (s, v) in need.items():
                        eng.wait_ge(s, v)
                        known[k] = v
                    ins = op.fn(eng)
                    if op.needs_inc:
                        ins.then_inc(op.sem, 16 if op.is_dma else 1)
                if is_last:
                    for k, (s, v) in finals.items():
                        eng.wait_ge(s, v)

            @block.tensor
            def _(e):
                gen("pe", e, False)

            @block.scalar
            def _(e):
                gen("act", e, False)

            @block.vector
            def _(e):
                gen("dve", e, False)

            @block.gpsimd
            def _(e):
                gen("pool", e, False)

            @block.sync
            def _(e):
                gen("sp", e, True)


def fm_slabs(W, kc_rows=128):
    K, N = W.shape
    KC = K // 128
    NCH = (N + 127) // 128
    Wp = np.zeros((K, NCH * 128), np.float32)
    Wp[:, :N] = W
    a = Wp.reshape(KC, 128, NCH, 128).transpose(2, 1, 0, 3)
    return np.ascontiguousarray(a.reshape(NCH * 128, KC * 128))


def tm_slabs(W, ncol):
    K, N = W.shape
    KC = K // 128
    G = N // ncol
    a = W.reshape(KC, 128, G, ncol).transpose(2, 1, 0, 3)
    return np.ascontiguousarray(a.reshape(G * 128, KC * ncol))


def feat_major(v, kc):
    return np.ascontiguousarray(np.moveaxis(v.reshape(v.shape[:-1] + (kc, 128)), -1, 0))


class Cfg:
    def __init__(self, n_pseq=4, seq=2048, n_sseq=4, dec_seq=64, past=1024, layers=DEPTH):
        self.n_pseq, self.seq, self.n_sseq, self.dec_seq, self.past = n_pseq, seq, n_sseq, dec_seq, past
        self.layers = layers
        self.TT = 512
        self.ntile = seq // self.TT
        self.n_even = (layers + 1) // 2
        self.n_odd = layers // 2
        self.band_len = min(WINDOW, past)


def build_program(cfg):
    nc = bass.Bass("TRN2", target_bir_lowering=False)
    P = Prog(nc)
    es = contextlib.ExitStack()
    L = cfg.layers
    TT = cfg.TT

    def din(name, shape, dt=F32):
        return nc.dram_tensor(name, list(shape), dt, kind="ExternalInput").ap()

    def dout(name, shape, dt=F32):
        return nc.dram_tensor(name, list(shape), dt, kind="ExternalOutput").ap()

    def dscr(name, shape, dt):
        return nc.dram_tensor(name, list(shape), dt, kind="Internal").ap()

    _n = [0]

    def sb(name, shape, dt):
        return P.tile(name, es.enter_context(nc.sbuf_tensor("sb_" + name, list(shape), dt)))

    xp_d = din("xp", [cfg.n_pseq, 128, KD, cfg.seq])
    xs_d = din("xs", [128, KD, cfg.n_sseq * cfg.dec_seq])
    yp_d = dout("yp", [cfg.n_pseq, 128, KD, cfg.seq])
    ys_d = dout("ys", [128, KD, cfg.n_sseq * cfg.dec_seq])
    ffnst_d = din("ffnst", [128, L, KF, cfg.n_sseq, 2])
    ffnc_p_d = dout("ffnc_p", [L, cfg.n_pseq, 128, KF, 2])
    ffnc_s_d = dout("ffnc_s", [128, L, KF, cfg.n_sseq, 2])
    vecs_d = din("vecs", [128, 2 * L * KD + KD + L * KF * 4])
    consts_d = din("consts", [128, NCST])
    NO = max(cfg.n_odd, 1)
    bgain_d = din("bgain", [128, NO, 2])
    btab_d = din("btab", [NO, 128, H_C, 256])
    bfar_d = din("bfar", [128, NO, H_C])
    bkc_d = din("bkc", [NO, cfg.n_sseq, 128, KD, 512])
    bvc_d = din("bvc", [NO, cfg.n_sseq, 128, 4, 1024])
    bk_p_d = dout("bk_p", [NO, cfg.n_pseq, 128, KD, 512])
    bv_p_d = dout("bv_p", [NO, cfg.n_pseq, 512, 1024])
    bk_s_d = dout("bk_s", [NO, 128, KD, cfg.n_sseq * cfg.dec_seq])
    bv_s_d = dout("bv_s", [NO, cfg.n_sseq * cfg.dec_seq, 1024])
    NE = cfg.n_even
    egain_d = din("egain", [128, NE, 4])
    dkc_d = din("dkc", [NE, cfg.n_sseq, 64, cfg.past])
    dic_d = din("dic", [NE, cfg.n_sseq, 64, cfg.past])
    dvc_d = din("dvc", [NE, cfg.n_sseq, 128, cfg.past // 128, 64])
    dk_p_d = dout("dk_p", [NE, cfg.n_pseq, 64, cfg.seq])
    dv_p_d = dout("dv_p", [NE, cfg.n_pseq, 64, cfg.seq])
    di_p_d = dout("di_p", [NE, cfg.n_pseq, 64, cfg.seq])
    dk_s_d = dout("dk_s", [NE, 64, cfg.n_sseq * cfg.dec_seq])
    dv_s_d = dout("dv_s", [NE, 64, cfg.n_sseq * cfg.dec_seq])
    di_s_d = dout("di_s", [NE, 64, cfg.n_sseq * cfg.dec_seq])
    dnw_d = din("dnw", [128, NE, 12, 4])
    dnc_d = din("dnc", [128, NE, 8])
    dSst_d = din("dSst", [NE, cfg.n_sseq, 128, 4, 128])
    dcst_d = din("dcst", [128, NE, 12, cfg.n_sseq, 3])
    dS_p_d = dout("dS_p", [NE, cfg.n_pseq, 128, 4, 128])
    dS_s_d = dout("dS_s", [NE, cfg.n_sseq, 128, 4, 128])
    dc_p_d = dout("dc_p", [NE, cfg.n_pseq, 128, 12, 3])
    dc_s_d = dout("dc_s", [128, NE, 12, cfg.n_sseq, 3])
    kscr_d = dscr("kscr", [NO, 2, 128, KD * 512], BF16)
    vscr_d = dscr("vscr", [NO, 2, 128, 4 * 1024], BF16)
    kscr_tl = [[P.tile("kscr%d_%d" % (a, b), None) for b in range(2)] for a in range(NO)]
    vscr_tl = [[P.tile("vscr%d_%d" % (a, b), None) for b in range(2)] for a in range(NO)]

    wspec = {}
    for l in range(L):
        wspec["ffa%d" % l] = (KF * 128, KD * 128)
        wspec["ffg%d" % l] = (KF * 128, KD * 128)
        wspec["ffd%d" % l] = (2 * KD * 128, 11 * 128)
        if l % 2 == 0:
            wspec["eA%d" % l] = (17 * 128, KD * 128)
            wspec["eB%d" % l] = (14 * 128, KD * 128)
            wspec["eo%d" % l] = (KD * 128, 12 * 128)
        if l % 2 == 1:
            wspec["oqk%d" % l] = (16 * 128, KD * 128)
            wspec["ov%d" % l] = (8 * 128, 4 * 256)
            wspec["oo%d" % l] = (KD * 128, KD * 128)
    w_in = {k: din("wf_" + k, s) for k, s in wspec.items()}
    w_bf = {k: dscr("wb_" + k, s, BF16) for k, s in wspec.items()}
    w_tl = {k: P.tile("wscr_" + k, None) for k in wspec}

    NW = 4
    wbuf = [sb("wbuf%d" % i, [128, 1536], BF16) for i in range(NW)]
    Kcur = sb("Kcur", [128, KD, 512], BF16)
    Kprev = sb("Kprev", [128, KD, 512], BF16)
    Vcur = sb("Vcur", [128, 4, 1024], BF16)
    Vprev = sb("Vprev", [128, 4, 1024], BF16)
    stg_f = [Kcur, Kprev]
    stg_b = [Vcur, Vprev]
    stg_fv = [Kcur[:].bitcast(F32).rearrange("p a b -> p (a b)"), Kprev[:].bitcast(F32).rearrange("p a b -> p (a b)")]
    stg_bv = [Vcur[:].rearrange("p a b -> p (a b)"), Vprev[:].rearrange("p a b -> p (a b)")]
    qT = sb("qT", [128, KD, 512], BF16)
    oT = sb("oT", [128, KD, 512], BF16)
    Wf = [sb("Wf%d" % i, [128, 512], F32) for i in range(4)]
    Wb = [sb("Wb%d" % i, [128, 512], BF16) for i in range(4)]
    PTf = [sb("PTf%d" % i, [128, 384], BF16) for i in range(3)]
    PTn = [sb("PTn%d" % i, [128, 256], BF16) for i in range(3)]
    tnr = [sb("tnr%d" % i, [128, 256], F32) for i in range(2)]
    Tn = sb("Tn", [128, H_C, 256], F32)
    bgain = sb("bgain", [128, NO, 2], F32)
    bfar = sb("bfar", [128, NO, H_C], F32)
    blk64 = sb("blk64", [128, 128], BF16)
    ones64 = sb("ones64", [128, 64], BF16)
    ident = sb("ident", [128, 128], BF16)
    identf = sb("identf", [128, 128], F32)
    idrep = sb("idrep", [128, 512], BF16)
    idrep64 = sb("idrep64", [128, 512], BF16)
    egain = sb("egain", [128, NE, 4], F32)
    SK = max(cfg.seq, cfg.past + 256)
    NVB = SK // 128
    kbT_c = [sb("kbT_c%d" % e_, [64, SK], BF16) for e_ in range(NE)]
    kiT_c = [sb("kiT_c%d" % e_, [64, SK], BF16) for e_ in range(NE)]
    Vb_c = [sb("Vb_c%d" % e_, [128, NVB, 64], BF16) for e_ in range(NE)]
    oTb = Kprev
    PTd = [sb("PTd%d" % i, [128, 1024], BF16) for i in range(2)]
    wi_tm = sb("wi_tm", [128, 4, 4], F32)
    wtab = sb("wtab", [128, 16], F32)
    bis = sb("bis", [128, 8], F32)
    S_f = [sb("S_f%d" % e_, [128, 4, 128], F32) for e_ in range(NE)]
    S_b = [sb("S_b%d" % e_, [128, 4, 128], BF16) for e_ in range(NE)]
    dtail = [sb("dtail%d" % e_, [128, 12, 3], F32) for e_ in range(NE)]
    dnw = sb("dnw", [128, NE, 12, 4], F32)
    dnc = sb("dnc", [128, NE, 8], F32)
    nexpA = sb("nexpA", [128, NE, 4], F32)
    dcst = sb("dcst", [128, NE, 12, cfg.n_sseq, 3], F32)
    dcso = sb("dcso", [128, NE, 12, cfg.n_sseq, 3], F32)
    gb_tm = sb("gb_tm", [128, 4, 8], F32)
    bt_tm = sb("bt_tm", [128, 4, 4], F32)
    lnb_tm = sb("lnb_tm", [128, 4, 4], F32)
    g_tm = sb("g_tm", [128, 4, 4], F32)
    gsm = sb("gsm", [128, 8], F32)
    sm2 = sb("sm2", [128, 16], F32)
    bl8 = sb("bl8", [128, 8], F32)
    egl8 = sb("egl8", [128, 8], F32)
    nwT = sb("nwT", [128, 4, 128], BF16)
    qgT = sb("qgT", [128, 4, 128], BF16)
    vnew = sb("vnew", [128, 4, 128], BF16)
    vecs = sb("vecs", [128, 2 * L * KD + KD + L * KF * 4], F32)
    cst = sb("cst", [128, NCST], F32)
    onesD = sb("onesD", [128, 128], BF16)
    xT = sb("xT", [128, KD, TT], F32)
    hT = sb("hT", [128, KD, TT], BF16)
    rstd = sb("rstd", [128, TT], F32)
    aT = [sb("aT%d" % i, [128, TT + 8], F32) for i in range(2)]
    cv = [sb("cv%d" % i, [128, TT], F32) for i in range(2)]
    sg = [sb("sg%d" % i, [128, TT], F32) for i in range(2)]
    actT = sb("actT", [128, KF, TT], BF16)
    sq = actT
    ftail = sb("ftail", [128, L, KF, 2], F32)
    banks = [P.tile("bank%d" % i, es.enter_context(nc.psum_tensor("bank%d" % i, [128, 512], F32)), psum=True) for i in range(8)]
    _bk = [0]

    def bank():
        b = banks[_bk[0] % 8]
        _bk[0] += 1
        return b

    _rr = [0]

    def evac_eng():
        _rr[0] += 1
        return "act" if _rr[0] % 2 else "dve"

    def o_nmix(l):
        return l * KD

    def o_nffn(l):
        return L * KD + l * KD

    o_nfin = 2 * L * KD
    o_fcv = 2 * L * KD + KD

    P.dma("sp", lambda e: e.dma_start(out=vecs[:], in_=vecs_d[:, :]), vecs, writes=[vecs])
    P.dma("sp", lambda e: e.dma_start(out=cst[:], in_=consts_d[:, :]), cst, writes=[cst])
    P.op("dve", lambda e: e.tensor_copy(out=onesD[:], in_=cst[:, 0:128]), reads=[cst], writes=[onesD])
    P.op("dve", lambda e: e.tensor_copy(out=blk64[:], in_=cst[:, 128:256]), reads=[cst], writes=[blk64])
    P.op("dve", lambda e: e.tensor_copy(out=ones64[:], in_=cst[:, 256:320]), reads=[cst], writes=[ones64])
    P.op("dve", lambda e: e.tensor_copy(out=ident[:], in_=cst[:, 320:448]), reads=[cst], writes=[ident])
    P.op("dve", lambda e: e.tensor_copy(out=identf[:], in_=cst[:, 320:448]), reads=[cst], writes=[identf])
    for r_ in range(4):
        P.op("dve", lambda e, r_=r_: e.tensor_copy(out=idrep[:, r_ * 128:(r_ + 1) * 128], in_=cst[:, 320:448]), reads=[cst], writes=[idrep] if r_ == 0 else (), pwrites=() if r_ == 0 else [idrep])
    for r_ in range(8):
        P.op("dve", lambda e, r_=r_: e.tensor_copy(out=idrep64[0:64, r_ * 64:(r_ + 1) * 64], in_=cst[0:64, 320:384]), reads=[cst], writes=[idrep64] if r_ == 0 else (), pwrites=() if r_ == 0 else [idrep64])
        P.op("dve", lambda e, r_=r_: e.tensor_copy(out=idrep64[64:128, r_ * 64:(r_ + 1) * 64], in_=cst[64:128, 384:448]), reads=[cst], pwrites=[idrep64])
    P.dma("sp", lambda e: e.dma_start(out=egain[:], in_=egain_d[:, :, :]), egain, writes=[egain])
    P.dma("sp", lambda e: e.dma_start(out=dnw[:], in_=dnw_d[:, :, :, :]), dnw, writes=[dnw])
    P.dma("sp", lambda e: e.dma_start(out=dnc[:], in_=dnc_d[:, :, :]), dnc, writes=[dnc])
    P.dma("sp", lambda e: e.dma_start(out=dcst[:], in_=dcst_d[:, :, :, :, :]), dcst, writes=[dcst])
    P.op("act", lambda e: e.activation(out=nexpA[:], in_=dnc[:, :, 0:4], func=AF.Exp), reads=[dnc], writes=[nexpA])
    P.op("dve", lambda e: e.tensor_scalar(out=nexpA[:], in0=nexpA[:], scalar1=-1.0, scalar2=None, op0=ALU.mult), reads=[nexpA], writes=[nexpA])
    P.dma("sp", lambda e: e.dma_start(out=bgain[:], in_=bgain_d[:, :, :]), bgain, writes=[bgain])
    P.dma("sp", lambda e: e.dma_start(out=bfar[:], in_=bfar_d[:, :, :]), bfar, writes=[bfar])

    def convert(key):
        R, C = wspec[key]
        src, dst, tl = w_in[key], w_bf[key], w_tl[key]
        i = 0
        for r0 in range(0, R, 128):
            for c0 in range(0, C, 2048):
                cw = min(2048, C - c0)
                f, b = stg_f[i % 2], stg_b[i % 2]
                fv, bv = stg_fv[i % 2], stg_bv[i % 2]
                P.dma("sp", lambda e, fv=fv, r0=r0, c0=c0, cw=cw: e.dma_start(out=fv[:, 0:cw], in_=src[r0:r0 + 128, c0:c0 + cw]), f, writes=[f])
                ce = ("pool", "dve", "act")[i % 3]
                if ce == "act":
                    P.op("act", lambda e, fv=fv, bv=bv, cw=cw: e.activation(out=bv[:, 0:cw], in_=fv[:, 0:cw], func=AF.Copy), reads=[f], writes=[b])
                else:
                    P.op(ce, lambda e, fv=fv, bv=bv, cw=cw: e.tensor_copy(out=bv[:, 0:cw], in_=fv[:, 0:cw]), reads=[f], writes=[b])
                P.dma("act", lambda e, bv=bv, r0=r0, c0=c0, cw=cw: e.dma_start(out=dst[r0:r0 + 128, c0:c0 + cw], in_=bv[:, 0:cw]), b, reads=[b], pwrites=[tl])
                i += 1

    for l in range(L):
        if l % 2 == 0:
            convert("eA%d" % l)
            convert("eB%d" % l)
            convert("eo%d" % l)
        if l % 2 == 1:
            convert("oqk%d" % l)
            convert("ov%d" % l)
            convert("oo%d" % l)
        convert("ffa%d" % l)
        convert("ffg%d" % l)
        convert("ffd%d" % l)

    class WStream:
        def __init__(self):
            self.reqs = []
            self.loaded = 0
            self.used = 0

        def plan(self, lst):
            self.reqs.extend(lst)

        def _load(self, i):
            key, c, width = self.reqs[i]
            buf = wbuf[i % NW]
            src, tl = w_bf[key], w_tl[key]
            P.dma("sp", lambda e: e.dma_start(out=buf[:, 0:width], in_=src[c * 128:(c + 1) * 128, 0:width]), buf, reads=[tl], writes=[buf])

        def get(self, key, c):
            i = self.used
            assert self.reqs[i][0] == key and self.reqs[i][1] == c, (self.reqs[i], key, c)
            while self.loaded < min(len(self.reqs), i + NW):
                self._load(self.loaded)
                self.loaded += 1
            self.used += 1
            return wbuf[i % NW]

    ws = WStream()

    def rmsnorm(n, goff):
        P.op("act", lambda e: e.activation(out=sq[:, 0:KD, 0:n], in_=xT[:, :, 0:n], func=AF.Square), reads=[xT], writes=[sq])
        b = bank()
        for k in range(KD):
            P.op("pe", lambda e, k=k: e.matmul(b[:, 0:n], onesD[:], sq[:, k, 0:n], start=(k == 0), stop=(k == KD - 1)), reads=[onesD, sq], writes=[b] if k == 0 else (), pwrites=() if k == 0 else [b])
        P.op("act", lambda e: e.activation(out=rstd[:, 0:n], in_=b[:, 0:n], func=AF.Ln, bias=EPS, scale=1.0), reads=[b], writes=[rstd])
        P.op("act", lambda e: e.activation(out=rstd[:, 0:n], in_=rstd[:, 0:n], func=AF.Exp, scale=-0.5), reads=[rstd], writes=[rstd])
        for k in range(KD):
            P.op("dve", lambda e, k=k: e.scalar_tensor_tensor(out=hT[:, k, 0:n], in0=xT[:, k, 0:n], scalar=vecs[:, goff + k:goff + k + 1], in1=rstd[:, 0:n], op0=ALU.mult, op1=ALU.mult),
                 reads=[xT, vecs, rstd], writes=[hT] if k == 0 else (), pwrites=() if k == 0 else [hT])

    def proj_fm(key, c, n, src, kc, m=128):
        wt = ws.get(key, c)
        b = bank()
        for k in range(kc):
            P.op("pe", lambda e, k=k: e.matmul(b[0:m, 0:n], wt[:, k * 128:k * 128 + m], src[:, k, 0:n], start=(k == 0), stop=(k == kc - 1)),
                 reads=[wt, src], writes=[b] if k == 0 else (), pwrites=() if k == 0 else [b])
        return b

    def ffn(l, n, nseq, last_tile, job_out):
        rmsnorm(n, o_nffn(l))
        ls = n // nseq
        seg = ls + 2
        for c in range(KF):
            a_t, cv_t, sg_t = aT[c % 2], cv[c % 2], sg[c % 2]
            xpv = a_t[:, 0:nseq * seg].rearrange("p (s c) -> p s c", c=seg)
            cvv = cv_t[:, 0:n].rearrange("p (s c) -> p s c", c=ls)
            if nseq == 1:
                t_src, t_dst, t_tl, t_dtl = ftail[:, l, c:c + 1, :], ftail[:, l, c:c + 1, :], ftail, ftail
            else:
                t_src, t_dst, t_tl, t_dtl = ftail_s[:, l, c, :, :], fout_s[:, l, c, :, :], ftail_s, fout_s
            ba = proj_fm("ffa%d" % l, c, n, hT, KD)
            P.op("act", lambda e, ba=ba, xpv=xpv: e.activation(out=xpv[:, :, 2:seg], in_=ba[:, 0:n].rearrange("p (s c) -> p s c", c=ls), func=AF.Copy), reads=[ba], writes=[a_t])
            P.op("pool", lambda e, xpv=xpv, t_src=t_src: e.tensor_copy(out=xpv[:, :, 0:2], in_=t_src), reads=[t_tl], pwrites=[a_t])
            bg = proj_fm("ffg%d" % l, c, n, hT, KD)
            wv = o_fcv + (l * KF + c) * 4
            P.op("dve", lambda e, xpv=xpv, cvv=cvv, wv=wv: e.tensor_scalar(out=cvv, in0=xpv[:, :, 0:ls], scalar1=vecs[:, wv:wv + 1], scalar2=None, op0=ALU.mult), reads=[a_t, vecs], writes=[cv_t])
            P.op("dve", lambda e, xpv=xpv, cvv=cvv, wv=wv: e.scalar_tensor_tensor(out=cvv, in0=xpv[:, :, 1:1 + ls], scalar=vecs[:, wv + 1:wv + 2], in1=cvv, op0=ALU.mult, op1=ALU.add), reads=[a_t, vecs, cv_t], writes=[cv_t])
            P.op("dve", lambda e, xpv=xpv, cvv=cvv, wv=wv: e.scalar_tensor_tensor(out=cvv, in0=xpv[:, :, 2:2 + ls], scalar=vecs[:, wv + 2:wv + 3], in1=cvv, op0=ALU.mult, op1=ALU.add), reads=[a_t, vecs, cv_t], writes=[cv_t])
            P.op("pool", lambda e, xpv=xpv, t_dst=t_dst: e.tensor_copy(out=t_dst, in_=xpv[:, :, ls:ls + 2]), reads=[a_t], pwrites=[t_dtl])
            P.op("act", lambda e, cv_t=cv_t, sg_t=sg_t, wv=wv: e.activation(out=sg_t[:, 0:n], in_=cv_t[:, 0:n], func=AF.Silu, bias=vecs[:, wv + 3:wv + 4], scale=1.0), reads=[cv_t, vecs], writes=[sg_t])
            P.op("dve", lambda e, sg_t=sg_t, bg=bg, c=c: e.tensor_tensor(out=actT[:, c, 0:n], in0=bg[:, 0:n], in1=sg_t[:, 0:n], op=ALU.mult), reads=[bg, sg_t], writes=[actT] if c == 0 else (), pwrites=() if c == 0 else [actT])
        if nseq == 1 and last_tile:
            P.dma("act", lambda e: e.dma_start(out=ffnc_p_d[l, job_out], in_=ftail[:, l, :, :]), ftail, reads=[ftail], is_output=True)
        for c in range(KD):
            bd = bank()
            for hf in range(2):
                wt = ws.get("ffd%d" % l, 2 * c + hf)
                for k in range(11):
                    st, sp_ = (hf == 0 and k == 0), (hf == 1 and k == 10)
                    P.op("pe", lambda e, bd=bd, wt=wt, k=k, hf=hf, st=st, sp_=sp_: e.matmul(bd[:, 0:n], wt[:, k * 128:(k + 1) * 128], actT[:, hf * 11 + k, 0:n], start=st, stop=sp_),
                         reads=[wt, actT], writes=[bd] if st else (), pwrites=() if st else [bd])
            P.op("dve", lambda e, bd=bd, c=c: e.tensor_tensor(out=xT[:, c, 0:n], in0=bd[:, 0:n], in1=xT[:, c, 0:n], op=ALU.add), reads=[bd, xT], pwrites=[xT])

    def plan_ffn(l):
        lst = []
        for c in range(KF):
            lst.append(("ffa%d" % l, c, KD * 128))
            lst.append(("ffg%d" % l, c, KD * 128))
        for c in range(2 * KD):
            lst.append(("ffd%d" % l, c, 11 * 128))
        return lst

    _wf = [0]

    def wf():
        _wf[0] += 1
        return Wf[_wf[0] % 4]

    _wb = [0]

    def wbt():
        _wb[0] += 1
        return Wb[_wb[0] % 4]

    def final_norm(n, dst_fn):
        P.op("act", lambda e: e.activation(out=sq[:, 0:KD, 0:n], in_=xT[:, :, 0:n], func=AF.Square), reads=[xT], writes=[sq])
        b = bank()
        for k in range(KD):
            P.op("pe", lambda e, k=k: e.matmul(b[:, 0:n], onesD[:], sq[:, k, 0:n], start=(k == 0), stop=(k == KD - 1)), reads=[onesD, sq], writes=[b] if k == 0 else (), pwrites=() if k == 0 else [b])
        P.op("act", lambda e: e.activation(out=rstd[:, 0:n], in_=b[:, 0:n], func=AF.Ln, bias=EPS, scale=1.0), reads=[b], writes=[rstd])
        P.op("act", lambda e: e.activation(out=rstd[:, 0:n], in_=rstd[:, 0:n], func=AF.Exp, scale=-0.5), reads=[rstd], writes=[rstd])
        for k in range(KD):
            y = wf()
            P.op("dve", lambda e, k=k, y=y: e.scalar_tensor_tensor(out=y[:, 0:n], in0=xT[:, k, 0:n], scalar=vecs[:, o_nfin + k:o_nfin + k + 1], in1=rstd[:, 0:n], op0=ALU.mult, op1=ALU.mult),
                 reads=[xT, vecs, rstd], writes=[y])
            P.dma("act", lambda e, k=k, y=y: e.dma_start(out=dst_fn(k), in_=y[:, 0:n]), y, reads=[y], is_output=True)

    def headnorm(b, n, gcol, dst_fn):
        sqb = wbt()
        P.op("act", lambda e: e.activation(out=sqb[:, 0:n], in_=b[:, 0:n], func=AF.Square), reads=[b], writes=[sqb])
        b2 = bank()
        P.op("pe", lambda e: e.matmul(b2[:, 0:n], blk64[:], sqb[:, 0:n], start=True, stop=True), reads=[blk64, sqb], writes=[b2])
        rr = wf()
        P.op("act", lambda e: e.activation(out=rr[:, 0:n], in_=b2[:, 0:n], func=AF.Ln, bias=EPS, scale=1.0), reads=[b2], writes=[rr])
        P.op("act", lambda e: e.activation(out=rr[:, 0:n], in_=rr[:, 0:n], func=AF.Exp, scale=-0.5), reads=[rr], writes=[rr])
        return rr

    def band_inproj(l, n, want_out, kout_fn, vout_fn):
        jo = l // 2
        rmsnorm(n, o_nmix(l))
        def post_qk(c, b):
            rr = headnorm(b, n, None, None)
            if c < 8:
                P.op("dve", lambda e, b=b, rr=rr, c=c: e.scalar_tensor_tensor(out=qT[:, c, 0:n], in0=b[:, 0:n], scalar=bgain[:, jo, 0:1], in1=rr[:, 0:n], op0=ALU.mult, op1=ALU.mult),
                     reads=[b, bgain, rr], writes=[qT] if c == 0 else (), pwrites=() if c == 0 else [qT])
            else:
                kf = wf()
                P.op("dve", lambda e, b=b, rr=rr, kf=kf: e.scalar_tensor_tensor(out=kf[:, 0:n], in0=b[:, 0:n], scalar=bgain[:, jo, 1:2], in1=rr[:, 0:n], op0=ALU.mult, op1=ALU.mult),
                     reads=[b, bgain, rr], writes=[kf])
                P.op("pool", lambda e, kf=kf, c=c: e.tensor_copy(out=Kcur[:, c - 8, 0:n], in_=kf[:, 0:n]), reads=[kf], writes=[Kcur] if c == 8 else (), pwrites=() if c == 8 else [Kcur])
                if want_out:
                    P.dma("act", lambda e, kf=kf, c=c: e.dma_start(out=kout_fn(c - 8), in_=kf[:, 0:n]), kf, reads=[kf], is_output=True)
        pend = []
        for c in range(16):
            b = proj_fm("oqk%d" % l, c, n, hT, KD)
            pend.append((c, b))
            if len(pend) > 1:
                post_qk(*pend.pop(0))
        while pend:
            post_qk(*pend.pop(0))
        ntb = n // 128
        for g in range(4):
            bks = [bank() for _ in range(ntb)]
            for kh in range(2):
                wt = ws.get("ov%d" % l, 2 * g + kh)
                for tb in range(ntb):
                    b = bks[tb]
                    for kk in range(4):
                        k = kh * 4 + kk
                        P.op("pe", lambda e, b=b, k=k, kk=kk, tb=tb, wt=wt: e.matmul(b[:, 0:256], hT[:, k, tb * 128:(tb + 1) * 128], wt[:, kk * 256:(kk + 1) * 256], start=(k == 0), stop=(k == KD - 1)),
                             reads=[wt, hT], writes=[b] if k == 0 else (), pwrites=() if k == 0 else [b])
            for tb in range(ntb):
                b = bks[tb]
                first = (g == 0 and tb == 0)
                P.op("act", lambda e, b=b, tb=tb, g=g: e.activation(out=Vcur[:, tb, g * 256:(g + 1) * 256], in_=b[:, 0:256], func=AF.Copy), reads=[b], writes=[Vcur] if first else (), pwrites=() if first else [Vcur])
                if want_out:
                    vf = wf()
                    P.op("dve", lambda e, b=b, vf=vf: e.tensor_copy(out=vf[:, 0:256], in_=b[:, 0:256]), reads=[b], writes=[vf])
                    P.dma("act", lambda e, vf=vf, tb=tb, g=g: e.dma_start(out=vout_fn(tb, g), in_=vf[:, 0:256]), vf, reads=[vf], is_output=True)

    _bc = [0]

    def band_qgroup(jo, c0, nq, blocks, is_prompt):
        valid = [jb for jb in range(5) if blocks[jb] is not None]
        far = [jb for jb in valid if jb <= 2]
        def stage1(h):
            hg, hh = h // 4, h % 4
            hc, hp, hl = h // 2, h % 2, hh // 2
            r = _bc[0] % 3
            _bc[0] += 1
            bA, bB = banks[2 + 2 * r], banks[3 + 2 * r]
            ptf, ptn, tn = PTf[r], PTn[r], tnr[_bc[0] % 2]
            hs = slice(hp * 64, hp * 64 + 64)
            firstA, firstB = True, True
            for jb in valid:
                Kt, koff, Vt, vtb, nk, pb = blocks[jb]
                if jb <= 2:
                    dstb, col = bA, jb * nq
                else:
                    dstb, col = bB, (0 if jb == 4 else nq)
                fl = firstA if jb <= 2 else firstB
                P.op("pe", lambda e, dstb=dstb, col=col, Kt=Kt, koff=koff, nk=nk, pb=pb, hs=hs, hc=hc: e.matmul(dstb[pb:pb + nk, col:col + nq], Kt[hs, hc, koff:koff + nk], qT[hs, hc, c0:c0 + nq], start=True, stop=True),
                     reads=[Kt, qT], writes=[dstb] if fl else (), pwrites=() if fl else [dstb])
                if jb <= 2:
                    firstA = False
                else:
                    firstB = False
            if far:
                f0 = far[0] * nq
                P.op("act", lambda e, bA=bA, ptf=ptf, f0=f0, h=h: e.activation(out=ptf[:, f0:3 * nq], in_=bA[:, f0:3 * nq], func=AF.Exp, bias=bfar[:, jo, h:h + 1], scale=0.125),
                     reads=[bA, bfar], writes=[ptf])
                if is_prompt and 0 in far:
                    P.op("pool", lambda e, ptf=ptf: e.memset(ptf[0:64, 64:128], 0.0), reads=(), pwrites=[ptf])
            _, _, _, _, nk4, pb4 = blocks[4]
            if 3 in valid and nk4 == 128 and nq == 128:
                P.op("dve", lambda e, bB=bB, tn=tn, h=h: e.scalar_tensor_tensor(out=tn[:, 0:256], in0=bB[:, 0:256], scalar=0.125, in1=Tn[:, h, 0:256], op0=ALU.mult, op1=ALU.add), reads=[bB, Tn], writes=[tn])
                P.op("act", lambda e, tn=tn, ptn=ptn: e.activation(out=ptn[:, 0:256], in_=tn[:, 0:256], func=AF.Exp), reads=[tn], writes=[ptn])
            else:
                ps4 = slice(pb4, pb4 + nk4)
                tcol = pb4 if nq == 64 else 0
                P.op("dve", lambda e, bB=bB, tn=tn, h=h, ps4=ps4, tcol=tcol: e.scalar_tensor_tensor(out=tn[ps4, 0:nq], in0=bB[ps4, 0:nq], scalar=0.125, in1=Tn[ps4, h, tcol:tcol + nq], op0=ALU.mult, op1=ALU.add), reads=[bB, Tn], writes=[tn])
                P.op("act", lambda e, tn=tn, ptn=ptn, ps4=ps4: e.activation(out=ptn[ps4, 0:nq], in_=tn[ps4, 0:nq], func=AF.Exp), reads=[tn], writes=[ptn])
                if 3 in valid:
                    P.op("dve", lambda e, bB=bB, tn=tn, h=h: e.scalar_tensor_tensor(out=tn[:, nq:2 * nq], in0=bB[:, nq:2 * nq], scalar=0.125, in1=Tn[:, h, 128:128 + nq], op0=ALU.mult, op1=ALU.add), reads=[bB, Tn], pwrites=[tn])
                    P.op("act", lambda e, tn=tn, ptn=ptn: e.activation(out=ptn[:, nq:2 * nq], in_=tn[:, nq:2 * nq], func=AF.Exp), reads=[tn], pwrites=[ptn])
            return dict(h=h, hg=hg, hh=hh, hc=hc, hp=hp, hl=hl, hs=hs, ptf=ptf, ptn=ptn)

        def stage2(cx):
            h, hg, hh, hc, hp, hl, hs, ptf, ptn = (cx[k_] for k_ in ("h", "hg", "hh", "hc", "hp", "hl", "hs", "ptf", "ptn"))
            acc = banks[hg % 2]
            srcs = []
            for jb in valid:
                Kt, koff, Vt, vtb, nk, pb = blocks[jb]
                if jb <= 2:
                    srcs.append((ptf[pb:pb + nk, jb * nq:(jb + 1) * nq], ptf, Vt, vtb, nk, pb))
                else:
                    col = 0 if jb == 4 else nq
                    srcs.append((ptn[pb:pb + nk, col:col + nq], ptn, Vt, vtb, nk, pb))
            for i, (src, srct, Vt, vtb, nk, pb) in enumerate(srcs):
                st, sp_ = (i == 0), (i == len(srcs) - 1)
                fw = (hh == 0 and i == 0)
                P.op("pe", lambda e, acc=acc, hs=hs, hl=hl, Vt=Vt, vtb=vtb, pb=pb, nk=nk, h=h, src=src, st=st, sp_=sp_: e.matmul(acc[hs, hl * nq:(hl + 1) * nq], Vt[pb:pb + nk, vtb, h * 64:(h + 1) * 64], src, start=st, stop=sp_),
                     reads=[Vt, srct], writes=[acc] if fw else (), pwrites=() if fw else [acc])
            for i, (src, srct, Vt, vtb, nk, pb) in enumerate(srcs):
                st, sp_ = (i == 0), (i == len(srcs) - 1)
                P.op("pe", lambda e, acc=acc, hs=hs, hl=hl, pb=pb, nk=nk, src=src, st=st, sp_=sp_: e.matmul(acc[hs, 256 + hl * nq:256 + (hl + 1) * nq], ones64[pb:pb + nk, :], src, start=st, stop=sp_),
                     reads=[ones64, srct], pwrites=[acc])
            if hh == 3:
                norm_group(hg)

        def norm_group(hg):
            acc = banks[hg % 2]
            rd = wf()
            P.op("act", lambda e, acc=acc, rd=rd: e.activation(out=rd[:, 0:2 * nq], in_=acc[:, 256:256 + 2 * nq], func=AF.Ln), reads=[acc], writes=[rd])
            P.op("act", lambda e, rd=rd: e.activation(out=rd[:, 0:2 * nq], in_=rd[:, 0:2 * nq], func=AF.Exp, scale=-1.0), reads=[rd], writes=[rd])
            P.op("dve", lambda e, acc=acc, rd=rd, hg=hg: e.tensor_tensor(out=oT[:, 2 * hg:2 * hg + 2, c0:c0 + nq], in0=acc[:, 0:2 * nq].rearrange("p (a b) -> p a b", a=2), in1=rd[:, 0:2 * nq].rearrange("p (a b) -> p a b", a=2), op=ALU.mult),
                 reads=[acc, rd], pwrites=[oT])


        pendq = []
        for h in range(H_C):
            pendq.append(stage1(h))
            if len(pendq) > 2:
                stage2(pendq.pop(0))
        while pendq:
            stage2(pendq.pop(0))

    def out_proj(key, l, n, src, kc):
        for c in range(KD):
            b = proj_fm(key, c, n, src, kc)
            P.op("dve", lambda e, b=b, c=c: e.tensor_tensor(out=xT[:, c, 0:n], in0=b[:, 0:n], in1=xT[:, c, 0:n], op=ALU.add), reads=[b, xT], pwrites=[xT])

    def band_layer_prompt(l, j, t):
        jo = l // 2
        n = TT
        last = (t == cfg.ntile - 1)
        P.dma("sp", lambda e: e.dma_start(out=Tn[:], in_=btab_d[jo]), Tn, writes=[Tn])
        if t > 0:
            sl = (t - 1) % 2
            P.dma("sp", lambda e: e.dma_start(out=Kprev[:].rearrange("p a b -> p (a b)"), in_=kscr_d[jo, sl]), Kprev, reads=[kscr_tl[jo][sl]], writes=[Kprev])
            P.dma("sp", lambda e: e.dma_start(out=Vprev[:].rearrange("p a b -> p (a b)"), in_=vscr_d[jo, sl]), Vprev, reads=[vscr_tl[jo][sl]], writes=[Vprev])
        band_inproj(l, n, last, lambda c: bk_p_d[jo, j, :, c, :], lambda tb, g: bv_p_d[jo, j, tb * 128:(tb + 1) * 128, g * 256:(g + 1) * 256])
        if not last:
            sl = t % 2
            P.dma("act", lambda e: e.dma_start(out=kscr_d[jo, sl], in_=Kcur[:].rearrange("p a b -> p (a b)")), Kcur, reads=[Kcur], writes=[kscr_tl[jo][sl]])
            P.dma("act", lambda e: e.dma_start(out=vscr_d[jo, sl], in_=Vcur[:].rearrange("p a b -> p (a b)")), Vcur, reads=[Vcur], writes=[vscr_tl[jo][sl]])
        for m in range(4):
            blocks = []
            for jb in range(5):
                rel = m + jb - 4
                if t * 4 + rel < 0:
                    blocks.append(None)
                elif rel >= 0:
                    blocks.append((Kcur, rel * 128, Vcur, rel, 128, 0))
                else:
                    blocks.append((Kprev, (4 + rel) * 128, Vprev, 4 + rel, 128, 0))
            band_qgroup(jo, m * 128, 128, blocks, True)
        out_proj("oo%d" % l, l, n, oT, KD)

    def band_layer_sample(l):
        jo = l // 2
        n = ns_tok
        ls = cfg.dec_seq
        P.dma("sp", lambda e: e.dma_start(out=Tn[:], in_=btab_d[jo]), Tn, writes=[Tn])
        band_inproj(l, n, True, lambda c: bk_s_d[jo, :, c, :], lambda tb, g: bv_s_d[jo, tb * 128:(tb + 1) * 128, g * 256:(g + 1) * 256])
        for s_ in range(cfg.n_sseq):
            for c in range(KD):
                f = wf()
                P.dma("sp", lambda e, f=f, c=c, s_=s_: e.dma_start(out=f[:, 0:512], in_=bkc_d[jo, s_, :, c, :]), f, writes=[f])
                P.op("pool", lambda e, f=f, c=c: e.tensor_copy(out=Kprev[:, c, :], in_=f[:, 0:512]), reads=[f], writes=[Kprev] if c == 0 else (), pwrites=() if c == 0 else [Kprev])
            for tb in range(4):
                for hf in range(2):
                    f = wf()
                    P.dma("sp", lambda e, f=f, tb=tb, hf=hf, s_=s_: e.dma_start(out=f[:, 0:512], in_=bvc_d[jo, s_, :, tb, hf * 512:(hf + 1) * 512]), f, writes=[f])
                    fw = (tb == 0 and hf == 0)
                    P.op("pool", lambda e, f=f, tb=tb, hf=hf: e.tensor_copy(out=Vprev[:, tb, hf * 512:(hf + 1) * 512], in_=f[:, 0:512]), reads=[f], writes=[Vprev] if fw else (), pwrites=() if fw else [Vprev])
            blocks = [(Kprev, jb * 128, Vprev, jb, 128, 0) for jb in range(4)]
            pb = (s_ % 2) * 64
            blocks.append((Kcur, s_ * ls, Vcur, (s_ * ls) // 128, ls, pb))
            band_qgroup(jo, s_ * ls, ls, blocks, False)
        out_proj("oo%d" % l, l, n, oT, KD)


    qbT = qT
    qiT = Kcur
    isc_t = Tn
    iscv = Tn[:].rearrange("p a b -> p (a b)")
    junk = Vprev
    junkv = Vprev[:].rearrange("p a b -> p (a b)")
    negsel = Vcur
    negvf = Vcur[:].rearrange("p a b -> p (a b)")
    NIT = 12

    def norm64(b, n, gcol):
        sqb = wbt()
        P.op("act", lambda e: e.activation(out=sqb[0:64, 0:n], in_=b[0:64, 0:n], func=AF.Square), reads=[b], writes=[sqb])
        b2 = bank()
        P.op("pe", lambda e: e.matmul(b2[0:64, 0:n], blk64[0:64, 0:64], sqb[0:64, 0:n], start=True, stop=True), reads=[blk64, sqb], writes=[b2])
        rr = wf()
        P.op("act", lambda e: e.activation(out=rr[0:64, 0:n], in_=b2[0:64, 0:n], func=AF.Ln, bias=EPS, scale=1.0), reads=[b2], writes=[rr])
        P.op("act", lambda e: e.activation(out=rr[0:64, 0:n], in_=rr[0:64, 0:n], func=AF.Exp, scale=-0.5), reads=[rr], writes=[rr])
        return rr

    def dsa_inproj(l, n, kcol0, vblk0, kout, vout, iout):
        e_ = l // 2
        key = "eB%d" % l
        def post_qb(h, b):
            rr = norm64(b, n, None)
            P.op("dve", lambda e, b=b, rr=rr, h=h: e.scalar_tensor_tensor(out=qbT[0:64, h, 0:n], in0=b[0:64, 0:n], scalar=egain[0:64, e_, 0:1], in1=rr[0:64, 0:n], op0=ALU.mult, op1=ALU.mult),
                 reads=[b, egain, rr], writes=[qbT] if h == 0 else (), pwrites=() if h == 0 else [qbT])
        pend = []
        for h in range(H_B):
            b = proj_fm(key, h, n, hT, KD, m=64)
            pend.append((h, b))
            if len(pend) > 1:
                post_qb(*pend.pop(0))
        while pend:
            post_qb(*pend.pop(0))
        b = proj_fm(key, 8, n, hT, KD)
        rr = norm64(b, n, None)
        kv = wf()
        P.op("dve", lambda e, b=b, rr=rr, kv=kv: e.scalar_tensor_tensor(out=kv[0:64, 0:n], in0=b[0:64, 0:n], scalar=egain[0:64, e_, 1:2], in1=rr[0:64, 0:n], op0=ALU.mult, op1=ALU.mult), reads=[b, egain, rr], writes=[kv])
        P.op("dve", lambda e, b=b, kv=kv: e.tensor_copy(out=kv[64:128, 0:n], in_=b[64:128, 0:n]), reads=[b], pwrites=[kv])
        P.op("pool", lambda e, kv=kv: e.tensor_copy(out=kbT_c[e_][0:64, kcol0:kcol0 + n], in_=kv[0:64, 0:n]), reads=[kv], pwrites=[kbT_c[e_]])
        vbf = wbt()
        P.op("pool", lambda e, kv=kv, vbf=vbf: e.tensor_copy(out=vbf[64:128, 0:n], in_=kv[64:128, 0:n]), reads=[kv], writes=[vbf])
        P.dma("act", lambda e, kv=kv: e.dma_start(out=kout, in_=kv[0:64, 0:n]), kv, reads=[kv], is_output=True)
        P.dma("act", lambda e, kv=kv: e.dma_start(out=vout, in_=kv[64:128, 0:n]), kv, reads=[kv], is_output=True)
        for tb in range(n // 128):
            b2 = bank()
            P.op("pe", lambda e, b2=b2, vbf=vbf, tb=tb: e.matmul(b2[:, 0:64], vbf[64:128, tb * 128:(tb + 1) * 128], ident[64:128, 64:128], start=True, stop=True), reads=[vbf, ident], writes=[b2])
            P.op("act", lambda e, b2=b2, tb=tb: e.activation(out=Vb_c[e_][:, vblk0 + tb, :], in_=b2[:, 0:64], func=AF.Copy), reads=[b2], pwrites=[Vb_c[e_]])
        for h in range(H_IDX):
            b = proj_fm(key, 9 + h, n, hT, KD, m=64)
            P.op("act", lambda e, b=b, h=h: e.activation(out=qiT[0:64, h, 0:n], in_=b[0:64, 0:n], func=AF.Copy), reads=[b], writes=[qiT] if h == 0 else (), pwrites=() if h == 0 else [qiT])
        b = proj_fm(key, 13, n, hT, KD)
        kw = wf()
        P.op("act", lambda e, b=b, kw=kw: e.activation(out=kw[:, 0:n], in_=b[:, 0:n], func=AF.Copy), reads=[b], writes=[kw])
        P.op("pool", lambda e, kw=kw: e.tensor_copy(out=kiT_c[e_][0:64, kcol0:kcol0 + n], in_=kw[0:64, 0:n]), reads=[kw], pwrites=[kiT_c[e_]])
        P.dma("act", lambda e, kw=kw: e.dma_start(out=iout, in_=kw[0:64, 0:n]), kw, reads=[kw], is_output=True)
        for tb in range(n // 128):
            b2 = bank()
            P.op("pe", lambda e, b2=b2, kw=kw, tb=tb: e.matmul(b2[:, 0:4], kw[64:68, tb * 128:(tb + 1) * 128], identf[64:68, 64:68], start=True, stop=True), reads=[kw, identf], writes=[b2])
            P.op("act", lambda e, b2=b2, tb=tb: e.activation(out=wi_tm[:, tb, :], in_=b2[:, 0:4], func=AF.Copy), reads=[b2], writes=[wi_tm] if tb == 0 else (), pwrites=() if tb == 0 else [wi_tm])

    _dq = [0]

    def dsa_qblock(e_, nq, c0, wrow, wtb, segs, blocks, NK, mask_a, select, par):
        qs = wrow
        negv = negvf[:, par * 2048:(par + 1) * 2048]
        return (lambda: _dsa_idx(e_, nq, c0, qs, wtb, segs), lambda: _dsa_sel(nq, qs, NK, mask_a, select, negv),
                lambda: _dsa_main(e_, nq, c0, qs, blocks, negv), lambda: _dsa_epi(nq, c0))

    def _dsa_idx(e_, nq, c0, qs, wtb, segs):
        wrow = qs
        for hi in range(H_IDX):
            col = 0
            for (kap, w) in segs:
                bq = banks[4 + (_dq[0] % 4)]
                _dq[0] += 1
                P.op("pe", lambda e, bq=bq, kap=kap, w=w, hi=hi: e.matmul(bq[qs, 0:w], qiT[0:64, hi, c0:c0 + nq], kap, start=True, stop=True), reads=[qiT, kiT_c[e_]], writes=[bq])
                if hi == 0:
                    P.op("dve", lambda e, bq=bq, w=w, col=col: e.tensor_scalar(out=iscv[qs, col:col + w], in0=bq[qs, 0:w], scalar1=0.0, scalar2=wi_tm[wrow, wtb, 0:1], op0=ALU.max, op1=ALU.mult),
                         reads=[bq, wi_tm], writes=[isc_t] if col == 0 else (), pwrites=() if col == 0 else [isc_t])
                else:
                    r_ = wf()
                    P.op("act", lambda e, bq=bq, r_=r_, w=w: e.activation(out=r_[qs, 0:w], in_=bq[qs, 0:w], func=AF.Relu), reads=[bq], writes=[r_])
                    P.op("dve", lambda e, r_=r_, w=w, col=col, hi=hi: e.scalar_tensor_tensor(out=iscv[qs, col:col + w], in0=r_[qs, 0:w], scalar=wi_tm[wrow, wtb, hi:hi + 1], in1=iscv[qs, col:col + w], op0=ALU.mult, op1=ALU.add),
                         reads=[r_, wi_tm, isc_t], pwrites=[isc_t])
                col += w

    def _dsa_sel(nq, qs, NK, mask_a, select, negv):
        if select:
            P.op("dve", lambda e: e.tensor_reduce(out=bis[qs, 0:1], in_=iscv[qs, 0:NK], axis=mybir.AxisListType.X, op=ALU.min), reads=[isc_t], writes=[bis])
            P.op("dve", lambda e: e.tensor_reduce(out=bis[qs, 1:2], in_=iscv[qs, 0:NK], axis=mybir.AxisListType.X, op=ALU.max), reads=[isc_t], pwrites=[bis])
        if mask_a:
            P.op("dve", lambda e: e.memset(iscv[0:64, NK - 64:NK], -1e30), reads=[isc_t], pwrites=[isc_t])
        if select:
            P.op("dve", lambda e: e.tensor_tensor(out=bis[qs, 2:3], in0=bis[qs, 1:2], in1=bis[qs, 0:1], op=ALU.subtract), reads=[bis], pwrites=[bis])
            P.op("dve", lambda e: e.tensor_scalar(out=wtab[qs, 0:NIT + 1], in0=cst[qs, 1218:1218 + NIT + 1], scalar1=bis[qs, 2:3], scalar2=None, op0=ALU.mult), reads=[cst, bis], writes=[wtab])
            P.op("dve", lambda e: e.tensor_tensor(out=bis[qs, 4:5], in0=bis[qs, 0:1], in1=wtab[qs, 0:1], op=ALU.add), reads=[bis, wtab], pwrites=[bis])
            for it in range(NIT):
                P.op("dve", lambda e: e.tensor_scalar(out=junkv[qs, 0:NK], in0=iscv[qs, 0:NK], scalar1=bis[qs, 4:5], scalar2=None, op0=ALU.is_ge, op1=ALU.add, accum_out=bis[qs, 5:6]), reads=[isc_t, bis], writes=[junk], pwrites=[bis])
                P.op("dve", lambda e, it=it: e.scalar_tensor_tensor(out=bis[qs, 6:7], in0=bis[qs, 5:6], scalar=float(TOPK_MAX), in1=wtab[qs, it:it + 1], op0=ALU.is_ge, op1=ALU.mult), reads=[bis, wtab], pwrites=[bis])
                P.op("dve", lambda e, it=it: e.scalar_tensor_tensor(out=bis[qs, 4:5], in0=bis[qs, 4:5], scalar=wtab[qs, it + 1:it + 2], in1=bis[qs, 6:7], op0=ALU.subtract, op1=ALU.add), reads=[bis, wtab], pwrites=[bis])
            P.op("dve", lambda e: e.tensor_tensor(out=bis[qs, 3:4], in0=bis[qs, 4:5], in1=wtab[qs, NIT:NIT + 1], op=ALU.subtract), reads=[bis, wtab], pwrites=[bis])
            P.op("dve", lambda e: e.tensor_scalar(out=negv[qs, 0:NK], in0=iscv[qs, 0:NK], scalar1=bis[qs, 3:4], scalar2=NEG, op0=ALU.is_lt, op1=ALU.mult), reads=[isc_t, bis], writes=[negsel])
        else:
            P.op("dve", lambda e: e.tensor_scalar(out=negv[qs, 0:NK], in0=iscv[qs, 0:NK], scalar1=-1e29, scalar2=NEG, op0=ALU.is_lt, op1=ALU.mult), reads=[isc_t], writes=[negsel])

    def _dsa_main(e_, nq, c0, qs, blocks, negv):
        hpb = 512 // nq
        nbk = H_B // hpb
        idr = idrep if nq == 128 else idrep64
        accO = [banks[0], banks[1]][:nbk]
        accD = [banks[2], banks[3]][:nbk]
        def st1(bi):
            kap, vap, nk, pb, col0 = blocks[bi]
            pt = PTd[bi % 2]
            ks = slice(pb, pb + nk)
            for g in range(nbk):
                bs = banks[4 + 2 * (bi % 2) + g]
                P.op("pe", lambda e, bs=bs, kap=kap, g=g, ks=ks: e.matmul(bs[ks, 0:512], kap, qbT[0:64, g * hpb:(g + 1) * hpb, c0:c0 + nq], start=True, stop=False), reads=[kbT_c[e_], qbT], writes=[bs])
                P.op("pe", lambda e, bs=bs, ks=ks, col0=col0, nk=nk: e.matmul(bs[ks, 0:512], negv[qs, col0:col0 + nk], idr[qs, :], start=False, stop=True), reads=[negsel, idr], pwrites=[bs], pe_sync=(qs.start != 0))
                P.op("act", lambda e, bs=bs, pt=pt, g=g, ks=ks: e.activation(out=pt[ks, g * 512:(g + 1) * 512], in_=bs[ks, 0:512], func=AF.Exp, scale=0.125), reads=[bs], writes=[pt] if g == 0 else (), pwrites=() if g == 0 else [pt])

        def st2(bi):
            kap, vap, nk, pb, col0 = blocks[bi]
            pt = PTd[bi % 2]
            ks = slice(pb, pb + nk)
            st, sp_ = (bi == 0), (bi == len(blocks) - 1)
            for g in range(nbk):
                P.op("pe", lambda e, g=g, vap=vap, pt=pt, ks=ks, st=st, sp_=sp_: e.matmul(accO[g][0:64, 0:512], vap, pt[ks, g * 512:(g + 1) * 512], start=st, stop=sp_), reads=[Vb_c[e_], pt], writes=[accO[g]] if st else (), pwrites=() if st else [accO[g]])
                P.op("pe", lambda e, g=g, pt=pt, ks=ks, st=st, sp_=sp_: e.matmul(accD[g][0:64, 0:512], ones64[ks, :], pt[ks, g * 512:(g + 1) * 512], start=st, stop=sp_), reads=[ones64, pt], writes=[accD[g]] if st else (), pwrites=() if st else [accD[g]])

        st1(0)
        for bi in range(1, len(blocks)):
            st1(bi)
            st2(bi - 1)
        st2(len(blocks) - 1)

    def _dsa_epi(nq, c0):
        hpb = 512 // nq
        nbk = H_B // hpb
        accO = [banks[0], banks[1]][:nbk]
        accD = [banks[2], banks[3]][:nbk]
        for g in range(nbk):
            rd = wf()
            P.op("act", lambda e, g=g, rd=rd: e.activation(out=rd[0:64, :], in_=accD[g][0:64, :], func=AF.Ln), reads=[accD[g]], writes=[rd])
            P.op("act", lambda e, rd=rd: e.activation(out=rd[0:64, :], in_=rd[0:64, :], func=AF.Exp, scale=-1.0), reads=[rd], writes=[rd])
            P.op("dve", lambda e, g=g, rd=rd: e.tensor_tensor(out=oTb[0:64, g * hpb:(g + 1) * hpb, c0:c0 + nq], in0=accO[g][0:64, :].rearrange("p (a b) -> p a b", a=hpb), in1=rd[0:64, :].rearrange("p (a b) -> p a b", a=hpb), op=ALU.mult),
                 reads=[accO[g], rd], pwrites=[oTb])

    def dsa_pipeline(qbs):
        qbs[0][0]()
        qbs[0][1]()
        for m in range(1, len(qbs)):
            qbs[m][0]()
            qbs[m - 1][2]()
            qbs[m - 1][3]()
            qbs[m][1]()
        qbs[-1][2]()
        qbs[-1][3]()
        return
        qbs[0][0]()
        qbs[0][1]()
        for m in range(1, len(qbs)):
            qbs[m][0]()
            qbs[m - 1][2]()
            qbs[m][1]()
            qbs[m - 1][3]()
        qbs[-1][2]()
        qbs[-1][3]()

    def even_out_proj(l, n, with_a):
        key = "eo%d" % l
        for c in range(KD):
            wt = ws.get(key, c)
            b = bank()
            ks = list(range(4)) if with_a else []
            ks += list(range(4, 12))
            for i, k in enumerate(ks):
                st, sp_ = (i == 0), (i == len(ks) - 1)
                if k < 4:
                    P.op("pe", lambda e, b=b, wt=wt, k=k, st=st, sp_=sp_: e.matmul(b[:, 0:n], wt[:, k * 128:(k + 1) * 128], oT[:, k, 0:n], start=st, stop=sp_), reads=[wt, oT], writes=[b] if st else (), pwrites=() if st else [b])
                else:
                    P.op("pe", lambda e, b=b, wt=wt, k=k, st=st, sp_=sp_: e.matmul(b[:, 0:n], wt[0:64, k * 128:(k + 1) * 128], oTb[0:64, k - 4, 0:n], start=st, stop=sp_), reads=[wt, oTb], writes=[b] if st else (), pwrites=() if st else [b])
            P.op("dve", lambda e, b=b, c=c: e.tensor_tensor(out=xT[:, c, 0:n], in0=b[:, 0:n], in1=xT[:, c, 0:n], op=ALU.add), reads=[b, xT], pwrites=[xT])

    def even_layer_prompt(l, j, t):
        e_ = l // 2
        n = TT
        t0 = t * TT
        rmsnorm(n, o_nmix(l))
        dst = dn_layer_A(l, n, 1, j, t)
        dsa_inproj(l, n, t0, t * 4, dk_p_d[e_, j, :, t0:t0 + n], dv_p_d[e_, j, :, t0:t0 + n], di_p_d[e_, j, :, t0:t0 + n])
        dn_recur(dst)
        qbs = []
        for m in range(4):
            NK = t0 + (m + 1) * 128
            segs = []
            c_ = 0
            while c_ < NK:
                w = min(512, NK - c_)
                segs.append((kiT_c[e_][0:64, c_:c_ + w], w))
                c_ += w
            blocks = [(kbT_c[e_][0:64, kb * 128:(kb + 1) * 128], Vb_c[e_][:, kb, :], 128, 0, kb * 128) for kb in range(NK // 128)]
            qbs.append(dsa_qblock(e_, 128, m * 128, slice(0, 128), m, segs, blocks, NK, True, NK > TOPK_MAX, m % 2))
        dsa_pipeline(qbs)
        even_out_proj(l, n, DN_ON)

    def even_layer_sample(l):
        e_ = l // 2
        n = ns_tok
        ls = cfg.dec_seq
        PC = cfg.past
        rmsnorm(n, o_nmix(l))
        dst = dn_layer_A(l, n, cfg.n_sseq, None, None)
        dsa_inproj(l, n, PC, PC // 128, dk_s_d[e_, :, :], dv_s_d[e_, :, :], di_s_d[e_, :, :])
        dn_recur(dst)
        for s_ in range(cfg.n_sseq):
            for hf in range(PC // 512):
                for (src, dstc) in ((dkc_d, kbT_c[e_]), (dic_d, kiT_c[e_])):
                    f = wf()
                    P.dma("sp", lambda e, f=f, src=src, hf=hf, s_=s_: e.dma_start(out=f[0:64, 0:512], in_=src[e_, s_, :, hf * 512:(hf + 1) * 512]), f, writes=[f])
                    P.op("pool", lambda e, f=f, dstc=dstc, hf=hf: e.tensor_copy(out=dstc[0:64, hf * 512:(hf + 1) * 512], in_=f[0:64, 0:512]), reads=[f], pwrites=[dstc])
            f = wf()
            nvb = PC // 128
            P.dma("sp", lambda e, f=f, s_=s_: e.dma_start(out=f[:, 0:nvb * 64].rearrange("p (a b) -> p a b", a=nvb), in_=dvc_d[e_, s_, :, :, :]), f, writes=[f])
            P.op("pool", lambda e, f=f: e.tensor_copy(out=Vb_c[e_][:, 0:nvb, :], in_=f[:, 0:nvb * 64].rearrange("p (a b) -> p a b", a=nvb)), reads=[f], pwrites=[Vb_c[e_]])
            NK = PC + ls
            segs = []
            c_ = 0
            while c_ < PC:
                w = min(512, PC - c_)
                segs.append((kiT_c[e_][0:64, c_:c_ + w], w))
                c_ += w
            segs.append((kiT_c[e_][0:64, PC + s_ * ls:PC + (s_ + 1) * ls], ls))
            blocks = [(kbT_c[e_][0:64, kb * 128:(kb + 1) * 128], Vb_c[e_][:, kb, :], 128, 0, kb * 128) for kb in range(PC // 128)]
            pb = (s_ % 2) * 64
            blocks.append((kbT_c[e_][0:64, PC + s_ * ls:PC + (s_ + 1) * ls], Vb_c[e_][pb:pb + ls, PC // 128 + s_ // 2, :], ls, pb, PC))
            st_ = dsa_qblock(e_, ls, s_ * ls, slice(pb, pb + ls), s_ // 2, segs, blocks, NK, False, NK > TOPK_MAX, 0)
            for f_ in st_:
                f_()
        even_out_proj(l, n, DN_ON)

    DN_ON = True
    actflat = actT[:].rearrange("p a b -> p (a b)")
    qkn = actT[:, 13:21, :]
    vzv = Tn[:].rearrange("p a b -> p (a b)")[:, 2048:4096].bitcast(BF16).rearrange("p (a b) -> p a b", a=8)
    vz_t = Tn
    orawv = hT[:].rearrange("p a b -> p (a b)").bitcast(F32).rearrange("p (a b) -> p a b", a=4)
    FW = [aT[0], aT[1], cv[0], cv[1], sg[0], sg[1]]
    kpv = Kprev[:].rearrange("p a b -> p (a b)")
    BW = [(Kprev, kpv[:, i * 512:(i + 1) * 512]) for i in range(8)]
    BW += [(PTd[0], PTd[0][:, 0:512]), (PTd[0], PTd[0][:, 512:1024]), (PTd[1], PTd[1][:, 0:512]), (PTd[1], PTd[1][:, 512:1024])]
    cU, cSC, cNU, cNSU, cNSL, cALL1 = (cst[:, 448:576], cst[:, 576:704], cst[:, 704:832], cst[:, 832:960], cst[:, 960:1088], cst[:, 1088:1216])
    cCH = cst[:, 1216:1218]

    def v4(ap):
        return ap.rearrange("p (a b) -> p a b", a=4)

    def bc_h(ap128):
        return ap128.unsqueeze(1).to_broadcast([128, 4, 128])

    def bc_x(ap4):
        return ap4.unsqueeze(2).to_broadcast([128, 4, 128])

    def dn_layer_A(l, n, nseq, j, t):
        e_ = l // 2
        key = "eA%d" % l
        ls = n // nseq
        seg = ls + 3
        SEGT = nseq * seg
        nblk = n // 128
        prompt = (nseq == 1)
        if prompt and t == 0:
            P.op("pool", lambda e: e.memset(S_f[e_][:], 0.0), writes=[S_f[e_]])
            P.op("pool", lambda e: e.memset(S_b[e_][:], 0.0), writes=[S_b[e_]])
            P.op("pool", lambda e: e.memset(dtail[e_][:], 0.0), writes=[dtail[e_]])
        for c in range(12):
            b = proj_fm(key, c, n, hT, KD)
            xv = actflat[:, c * SEGT:(c + 1) * SEGT].rearrange("p (s w) -> p s w", w=seg)
            P.op("act", lambda e, b=b, xv=xv: e.activation(out=xv[:, :, 3:seg], in_=b[:, 0:n].rearrange("p (s w) -> p s w", w=ls), func=AF.Copy), reads=[b], pwrites=[actT])
            tsrc = dtail[e_][:, c:c + 1, :] if prompt else dcst[:, e_, c, :, :]
            ttl = dtail[e_] if prompt else dcst
            P.op("pool", lambda e, xv=xv, tsrc=tsrc: e.tensor_copy(out=xv[:, :, 0:3], in_=tsrc), reads=[ttl], pwrites=[actT])
            acc = FW[2 + c % 2]
            accv = acc[:, 0:n].rearrange("p (s w) -> p s w", w=ls)
            P.op("dve", lambda e, xv=xv, accv=accv, c=c: e.tensor_scalar(out=accv, in0=xv[:, :, 0:ls], scalar1=dnw[:, e_, c, 0:1], scalar2=None, op0=ALU.mult), reads=[actT, dnw], writes=[acc])
            for jt in range(1, 4):
                P.op("dve", lambda e, xv=xv, accv=accv, c=c, jt=jt: e.scalar_tensor_tensor(out=accv, in0=xv[:, :, jt:jt + ls], scalar=dnw[:, e_, c, jt:jt + 1], in1=accv, op0=ALU.mult, op1=ALU.add), reads=[actT, dnw, acc], writes=[acc])
            if prompt:
                P.op("pool", lambda e, xv=xv, c=c: e.tensor_copy(out=dtail[e_][:, c:c + 1, :], in_=xv[:, :, ls:ls + 3]), reads=[actT], pwrites=[dtail[e_]])
            else:
                P.op("pool", lambda e, xv=xv, c=c: e.tensor_copy(out=dcso[:, e_, c, :, :], in_=xv[:, :, ls:ls + 3]), reads=[actT], pwrites=[dcso])
            if c < 8:
                raw = wbt()
                P.op("act", lambda e, acc=acc, raw=raw: e.activation(out=raw[:, 0:n], in_=acc[:, 0:n], func=AF.Silu), reads=[acc], writes=[raw])
                sqb = wbt()
                P.op("act", lambda e, raw=raw, sqb=sqb: e.activation(out=sqb[:, 0:n], in_=raw[:, 0:n], func=AF.Square), reads=[raw], writes=[sqb])
                b2 = bank()
                P.op("pe", lambda e, b2=b2, sqb=sqb: e.matmul(b2[:, 0:n], onesD[:], sqb[:, 0:n], start=True, stop=True), reads=[onesD, sqb], writes=[b2])
                rr = wf()
                P.op("act", lambda e, b2=b2, rr=rr: e.activation(out=rr[:, 0:n], in_=b2[:, 0:n], func=AF.Ln, bias=EPS, scale=float(D)), reads=[b2], writes=[rr])
                P.op("act", lambda e, rr=rr: e.activation(out=rr[:, 0:n], in_=rr[:, 0:n], func=AF.Exp, scale=-0.5), reads=[rr], writes=[rr])
                scl = DK_A ** -0.5 if c < 4 else 1.0
                P.op("dve", lambda e, raw=raw, rr=rr, c=c, scl=scl: e.scalar_tensor_tensor(out=qkn[:, c, 0:n], in0=raw[:, 0:n], scalar=scl, in1=rr[:, 0:n], op0=ALU.mult, op1=ALU.mult), reads=[raw, rr], pwrites=[actT])
            else:
                P.op("act", lambda e, acc=acc, c=c: e.activation(out=vzv[:, c - 8, 0:n], in_=acc[:, 0:n], func=AF.Silu), reads=[acc], pwrites=[vz_t])
        for c in range(12, 16):
            b = proj_fm(key, c, n, hT, KD)
            P.op("act", lambda e, b=b, c=c: e.activation(out=vzv[:, c - 8, 0:n], in_=b[:, 0:n], func=AF.Silu), reads=[b], pwrites=[vz_t])
        b = proj_fm(key, 16, n, hT, KD)
        bt8 = wf()
        P.op("act", lambda e, b=b, bt8=bt8: e.activation(out=bt8[0:8, 0:n], in_=b[0:8, 0:n], func=AF.Copy), reads=[b], writes=[bt8])
        b2 = bank()
        for blk in range(nblk):
            P.op("pe", lambda e, b2=b2, bt8=bt8, blk=blk: e.matmul(b2[:, blk * 8:(blk + 1) * 8], bt8[0:8, blk * 128:(blk + 1) * 128], identf[0:8, 0:8], start=True, stop=True), reads=[bt8, identf], writes=[b2] if blk == 0 else (), pwrites=() if blk == 0 else [b2])
        P.op("act", lambda e, b2=b2: e.activation(out=gb_tm[:, 0:nblk, :], in_=b2[:, 0:nblk * 8].rearrange("p (a b) -> p a b", b=8), func=AF.Copy), reads=[b2], writes=[gb_tm])
        nb = slice(0, nblk)
        P.op("act", lambda e: e.activation(out=bt_tm[:, nb, :], in_=gb_tm[:, nb, 0:4], func=AF.Sigmoid), reads=[gb_tm], writes=[bt_tm])
        P.op("act", lambda e: e.activation(out=lnb_tm[:, nb, :], in_=bt_tm[:, nb, :], func=AF.Ln), reads=[bt_tm], writes=[lnb_tm])
        P.op("dve", lambda e: e.tensor_tensor(out=g_tm[:, nb, :], in0=gb_tm[:, nb, 4:8], in1=dnc[:, e_:e_ + 1, 4:8].to_broadcast([128, nblk, 4]), op=ALU.add), reads=[gb_tm, dnc], writes=[g_tm])
        P.op("act", lambda e: e.activation(out=g_tm[:, nb, :], in_=g_tm[:, nb, :], func=AF.Exp), reads=[g_tm], writes=[g_tm])
        P.op("act", lambda e: e.activation(out=g_tm[:, nb, :], in_=g_tm[:, nb, :], func=AF.Ln, bias=1.0), reads=[g_tm], writes=[g_tm])
        P.op("dve", lambda e: e.tensor_tensor(out=g_tm[:, nb, :], in0=g_tm[:, nb, :], in1=nexpA[:, e_:e_ + 1, :].to_broadcast([128, nblk, 4]), op=ALU.mult), reads=[g_tm, nexpA], writes=[g_tm])
        return dict(e_=e_, n=n, nseq=nseq, nblk=nblk, prompt=prompt, j=j, t=t)

    def dn_recur(st):
        e_, n, nseq, nblk, prompt, j, t = st["e_"], st["n"], st["nseq"], st["nblk"], st["prompt"], st["j"], st["t"]
        Sf, Sb = S_f[e_], S_b[e_]
        (tX, aX), (tXT, aXT), (tAT, aAT), (tTa, aTa), (tTb, aTb), (tPa, aPa), (tPb, aPb), (tQa, aQa), (tQb, aQb), (tkbg, akbg), (tkgl, akgl), (tvb, avb) = BW
        F0, F1, F2, F3, F4, F5 = FW
        for blk in range(nblk):
            tb = slice(blk * 128, (blk + 1) * 128)
            gcol = g_tm[:, blk, :]
            P.op("dve", lambda e, gcol=gcol: e.tensor_tensor(out=v4(F0[:, 0:512]), in0=bc_h(cU), in1=bc_x(gcol), op=ALU.mult), reads=[cst, g_tm], writes=[F0])
            for h in range(4):
                P.op("dve", lambda e, h=h, blk=blk: e.scalar_tensor_tensor(out=F1[:, h * 128:(h + 1) * 128], in0=identf[:], scalar=lnb_tm[:, blk, h:h + 1], in1=F0[:, h * 128:(h + 1) * 128], op0=ALU.mult, op1=ALU.add), reads=[identf, lnb_tm, F0], writes=[F1] if h == 0 else (), pwrites=() if h == 0 else [F1])
            P.op("dve", lambda e, gcol=gcol: e.tensor_tensor(out=bl8[:].rearrange("p (a b) -> p a b", a=4), in0=gcol.unsqueeze(2).to_broadcast([128, 4, 2]), in1=cCH.unsqueeze(1).to_broadcast([128, 4, 2]), op=ALU.mult), reads=[g_tm, cst], writes=[bl8])
            gG, gGp, gGF, gS = bank(), bank(), bank(), bank()
            P.op("pe", lambda e, gG=gG: e.matmul(gG[:, 0:512], cSC, F0[:, 0:512], start=True, stop=True), reads=[cst, F0], writes=[gG])
            P.op("pe", lambda e, gGp=gGp: e.matmul(gGp[:, 0:512], cSC, F1[:, 0:512], start=True, stop=True), reads=[cst, F1], writes=[gGp])
            P.op("pe", lambda e, gGF=gGF: e.matmul(gGF[:, 0:512], cALL1, F0[:, 0:512], start=True, stop=True), reads=[cst, F0], writes=[gGF])
            P.op("pe", lambda e, gS=gS, gcol=gcol: e.matmul(gS[:, 0:4], cU, gcol, start=True, stop=True), reads=[cst, g_tm], writes=[gS])
            P.op("pe", lambda e, gS=gS, gcol=gcol: e.matmul(gS[:, 4:8], cSC, gcol, start=True, stop=True), reads=[cst, g_tm], pwrites=[gS])
            P.op("pe", lambda e, gS=gS: e.matmul(gS[:, 8:16], cALL1, bl8[:], start=True, stop=True), reads=[cst, bl8], pwrites=[gS])
            P.op("act", lambda e, gS=gS: e.activation(out=gsm[:], in_=gS[:, 0:8], func=AF.Copy), reads=[gS], writes=[gsm])
            P.op("act", lambda e, gS=gS: e.activation(out=egl8[:], in_=gS[:, 8:16], func=AF.Exp), reads=[gS], writes=[egl8])
            P.op("act", lambda e: e.activation(out=sm2[:, 0:4], in_=gsm[:, 0:4], func=AF.Exp), reads=[gsm], writes=[sm2])
            P.op("dve", lambda e: e.tensor_tensor(out=sm2[:, 4:8], in0=gsm[:, 4:8], in1=gsm[:, 0:4], op=ALU.subtract), reads=[gsm], pwrites=[sm2])
            P.op("act", lambda e: e.activation(out=sm2[:, 4:8], in_=sm2[:, 4:8], func=AF.Exp), reads=[sm2], pwrites=[sm2])
            P.op("dve", lambda e, blk=blk: e.tensor_tensor(out=sm2[:, 8:12], in0=gsm[:, 0:4], in1=lnb_tm[:, blk, :], op=ALU.add), reads=[gsm, lnb_tm], pwrites=[sm2])
            P.op("dve", lambda e, blk=blk: e.tensor_tensor(out=sm2[:, 12:16], in0=sm2[:, 0:4], in1=bt_tm[:, blk, :], op=ALU.mult), reads=[sm2, bt_tm], pwrites=[sm2])
            P.op("dve", lambda e, gG=gG: e.tensor_tensor(out=v4(F2[:, 0:512]), in0=v4(gG[:, 0:512]), in1=bc_x(gsm[:, 0:4]), op=ALU.subtract), reads=[gG, gsm], writes=[F2])
            P.op("dve", lambda e: e.tensor_tensor(out=v4(F2[:, 0:512]), in0=v4(F2[:, 0:512]), in1=bc_h(cNU), op=ALU.add), reads=[F2, cst], writes=[F2])
            P.op("act", lambda e: e.activation(out=F2[:, 0:512], in_=F2[:, 0:512], func=AF.Exp), reads=[F2], writes=[F2])
            P.op("dve", lambda e, gGp=gGp: e.tensor_tensor(out=v4(F3[:, 0:512]), in0=v4(gGp[:, 0:512]), in1=bc_x(gsm[:, 0:4]), op=ALU.subtract), reads=[gGp, gsm], writes=[F3])
            P.op("dve", lambda e: e.tensor_tensor(out=v4(F3[:, 0:512]), in0=v4(F3[:, 0:512]), in1=bc_h(cNSU), op=ALU.add), reads=[F3, cst], writes=[F3])
            P.op("act", lambda e: e.activation(out=F3[:, 0:512], in_=F3[:, 0:512], func=AF.Exp), reads=[F3], writes=[F3])
            P.op("dve", lambda e, gG=gG: e.tensor_tensor(out=v4(F4[:, 0:512]), in0=bc_x(sm2[:, 8:12]), in1=v4(gG[:, 0:512]), op=ALU.subtract), reads=[gG, sm2], writes=[F4])
            P.op("dve", lambda e: e.tensor_tensor(out=v4(F4[:, 0:512]), in0=v4(F4[:, 0:512]), in1=bc_h(cNSL), op=ALU.add), reads=[F4, cst], writes=[F4])
            P.op("act", lambda e: e.activation(out=F4[:, 0:512], in_=F4[:, 0:512], func=AF.Exp), reads=[F4], writes=[F4])
            P.op("act", lambda e, gGF=gGF: e.activation(out=F5[:, 0:512], in_=gGF[:, 0:512], func=AF.Exp), reads=[gGF], writes=[F5])
            bKK, bKQ, bKT, bVT = bank(), bank(), bank(), bank()
            for h in range(4):
                fw = (h == 0)
                P.op("pe", lambda e, h=h, bKK=bKK: e.matmul(bKK[:, h * 128:(h + 1) * 128], qkn[:, 4 + h, tb], qkn[:, 4 + h, tb], start=True, stop=True), reads=[actT], writes=[bKK] if fw else (), pwrites=() if fw else [bKK])
            for h in range(4):
                fw = (h == 0)
                P.op("pe", lambda e, h=h, bKQ=bKQ: e.matmul(bKQ[:, h * 128:(h + 1) * 128], qkn[:, 4 + h, tb], qkn[:, h, tb], start=True, stop=True), reads=[actT], writes=[bKQ] if fw else (), pwrites=() if fw else [bKQ])
            for h in range(4):
                fw = (h == 0)
                P.op("pe", lambda e, h=h, bKT=bKT: e.matmul(bKT[:, h * 128:(h + 1) * 128], qkn[:, 4 + h, tb], ident[:], start=True, stop=True), reads=[actT, ident], writes=[bKT] if fw else (), pwrites=() if fw else [bKT])
            for h in range(4):
                fw = (h == 0)
                P.op("pe", lambda e, h=h, bVT=bVT: e.matmul(bVT[:, h * 128:(h + 1) * 128], vzv[:, h, tb], ident[:], start=True, stop=True), reads=[vz_t, ident], writes=[bVT] if fw else (), pwrites=() if fw else [bVT])
            P.op("dve", lambda e, bKK=bKK: e.tensor_tensor(out=aX, in0=bKK[:, 0:512], in1=F3[:, 0:512], op=ALU.mult), reads=[bKK, F3], pwrites=[tX])
            P.op("dve", lambda e, bKK=bKK: e.tensor_tensor(out=aXT, in0=bKK[:, 0:512], in1=F4[:, 0:512], op=ALU.mult), reads=[bKK, F4], pwrites=[tXT])
            P.op("dve", lambda e, bKQ=bKQ: e.tensor_tensor(out=aAT, in0=bKQ[:, 0:512], in1=F2[:, 0:512], op=ALU.mult), reads=[bKQ, F2], pwrites=[tAT])
            P.op("dve", lambda e, bKT=bKT: e.tensor_tensor(out=v4(akbg), in0=v4(bKT[:, 0:512]), in1=bc_x(sm2[:, 12:16]), op=ALU.mult), reads=[bKT, sm2], pwrites=[tkbg])
            P.op("dve", lambda e, bKT=bKT: e.tensor_tensor(out=v4(akgl), in0=v4(bKT[:, 0:512]), in1=bc_x(sm2[:, 4:8]), op=ALU.mult), reads=[bKT, sm2], pwrites=[tkgl])
            P.op("dve", lambda e, bVT=bVT, blk=blk: e.tensor_tensor(out=v4(avb), in0=v4(bVT[:, 0:512]), in1=bc_x(bt_tm[:, blk, :]), op=ALU.mult), reads=[bVT, bt_tm], pwrites=[tvb])
            P.op("dve", lambda e: e.tensor_tensor(out=qgT[:], in0=qkn[:, 0:4, tb], in1=v4(F5[:, 0:512]), op=ALU.mult), reads=[actT, F5], writes=[qgT])
            P.op("dve", lambda e: e.scalar_tensor_tensor(out=v4(aTa), in0=v4(aX), scalar=-1.0, in1=bc_h(ident[:]), op0=ALU.mult, op1=ALU.add), reads=[tX, ident], pwrites=[tTa])
            Tc, Tn_ = (tTa, aTa), (tTb, aTb)
            Pc, PTc = (tX, aX), (tXT, aXT)
            Pn, PTn_ = [(tPa, aPa), (tPb, aPb)], [(tQa, aQa), (tQb, aQb)]
            for lv in range(1, 6):
                pn, ptn = Pn[lv % 2], PTn_[lv % 2]
                if lv < 5:
                    bp = bank()
                    for h in range(4):
                        fw = (h == 0)
                        P.op("pe", lambda e, h=h, bp=bp, Pc=Pc, PTc=PTc: e.matmul(bp[:, h * 128:(h + 1) * 128], PTc[1][:, h * 128:(h + 1) * 128], Pc[1][:, h * 128:(h + 1) * 128], start=True, stop=True), reads=[Pc[0], PTc[0]], writes=[bp] if fw else (), pwrites=() if fw else [bp])
                    P.op("act", lambda e, bp=bp, pn=pn: e.activation(out=pn[1], in_=bp[:, 0:512], func=AF.Copy), reads=[bp], pwrites=[pn[0]])
                bq_ = bank()
                for h in range(4):
                    fw = (h == 0)
                    P.op("pe", lambda e, h=h, bq_=bq_, Pc=Pc, PTc=PTc: e.matmul(bq_[:, h * 128:(h + 1) * 128], Pc[1][:, h * 128:(h + 1) * 128], PTc[1][:, h * 128:(h + 1) * 128], start=True, stop=True), reads=[Pc[0], PTc[0]], writes=[bq_] if fw else (), pwrites=() if fw else [bq_])
                P.op("dve", lambda e, bq_=bq_, ptn=ptn: e.tensor_copy(out=ptn[1], in_=bq_[:, 0:512]), reads=[bq_], pwrites=[ptn[0]])
                bt_ = bank()
                for h in range(4):
                    fw = (h == 0)
                    P.op("pe", lambda e, h=h, bt_=bt_, ptn=ptn, Tc=Tc: e.matmul(bt_[:, h * 128:(h + 1) * 128], ptn[1][:, h * 128:(h + 1) * 128], Tc[1][:, h * 128:(h + 1) * 128], start=True, stop=False), reads=[ptn[0], Tc[0]], writes=[bt_] if fw else (), pwrites=() if fw else [bt_])
                    P.op("pe", lambda e, h=h, bt_=bt_, Tc=Tc: e.matmul(bt_[:, h * 128:(h + 1) * 128], ident[:], Tc[1][:, h * 128:(h + 1) * 128], start=False, stop=True), reads=[ident, Tc[0]], pwrites=[bt_])
                P.op("act", lambda e, bt_=bt_, Tn_=Tn_: e.activation(out=Tn_[1], in_=bt_[:, 0:512], func=AF.Copy), reads=[bt_], pwrites=[Tn_[0]])
                Tc, Tn_ = Tn_, Tc
                Pc, PTc = pn, ptn
            TT = Tc
            bw_ = bank()
            for h in range(4):
                fw = (h == 0)
                P.op("pe", lambda e, h=h, bw_=bw_, TT=TT: e.matmul(bw_[:, h * 128:(h + 1) * 128], akbg[:, h * 128:(h + 1) * 128], TT[1][:, h * 128:(h + 1) * 128], start=True, stop=True), reads=[tkbg, TT[0]], writes=[bw_] if fw else (), pwrites=() if fw else [bw_])
            P.op("act", lambda e, bw_=bw_: e.activation(out=nwT[:].rearrange("p a b -> p (a b)"), in_=bw_[:, 0:512], func=AF.Copy, scale=-1.0), reads=[bw_], writes=[nwT])
            for c in range(2):
                cs = slice(c * 64, c * 64 + 64)
                if not prompt:
                    sq_ = 2 * blk + c
                    P.dma("sp", lambda e, sq_=sq_: e.dma_start(out=Sf[:], in_=dSst_d[e_, sq_]), Sf, writes=[Sf])
                    P.op("act", lambda e: e.activation(out=Sb[:], in_=Sf[:], func=AF.Copy), reads=[Sf], writes=[Sb])
                P.op("dve", lambda e: e.tensor_tensor(out=Sf[:], in0=Sf[:], in1=egl8[:].rearrange("p (a b) -> p a b", a=4)[:, :, c:c + 1].to_broadcast([128, 4, 128]), op=ALU.mult), reads=[Sf, egl8], writes=[Sf])
                bvn = bank()
                for h in range(4):
                    fw = (h == 0)
                    hs_ = slice(h * 128, (h + 1) * 128)
                    P.op("pe", lambda e, h=h, hs_=hs_, bvn=bvn, TT=TT: e.matmul(bvn[cs, hs_], TT[1][cs, h * 128 + c * 64:h * 128 + c * 64 + 64], avb[cs, hs_], start=True, stop=False), reads=[TT[0], tvb], writes=[bvn] if fw else (), pwrites=() if fw else [bvn])
                    P.op("pe", lambda e, h=h, hs_=hs_, bvn=bvn: e.matmul(bvn[cs, hs_], nwT[:, h, cs], Sb[:, h, :], start=False, stop=True), reads=[nwT, Sb], pwrites=[bvn])
                P.op("act", lambda e, bvn=bvn: e.activation(out=vnew[cs, :, :].rearrange("p a b -> p (a b)"), in_=bvn[cs, 0:512], func=AF.Copy), reads=[bvn], writes=[vnew])
                bo = bank()
                for h in range(4):
                    fw = (h == 0)
                    P.op("pe", lambda e, h=h, bo=bo: e.matmul(bo[:, h * 64:(h + 1) * 64], Sb[:, h, :], qgT[:, h, cs], start=True, stop=False), reads=[Sb, qgT], writes=[bo] if fw else (), pwrites=() if fw else [bo])
                    P.op("pe", lambda e, h=h, bo=bo: e.matmul(bo[:, h * 64:(h + 1) * 64], vnew[cs, h, :], aAT[cs, h * 128 + c * 64:h * 128 + c * 64 + 64], start=False, stop=True), reads=[vnew, tAT], pwrites=[bo])
                t0c = blk * 128 + c * 64
                P.op("dve", lambda e, bo=bo, t0c=t0c: e.tensor_copy(out=orawv[:, :, t0c:t0c + 64], in_=bo[:, 0:256].rearrange("p (a b) -> p a b", a=4)), reads=[bo], pwrites=[hT])
                bs_ = bank()
                for h in range(4):
                    fw = (h == 0)
                    hs_ = slice(h * 128, (h + 1) * 128)
                    P.op("pe", lambda e, h=h, hs_=hs_, bs_=bs_: e.matmul(bs_[:, hs_], akgl[cs, hs_], vnew[cs, h, :], start=True, stop=True), reads=[tkgl, vnew], writes=[bs_] if fw else (), pwrites=() if fw else [bs_])
                P.op("dve", lambda e, bs_=bs_: e.tensor_tensor(out=Sb[:], in0=Sf[:], in1=v4(bs_[:, 0:512]), op=ALU.add), reads=[Sf, bs_], writes=[Sb])
                P.op("dve", lambda e, bs_=bs_: e.tensor_tensor(out=Sf[:], in0=Sf[:], in1=v4(bs_[:, 0:512]), op=ALU.add), reads=[Sf, bs_], writes=[Sf])
                if not prompt:
                    P.dma("act", lambda e, sq_=sq_: e.dma_start(out=dS_s_d[e_, sq_], in_=Sf[:]), Sf, reads=[Sf], is_output=True)
        for h in range(4):
            sqb = wbt()
            P.op("act", lambda e, h=h, sqb=sqb: e.activation(out=sqb[:, 0:n], in_=orawv[:, h, 0:n], func=AF.Square), reads=[hT], writes=[sqb])
            b2 = bank()
            P.op("pe", lambda e, b2=b2, sqb=sqb: e.matmul(b2[:, 0:n], onesD[:], sqb[:, 0:n], start=True, stop=True), reads=[onesD, sqb], writes=[b2])
            rr = wf()
            P.op("act", lambda e, b2=b2, rr=rr: e.activation(out=rr[:, 0:n], in_=b2[:, 0:n], func=AF.Ln, bias=EPS, scale=float(D) / DV_A), reads=[b2], writes=[rr])
            P.op("act", lambda e, rr=rr: e.activation(out=rr[:, 0:n], in_=rr[:, 0:n], func=AF.Exp, scale=-0.5), reads=[rr], writes=[rr])
            P.op("dve", lambda e, h=h, rr=rr: e.scalar_tensor_tensor(out=rr[:, 0:n], in0=orawv[:, h, 0:n], scalar=egain[:, e_, 2:3], in1=rr[:, 0:n], op0=ALU.mult, op1=ALU.mult), reads=[hT, egain, rr], writes=[rr])
            P.op("dve", lambda e, h=h, rr=rr: e.tensor_tensor(out=oT[:, h, 0:n], in0=rr[:, 0:n], in1=vzv[:, 4 + h, 0:n], op=ALU.mult), reads=[rr, vz_t], pwrites=[oT])
        if prompt and t == cfg.ntile - 1:
            P.dma("act", lambda e: e.dma_start(out=dS_p_d[e_, j], in_=Sf[:]), Sf, reads=[Sf], is_output=True)
            P.dma("act", lambda e: e.dma_start(out=dc_p_d[e_, j], in_=dtail[e_][:]), dtail[e_], reads=[dtail[e_]], is_output=True)

    def plan_even(l):
        lst = [("eA%d" % l, c, KD * 128) for c in range(17)]
        lst += [("eB%d" % l, c, KD * 128) for c in range(14)]
        lst += [("eo%d" % l, c, 12 * 128) for c in range(KD)]
        return lst

    def plan_band(l):
        lst = [("oqk%d" % l, c, KD * 128) for c in range(16)]
        lst += [("ov%d" % l, g, 4 * 256) for g in range(8)]
        lst += [("oo%d" % l, c, KD * 128) for c in range(KD)]
        return lst

    ns_tok = cfg.n_sseq * cfg.dec_seq
    ftail_s = sb("ftail_s", [128, L, KF, cfg.n_sseq, 2], F32)
    fout_s = sb("fout_s", [128, L, KF, cfg.n_sseq, 2], F32)

    def plan_layer(l):
        lst = []
        if l % 2 == 1:
            lst += plan_band(l)
        else:
            lst += plan_even(l)
        lst += plan_ffn(l)
        return lst

    for j in range(cfg.n_pseq):
        for t in range(cfg.ntile):
            for l in range(L):
                ws.plan(plan_layer(l))
    for l in range(L):
        ws.plan(plan_layer(l))

    for j in range(cfg.n_pseq):
        P.op("pool", lambda e: e.memset(ftail[:], 0.0), writes=[ftail])
        for t in range(cfg.ntile):
            P.dma("sp", lambda e, j=j, t=t: e.dma_start(out=xT[:], in_=xp_d[j, :, :, t * TT:(t + 1) * TT]), xT, writes=[xT])
            for l in range(L):
                if l % 2 == 1:
                    band_layer_prompt(l, j, t)
                else:
                    even_layer_prompt(l, j, t)
                ffn(l, TT, 1, t == cfg.ntile - 1, j)
            final_norm(TT, lambda k, j=j, t=t: yp_d[j, :, k, t * TT:(t + 1) * TT])
    P.dma("sp", lambda e: e.dma_start(out=ftail_s[:], in_=ffnst_d[:, :, :, :, :]), ftail_s, writes=[ftail_s])
    P.dma("sp", lambda e: e.dma_start(out=xT[:, :, 0:ns_tok], in_=xs_d[:, :, :]), xT, writes=[xT])
    for l in range(L):
        if l % 2 == 1:
            band_layer_sample(l)
        else:
            even_layer_sample(l)
        ffn(l, ns_tok, cfg.n_sseq, True, None)
    P.dma("act", lambda e: e.dma_start(out=ffnc_s_d[:, :, :, :, :], in_=fout_s[:]), fout_s, reads=[fout_s], is_output=True)
    P.dma("act", lambda e: e.dma_start(out=dc_s_d[:, :, :, :, :], in_=dcso[:]), dcso, reads=[dcso], is_output=True)
    final_norm(ns_tok, lambda k: ys_d[:, k, :])

    global SBUF_LEFT
    SBUF_LEFT = nc.sbuf_bytes_remaining
    P.emit()
    es.close()
    return nc, P


def make_consts():
    c = np.zeros((128, NCST), np.float32)
    c[:, 0:128] = 1.0 / D
    for p in range(128):
        c[p, 128 + (p // 64) * 64:128 + (p // 64) * 64 + 64] = 1.0 / 64
    c[:, 256:320] = 1.0
    c[:, 320:448] = np.eye(128, dtype=np.float32)
    p, x = np.meshgrid(np.arange(128), np.arange(128), indexing="ij")
    same = (p // 64) == (x // 64)
    c[:, 448:576] = (same & (p <= x)).astype(np.float32)
    c[:, 576:704] = same.astype(np.float32)
    c[:, 704:832] = np.where(same & (x >= p), 0.0, NEG)
    c[:, 832:960] = np.where(same & (x > p), 0.0, NEG)
    c[:, 960:1088] = np.where(same & (x < p), 0.0, NEG)
    c[:, 1088:1216] = 1.0
    c[:, 1216] = (np.arange(128) < 64)
    c[:, 1217] = (np.arange(128) >= 64)
    for i in range(16):
        c[:, 1218 + i] = 2.0 ** -(i + 1)
    return c


def prep_core_inputs(cfg, inp, core):
    L = cfg.layers
    NO = max(cfg.n_odd, 1)
    ps = slice(core * cfg.n_pseq, (core + 1) * cfg.n_pseq)
    ss = slice(core * cfg.n_sseq, (core + 1) * cfg.n_sseq)
    m = {}
    xp = inp["x_prompt"][ps]
    m["xp"] = np.ascontiguousarray(xp.reshape(cfg.n_pseq, cfg.seq, KD, 128).transpose(0, 3, 2, 1))
    xs = inp["x_sample"][ss].reshape(cfg.n_sseq * cfg.dec_seq, KD, 128)
    m["xs"] = np.ascontiguousarray(xs.transpose(2, 1, 0))
    st = inp["state_ffn_conv"][:L, ss]
    m["ffnst"] = np.ascontiguousarray(st.reshape(L, cfg.n_sseq, 2, KF, 128).transpose(4, 0, 3, 1, 2))
    vec = np.zeros((128, 2 * L * KD + KD + L * KF * 4), np.float32)
    vec[:, 0:L * KD] = feat_major(inp["norm_mix"][:L], KD).reshape(128, L * KD)
    vec[:, L * KD:2 * L * KD] = feat_major(inp["norm_ffn"][:L], KD).reshape(128, L * KD)
    vec[:, 2 * L * KD:2 * L * KD + KD] = feat_major(inp["norm_final"], KD)
    fc = np.concatenate([inp["ffn_conv_w"][:L], inp["ffn_conv_b"][:L, None, :]], 1)
    fc = fc.reshape(L, 4, KF, 128).transpose(3, 0, 2, 1)
    vec[:, 2 * L * KD + KD:] = fc.reshape(128, L * KF * 4)
    m["vecs"] = vec
    m["consts"] = make_consts()
    bg = np.zeros((128, NO, 2), np.float32)
    bt = np.zeros((NO, 128, H_C, 256), np.float32)
    bf_ = np.zeros((128, NO, H_C), np.float32)
    bkc = np.zeros((NO, cfg.n_sseq, 128, KD, 512), np.float32)
    bvc = np.zeros((NO, cfg.n_sseq, 128, 4, 1024), np.float32)
    pp, xx = np.meshgrid(np.arange(128), np.arange(256), indexing="ij")
    u = xx - pp
    idx = np.clip(u, -(CHUNK - 1), REL_CLIP) + (CHUNK - 1)
    msk = (xx < 64) & (pp >= 64)
    for jo in range(cfg.n_odd):
        bg[:, jo, 0] = np.tile(inp["band_q_gain"][jo], 2)
        bg[:, jo, 1] = np.tile(inp["band_k_gain"][jo], 2)
        rb = inp["band_rel_bias"][jo]
        t = rb[:, idx]
        t = np.where(msk[None], np.float32(NEG), t)
        bt[jo] = t.transpose(1, 0, 2)
        bf_[:, jo, :] = rb[None, :, REL_CLIP + CHUNK - 1]
        ck = inp["cache_band_k"][jo, ss].reshape(cfg.n_sseq, 512, KD, 128)
        bkc[jo] = ck.transpose(0, 3, 2, 1)
        cv_ = inp["cache_band_v"][jo, ss].reshape(cfg.n_sseq, 4, 128, 1024)
        bvc[jo] = cv_.transpose(0, 2, 1, 3)
    m["bgain"], m["btab"], m["bfar"], m["bkc"], m["bvc"] = bg, bt, bf_, bkc, bvc
    NE = cfg.n_even
    eg = np.zeros((128, NE, 4), np.float32)
    PC = cfg.past
    dkc = np.zeros((NE, cfg.n_sseq, 64, PC), np.float32)
    dic = np.zeros((NE, cfg.n_sseq, 64, PC), np.float32)
    dvc = np.zeros((NE, cfg.n_sseq, 128, PC // 128, 64), np.float32)
    for e_ in range(NE):
        eg[:, e_, 0] = np.tile(inp["dsa_q_gain"][e_], 2)
        eg[:, e_, 1] = np.tile(inp["dsa_k_gain"][e_], 2)
        eg[:, e_, 2] = inp["dn_o_gain"][e_]
        dkc[e_] = inp["cache_dsa_k"][e_, ss].transpose(0, 2, 1)
        dic[e_] = inp["cache_dsa_kidx"][e_, ss].transpose(0, 2, 1)
        dvc[e_] = inp["cache_dsa_v"][e_, ss].reshape(cfg.n_sseq, PC // 128, 128, 64).transpose(0, 2, 1, 3)
    m["egain"], m["dkc"], m["dic"], m["dvc"] = eg, dkc, dic, dvc
    dnw = np.zeros((128, NE, 12, 4), np.float32)
    dnc = np.zeros((128, NE, 8), np.float32)
    dSst = np.zeros((NE, cfg.n_sseq, 128, 4, 128), np.float32)
    dcst = np.zeros((128, NE, 12, cfg.n_sseq, 3), np.float32)
    for e_ in range(NE):
        dnw[:, e_] = inp["dn_conv_w"][e_].reshape(4, 12, 128).transpose(2, 1, 0)
        dnc[:, e_, 0:4] = inp["dn_a_log"][e_][None, :]
        dnc[:, e_, 4:8] = inp["dn_dt_bias"][e_][None, :]
        dSst[e_] = inp["state_dn_S"][e_, ss].transpose(0, 2, 1, 3)
        dcst[:, e_] = inp["state_dn_conv"][e_, ss].reshape(cfg.n_sseq, 3, 12, 128).transpose(3, 2, 0, 1)
    m["dnw"], m["dnc"], m["dSst"], m["dcst"] = dnw, dnc, dSst, dcst
    return m


def prep_weights(cfg, inp):
    L = cfg.layers
    w = {}
    for l in range(L):
        w["wf_ffa%d" % l] = fm_slabs(inp["ffn_w_a"][l])
        w["wf_ffg%d" % l] = fm_slabs(inp["ffn_w_g"][l])
        fd = fm_slabs(inp["ffn_w_down"][l])
        w["wf_ffd%d" % l] = np.ascontiguousarray(fd.reshape(KD, 128, 2, 11 * 128).transpose(0, 2, 1, 3).reshape(2 * KD * 128, 11 * 128))
        if l % 2 == 0:
            e_ = l // 2
            we = inp["w_in_even"][e_]
            A = np.zeros((D, 17 * 128), np.float32)
            A[:, 0:2048] = we[:, 0:2048]
            A[:, 2048:2056] = we[:, 2048:2056]
            w["wf_eA%d" % l] = fm_slabs(A)
            B = np.zeros((D, 14 * 128), np.float32)
            for h in range(H_B):
                B[:, h * 128:h * 128 + 64] = we[:, 2056 + h * 64:2056 + (h + 1) * 64]
            B[:, 8 * 128:8 * 128 + 64] = we[:, 2568:2632]
            B[:, 8 * 128 + 64:9 * 128] = we[:, 2632:2696]
            for h in range(H_IDX):
                B[:, (9 + h) * 128:(9 + h) * 128 + 64] = we[:, 2696 + h * 64:2696 + (h + 1) * 64]
            B[:, 13 * 128:13 * 128 + 64] = we[:, 2952:3016]
            B[:, 13 * 128 + 64:13 * 128 + 68] = we[:, 3016:3020]
            w["wf_eB%d" % l] = fm_slabs(B)
            wo = inp["w_out_even"][e_]
            WO = np.zeros((12 * 128, D), np.float32)
            WO[0:512] = wo[0:512]
            for h in range(H_B):
                WO[512 + h * 128:512 + h * 128 + 64] = wo[512 + h * 64:512 + (h + 1) * 64]
            w["wf_eo%d" % l] = fm_slabs(WO)
        if l % 2 == 1:
            j = l // 2
            wi = inp["w_in_odd"][j]
            w["wf_oqk%d" % l] = fm_slabs(wi[:, 0:2048])
            tv = tm_slabs(wi[:, 2048:3072], 256)
            w["wf_ov%d" % l] = np.ascontiguousarray(tv.reshape(4, 128, 2, 4 * 256).transpose(0, 2, 1, 3).reshape(8 * 128, 4 * 256))
            w["wf_oo%d" % l] = fm_slabs(inp["w_out_odd"][j])
    return w


_CACHE = {}


def run_cfg(cfg, inp, n_cores, runner=None):
    key = (cfg.n_pseq, cfg.seq, cfg.n_sseq, cfg.dec_seq, cfg.past, cfg.layers)
    if key not in _CACHE:
        _CACHE[key] = build_program(cfg)
    nc, P = _CACHE[key]
    wts = prep_weights(cfg, inp)
    in_maps = []
    for c in range(n_cores):
        m = prep_core_inputs(cfg, inp, c)
        m.update(wts)
        in_maps.append(m)
    if runner is None:
        res = run_bass_kernel_spmd(nc, in_maps, core_ids=list(range(n_cores))).results
    else:
        res = runner(nc, in_maps)
    return res


def assemble(cfg, res, n_cores):
    L = cfg.layers
    NO = cfg.n_odd
    o = {}
    o["yp"] = np.concatenate([r["yp"].transpose(0, 3, 2, 1).reshape(cfg.n_pseq, cfg.seq, D) for r in res], 0)
    o["ys"] = np.concatenate([r["ys"].transpose(2, 1, 0).reshape(cfg.n_sseq, cfg.dec_seq, D) for r in res], 0)
    o["ffn_p"] = np.concatenate([r["ffnc_p"].transpose(0, 1, 4, 3, 2).reshape(L, cfg.n_pseq, 2, D_FF) for r in res], 1)
    o["ffn_s"] = np.concatenate([r["ffnc_s"].transpose(1, 3, 4, 2, 0).reshape(L, cfg.n_sseq, 2, D_FF) for r in res], 1)
    NE = cfg.n_even
    for nm, kp, ks_ in (("dsa_k", "dk_p", "dk_s"), ("dsa_v", "dv_p", "dv_s"), ("dsa_ki", "di_p", "di_s")):
        o[nm + "_p"] = np.concatenate([r[kp].transpose(0, 1, 3, 2) for r in res], 1)
        o[nm + "_s"] = np.concatenate([r[ks_].transpose(0, 2, 1).reshape(NE, cfg.n_sseq, cfg.dec_seq, 64) for r in res], 1)
    o["dn_S_p"] = np.concatenate([r["dS_p"].transpose(0, 1, 3, 2, 4) for r in res], 1)
    o["dn_S_s"] = np.concatenate([r["dS_s"].transpose(0, 1, 3, 2, 4) for r in res], 1)
    o["dn_conv_p"] = np.concatenate([r["dc_p"].transpose(0, 1, 4, 3, 2).reshape(NE, cfg.n_pseq, 3, 1536) for r in res], 1)
    o["dn_conv_s"] = np.concatenate([r["dc_s"].transpose(1, 3, 4, 2, 0).reshape(NE, cfg.n_sseq, 3, 1536) for r in res], 1)
    if NO:
        o["band_k_p"] = np.concatenate([r["bk_p"][:NO].transpose(0, 1, 4, 3, 2).reshape(NO, cfg.n_pseq, 512, H_C, HD_C) for r in res], 1)
        o["band_v_p"] = np.concatenate([r["bv_p"][:NO].reshape(NO, cfg.n_pseq, 512, H_C, HD_C) for r in res], 1)
        o["band_k_s"] = np.concatenate([r["bk_s"][:NO].transpose(0, 3, 2, 1).reshape(NO, cfg.n_sseq, cfg.dec_seq, H_C, HD_C) for r in res], 1)
        o["band_v_s"] = np.concatenate([r["bv_s"][:NO].reshape(NO, cfg.n_sseq, cfg.dec_seq, H_C, HD_C) for r in res], 1)
    return o


def kernel(**inputs):
    inp = {k: np.asarray(v) for k, v in inputs.items()}
    cfg = Cfg()
    res = run_cfg(cfg, inp, 8)
    o = assemble(cfg, res, 8)
    B, S, DB, DS = 32, cfg.seq, 32, cfg.dec_seq
    z = lambda *sh: np.zeros(sh, np.float32)
    return (o["yp"], o["ys"],
            o.get("dn_S_p", z(2, B, 4, 128, 128)), o.get("dn_S_s", z(2, DB, 4, 128, 128)),
            o.get("dn_conv_p", z(2, B, 3, 1536)), o.get("dn_conv_s", z(2, DB, 3, 1536)),
            o.get("dsa_k_p", z(2, B, S, 64)), o.get("dsa_k_s", z(2, DB, DS, 64)),
            o.get("dsa_v_p", z(2, B, S, 64)), o.get("dsa_v_s", z(2, DB, DS, 64)),
            o.get("dsa_ki_p", z(2, B, S, 64)), o.get("dsa_ki_s", z(2, DB, DS, 64)),
            o.get("band_k_p", z(2, B, 512, 16, 64)), o.get("band_k_s", z(2, DB, DS, 16, 64)),
            o.get("band_v_p", z(2, B, 512, 16, 64)), o.get("band_v_s", z(2, DB, DS, 16, 64)),
            o["ffn_p"], o["ffn_s"])
```
